# Optimizing a Trainium2 kernel written in Bass

```python
import numpy as np
import jax
import jax.numpy as jnp
from jax import lax

D_MODEL = 2048
BATCH = 2
SEQ = 4096
DEPTH = 4
DEC_BATCH = 8
DEC_SEQ = 8
PAST_LEN = 16384
PAGE_SIZE = 128

N_MEM = 256
HEAD_DIM = 128
ATT_W = D_MODEL // 2
N_ATT_HEADS = ATT_W // HEAD_DIM
DIL_PATTERNS = ((128, 1), (512, 4), (2048, 16))
WIN = max(w for w, _ in DIL_PATTERNS)
N_BUCKETS = 32
MAX_DIST = WIN
CONV_CH = D_MODEL // 4
CONV_K = 31
X_W = D_MODEL // 4
N_X_HEADS = X_W // HEAD_DIM
MIX_W = ATT_W + CONV_CH + X_W
IN_SIZES = (ATT_W, ATT_W, ATT_W, ATT_W, CONV_CH, CONV_CH, CONV_CH, X_W, X_W)
IN_W = sum(IN_SIZES)
SPLIT_AT = tuple(int(s) for s in np.cumsum(IN_SIZES)[:-1])
Q_BLOCK = 128
EPS = 1e-6
NEG = -1e30
SCALE = HEAD_DIM ** -0.5

kernel_name = "hybrid_dilated_conv_memory_decoder_step"


def t5_bucket(dist):
    dist = np.asarray(dist)
    max_exact = N_BUCKETS // 2
    large = max_exact + (np.log(np.maximum(dist, 1) / max_exact)
                         / np.log(MAX_DIST / max_exact) * (N_BUCKETS - max_exact)).astype(np.int32)
    large = np.minimum(large, N_BUCKETS - 1)
    return np.where(dist < max_exact, dist, large).astype(np.int32)


def rms_norm(x, g):
    xf = x.astype(jnp.float32)
    y = xf * lax.rsqrt(jnp.mean(xf * xf, axis=-1, keepdims=True) + EPS)
    return (y * g.astype(jnp.float32)).astype(x.dtype)


def layer_norm(x, g, b):
    xf = x.astype(jnp.float32)
    mu = jnp.mean(xf, axis=-1, keepdims=True)
    var = jnp.mean(jnp.square(xf - mu), axis=-1, keepdims=True)
    y = (xf - mu) * lax.rsqrt(var + EPS) * g.astype(jnp.float32) + b.astype(jnp.float32)
    return y.astype(x.dtype)


def heads(t, n):
    return t.reshape(t.shape[:-1] + (n, HEAD_DIM))


def dilated_attention(q, k_src, v_src, base, lo, rel_bias):
    outs, lses = [], []
    for w, d in DIL_PATTERNS:
        offs = np.arange(w // d + 1) * d
        bias = rel_bias[t5_bucket(offs)].T.astype(jnp.float32)
        idx = base[:, None] - offs[None, :]
        valid = idx >= lo
        idx = jnp.maximum(idx, 0)
        kg = jnp.take(k_src, idx, axis=1)
        vg = jnp.take(v_src, idx, axis=1)
        logits = jnp.einsum('bqhd,bqjhd->bqhj', q, kg,
                            preferred_element_type=jnp.float32) * SCALE + bias
        logits = jnp.where(valid[None, :, None, :], logits, NEG)
        m = jnp.max(logits, axis=-1, keepdims=True)
        p = jnp.exp(logits - m)
        s = jnp.sum(p, axis=-1, keepdims=True)
        outs.append(jnp.einsum('bqhj,bqjhd->bqhd', p, vg.astype(jnp.float32)) / s)
        lses.append((m + jnp.log(s))[..., 0])
    wts = jax.nn.softmax(jnp.stack(lses, axis=0), axis=0)
    o = sum(wts[i][..., None] * outs[i] for i in range(len(DIL_PATTERNS)))
    return o.astype(q.dtype)


def dilated_attention_prompt(q, k, v, rel_bias):
    B, S, H, Dh = q.shape
    pad = jnp.zeros((B, WIN, H, Dh), k.dtype)
    kp = jnp.concatenate([pad, k], axis=1)
    vp = jnp.concatenate([pad, v], axis=1)
    base = WIN + jnp.arange(Q_BLOCK)

    def block(b):
        q0 = b * Q_BLOCK
        qb = lax.dynamic_slice_in_dim(q, q0, Q_BLOCK, axis=1)
        ks = lax.dynamic_slice_in_dim(kp, q0, WIN + Q_BLOCK, axis=1)
        vs = lax.dynamic_slice_in_dim(vp, q0, WIN + Q_BLOCK, axis=1)
        return dilated_attention(qb, ks, vs, base, WIN - q0, rel_bias)

    out = lax.map(block, jnp.arange(S // Q_BLOCK))
    return out.transpose(1, 0, 2, 3, 4).reshape(B, S, H, Dh)


def dilated_attention_sample(q, k_new, v_new, k_buf, v_buf, rel_bias):
    L = k_buf.shape[1]
    k_src = jnp.concatenate([k_buf, k_new], axis=1)
    v_src = jnp.concatenate([v_buf, v_new], axis=1)
    base = L + jnp.arange(q.shape[1])
    o = dilated_attention(q, k_src, v_src, base, 0, rel_bias)
    return o, k_src[:, -L:], v_src[:, -L:]


def causal_dwconv(u_ext, w, b):
    y = lax.conv_general_dilated(u_ext, w[:, None, :].astype(u_ext.dtype), window_strides=(1,),
                                 padding='VALID', dimension_numbers=('NWC', 'WIO', 'NWC'),
                                 feature_group_count=u_ext.shape[-1])
    return y + b


def conv_tail(c, g, b, w_pw2):
    return jax.nn.silu(layer_norm(c, g, b)) @ w_pw2


def cross_attention(q, mk, mv):
    logits = jnp.einsum('bthd,bmhd->bhtm', q, mk, preferred_element_type=jnp.float32) * SCALE
    p = jax.nn.softmax(logits, axis=-1)
    o = jnp.einsum('bhtm,bmhd->bthd', p, mv.astype(jnp.float32))
    return o.astype(q.dtype)


def mix_out(x, a, c, m, gate_att, gate_conv, gate_mem, w_out, g_post):
    y = jnp.concatenate([a * jax.nn.silu(gate_att), c * jax.nn.silu(gate_conv),
                         m * jax.nn.silu(gate_mem)], axis=-1) @ w_out
    return x + rms_norm(y, g_post)


def setup_inputs(seed: int = 0) -> dict:
    key = jax.random.key(seed)
    ks = jax.random.split(key, 20)
    n = jax.random.normal
    l_buf = min(WIN, PAST_LEN)
    return {
        'x_prompt': n(ks[0], (BATCH, SEQ, D_MODEL), jnp.float32),
        'x_sample': n(ks[1], (DEC_BATCH, DEC_SEQ, D_MODEL), jnp.float32),
        'mem_prompt': n(ks[2], (BATCH, N_MEM, D_MODEL), jnp.float32),
        'cache_attn_k': n(ks[3], (DEPTH, DEC_BATCH, l_buf, N_ATT_HEADS, HEAD_DIM), jnp.float32),
        'cache_attn_v': n(ks[4], (DEPTH, DEC_BATCH, l_buf, N_ATT_HEADS, HEAD_DIM), jnp.float32),
        'state_conv': 0.5 * n(ks[5], (DEPTH, DEC_BATCH, CONV_K - 1, CONV_CH), jnp.float32),
        'cache_mem_k': n(ks[6], (DEPTH, DEC_BATCH, N_MEM, N_X_HEADS, HEAD_DIM), jnp.float32),
        'cache_mem_v': n(ks[7], (DEPTH, DEC_BATCH, N_MEM, N_X_HEADS, HEAD_DIM), jnp.float32),
        'rel_bias': 0.1 * n(ks[8], (N_BUCKETS, N_ATT_HEADS), jnp.float32),
        'norm_pre_g': 1.0 + 0.02 * n(ks[9], (DEPTH, D_MODEL), jnp.float32),
        'w_in': n(ks[10], (DEPTH, D_MODEL, IN_W), jnp.float32) * D_MODEL ** -0.5,
        'w_dw': n(ks[11], (DEPTH, CONV_K, CONV_CH), jnp.float32) * CONV_K ** -0.5,
        'b_dw': 0.01 * n(ks[12], (DEPTH, CONV_CH), jnp.float32),
        'ln_conv_g': 1.0 + 0.02 * n(ks[13], (DEPTH, CONV_CH), jnp.float32),
        'ln_conv_b': 0.02 * n(ks[14], (DEPTH, CONV_CH), jnp.float32),
        'w_pw2': n(ks[15], (DEPTH, CONV_CH, CONV_CH), jnp.float32) * CONV_CH ** -0.5,
        'w_mem_kv': n(ks[16], (DEPTH, D_MODEL, 2 * X_W), jnp.float32) * D_MODEL ** -0.5,
        'w_out': n(ks[17], (DEPTH, MIX_W, D_MODEL), jnp.float32) * MIX_W ** -0.5,
        'norm_post_g': 1.0 + 0.02 * n(ks[18], (DEPTH, D_MODEL), jnp.float32),
    }


def reference(x_prompt, x_sample, mem_prompt, cache_attn_k, cache_attn_v, state_conv,
              cache_mem_k, cache_mem_v, rel_bias, norm_pre_g, w_in, w_dw, b_dw,
              ln_conv_g, ln_conv_b, w_pw2, w_mem_kv, w_out, norm_post_g):
    xp, xs = x_prompt, x_sample
    bp, s_len, _ = xp.shape
    bs, t_len, _ = xs.shape
    l_prompt = min(WIN, s_len)
    akp, avp, cvp, mkp, mvp, aks, avs, cvs = [], [], [], [], [], [], [], []
    for li in range(DEPTH):
        qa, ka, va, ga, uv, ug, gc, qm, gm = jnp.split(
            rms_norm(xp, norm_pre_g[li]) @ w_in[li], SPLIT_AT, axis=-1)
        qa, ka, va = heads(qa, N_ATT_HEADS), heads(ka, N_ATT_HEADS), heads(va, N_ATT_HEADS)
        a = dilated_attention_prompt(qa, ka, va, rel_bias).reshape(bp, s_len, ATT_W)
        u = uv * jax.nn.sigmoid(ug)
        u_ext = jnp.concatenate([jnp.zeros((bp, CONV_K - 1, CONV_CH), u.dtype), u], axis=1)
        c = conv_tail(causal_dwconv(u_ext, w_dw[li], b_dw[li]), ln_conv_g[li], ln_conv_b[li], w_pw2[li])
        mk, mv = jnp.split(mem_prompt @ w_mem_kv[li], 2, axis=-1)
        mk, mv = heads(mk, N_X_HEADS), heads(mv, N_X_HEADS)
        m = cross_attention(heads(qm, N_X_HEADS), mk, mv).reshape(bp, s_len, X_W)
        akp.append(ka[:, -l_prompt:])
        avp.append(va[:, -l_prompt:])
        cvp.append(u_ext[:, -(CONV_K - 1):])
        mkp.append(mk)
        mvp.append(mv)
        xp = mix_out(xp, a, c, m, ga, gc, gm, w_out[li], norm_post_g[li])

        qa, ka, va, ga, uv, ug, gc, qm, gm = jnp.split(
            rms_norm(xs, norm_pre_g[li]) @ w_in[li], SPLIT_AT, axis=-1)
        qa, ka, va = heads(qa, N_ATT_HEADS), heads(ka, N_ATT_HEADS), heads(va, N_ATT_HEADS)
        a, k_buf, v_buf = dilated_attention_sample(qa, ka, va, cache_attn_k[li], cache_attn_v[li], rel_bias)
        a = a.reshape(bs, t_len, ATT_W)
        u = uv * jax.nn.sigmoid(ug)
        u_ext = jnp.concatenate([state_conv[li].astype(u.dtype), u], axis=1)
        c = conv_tail(causal_dwconv(u_ext, w_dw[li], b_dw[li]), ln_conv_g[li], ln_conv_b[li], w_pw2[li])
        m = cross_attention(heads(qm, N_X_HEADS), cache_mem_k[li], cache_mem_v[li]).reshape(bs, t_len, X_W)
        aks.append(k_buf)
        avs.append(v_buf)
        cvs.append(u_ext[:, -(CONV_K - 1):])
        xs = mix_out(xs, a, c, m, ga, gc, gm, w_out[li], norm_post_g[li])

    return (xp, xs, jnp.stack(akp), jnp.stack(avp), jnp.stack(cvp), jnp.stack(mkp), jnp.stack(mvp),
            jnp.stack(aks), jnp.stack(avs), jnp.stack(cvs))
```

```python
import numpy as np
import concourse.bass as bass
import concourse.mybir as mybir
from concourse.bass_utils import run_bass_kernel_spmd

F32 = mybir.dt.float32
BF16 = mybir.dt.bfloat16
AF = mybir.ActivationFunctionType
AL = mybir.AluOpType

D_MODEL = 2048
BATCH = 2
SEQ = 4096
DEPTH = 4
DEC_BATCH = 8
DEC_SEQ = 8
N_MEM = 256
HEAD_DIM = 128
ATT_W = 1024
N_ATT_HEADS = 8
DILS = (1, 4, 16)
WIN = 2048
N_BUCKETS = 32
MAX_DIST = WIN
CONV_CH = 512
CONV_K = 31
X_W = 512
N_X_HEADS = 4
MIX_W = 2048
IN_W = 6656
EPS = 1e-6
SCALE = HEAD_DIM ** -0.5
N_CORES = 8
T = 512
NCH = SEQ // T
L_CACHE = 2048
NS = DEC_SEQ

C_Q, C_K, C_V, C_GA, C_UV, C_UG, C_GC, C_QM, C_GM = 0, 1024, 2048, 3072, 4096, 4608, 5120, 5632, 6144


def t5_bucket(dist):
    dist = np.asarray(dist)
    max_exact = N_BUCKETS // 2
    large = max_exact + (np.log(np.maximum(dist, 1) / max_exact)
                         / np.log(MAX_DIST / max_exact) * (N_BUCKETS - max_exact)).astype(np.int32)
    large = np.minimum(large, N_BUCKETS - 1)
    return np.where(dist < max_exact, dist, large).astype(np.int32)


class Res:
    __slots__ = ("name", "w", "r", "slot", "dram", "excl")

    def __init__(self, name, dram=False, excl=False):
        self.name = name
        self.w = {}
        self.r = {}
        self.slot = None
        self.dram = dram
        self.excl = excl


class Eng:
    def __init__(self, name, h, sem):
        self.name, self.h, self.sem, self.count = name, h, sem, 0


class Slot:
    def __init__(self, sem):
        self.sem, self.count = sem, 0


class _Stop(Exception):
    pass


class Prog:
    def __init__(self, n_chunks=NCH, depth=DEPTH, with_sample=True, stop_after=None):
        self.n_chunks, self.depth, self.with_sample = n_chunks, depth, with_sample
        self.stop_after = stop_after
        self.SEQR = n_chunks * T
        nc = self.nc = bass.Bass("TRN2", target_bir_lowering=False)
        self.semid = 0
        self.PE = Eng("pe", nc.tensor, self.new_sem())
        self.ACT = Eng("act", nc.scalar, self.new_sem())
        self.DVE = Eng("dve", nc.vector, self.new_sem())
        self.POOL = Eng("pool", nc.gpsimd, self.new_sem())
        self.SP = Eng("sp", nc.sync, self.new_sem())
        self.waited = {}
        self.slots_by_name = {}
        self.pe_last = None
        self.spt = 0
        self.out_tokens = {}
        self.sem_names = {}
        self.declare_io()
        self.alloc()
        self.bank_i = 0
        self.ee = 0

    def new_sem(self):
        self.semid += 1
        return self.nc.alloc_semaphore("s%d" % self.semid)

    def new_slot(self):
        return Slot(self.new_sem())

    def _wait(self, eng, tok):
        sem, val = tok
        if sem is self.PE.sem and val > self.PE.count:
            assert self.pe_last is not None and val == self.PE.count + 1
            self.pe_last.then_inc(self.PE.sem, 1)
            self.PE.count += 1
            self.pe_last = None
        key = (eng.name, id(sem))
        if self.waited.get(key, 0) >= val:
            return
        eng.h.wait_ge(sem, val)
        self.waited[key] = val

    def _deps(self, eng, reads, writes, skip_waw=False):
        toks = []
        for r in reads:
            toks.extend(r.w.values())
            if r.excl:
                toks.extend(r.r.values())
        for w in writes:
            if not skip_waw:
                toks.extend(w.w.values())
            toks.extend(w.r.values())
        for t in toks:
            if eng is self.PE and t[0] is self.PE.sem:
                continue
            self._wait(eng, t)

    def _update(self, tok, reads, writes, accumulate=False):
        k = id(tok[0])
        for w in writes:
            if accumulate:
                if k not in w.w or w.w[k][1] < tok[1]:
                    w.w[k] = tok
            else:
                w.w = {k: tok}
            w.r = {}
        for r in reads:
            if r in writes:
                continue
            if k not in r.r or r.r[k][1] < tok[1]:
                r.r[k] = tok

    def op(self, eng, fn, reads=(), writes=(), signal=True):
        self._deps(eng, reads, writes)
        ins = fn()
        if signal:
            eng.count += 1
            ins.then_inc(eng.sem, 1)
            tok = (eng.sem, eng.count)
            if eng is self.PE:
                self.pe_last = None
        else:
            assert eng is self.PE
            tok = (eng.sem, eng.count + 1)
            self.pe_last = ins
        self._update(tok, reads, writes)
        return tok

    def dma(self, q, out, in_, reads, writes, is_output=False):
        owner = None
        for w in writes:
            if not w.dram:
                owner = w
                break
        if owner is None:
            owner = reads[0]
        key = (owner.name, q.name)
        if key not in self.slots_by_name:
            self.slots_by_name[key] = self.new_slot()
        slot = self.slots_by_name[key]
        self._deps(q, reads, writes, skip_waw=True)
        ins = q.h.dma_start(out=out, in_=in_)
        slot.count += 16
        ins.then_inc(slot.sem, 16)
        tok = (slot.sem, slot.count)
        self._update(tok, reads, writes, accumulate=True)
        if is_output:
            self.out_tokens[id(slot.sem)] = tok
        return tok

    def any_copy(self, out, in_, reads, writes):
        self.ee += 1
        if self.ee % 2:
            return self.op(self.ACT, lambda: self.nc.scalar.activation(out=out, in_=in_, func=AF.Copy), reads, writes)
        return self.op(self.DVE, lambda: self.nc.vector.tensor_copy(out=out, in_=in_), reads, writes)

    def bank(self):
        b = self.bank_i % 8
        self.bank_i += 1
        return b

    def mm(self, out, lhsT, rhs, start, stop, reads, writes, signal=None, **kw):
        if signal is None:
            signal = stop
        return self.op(self.PE, lambda: self.nc.tensor.matmul(out, lhsT, rhs, start=start, stop=stop, **kw),
                       reads, writes, signal=signal)

    def declare_io(self):
        nc = self.nc

        def din(name, shape, dt=F32):
            return nc.dram_tensor(name, list(shape), dt, kind="ExternalInput").ap()

        def dout(name, shape, dt=F32):
            return nc.dram_tensor(name, list(shape), dt, kind="ExternalOutput").ap()

        self.x_prompt = din("x_prompt", [self.SEQR, D_MODEL])
        self.memT = din("memT", [D_MODEL, N_MEM])
        self.rel_bias = din("rel_bias", [N_BUCKETS, N_ATT_HEADS])
        self.ohrev = din("ohrev", [3, N_BUCKETS, 384])
        self.ident_in = din("ident", [128, 128])
        self.g_pre = din("norm_pre_g", [self.depth, D_MODEL])
        self.g_post = din("norm_post_g", [self.depth, D_MODEL])
        self.w_in = din("w_in", [self.depth, D_MODEL, IN_W])
        self.w_out = din("w_out", [self.depth, MIX_W, D_MODEL])
        self.w_pw2 = din("w_pw2", [self.depth, CONV_CH, CONV_CH])
        self.w_mem_kv = din("w_mem_kv", [self.depth, D_MODEL, 2 * X_W])
        self.wdwT = din("wdwT", [128, DEPTH * 4 * CONV_K])
        self.cpar = din("cpar", [128, 3 * DEPTH * 4])
        self.y_prompt = dout("y_prompt", [self.SEQR, D_MODEL])
        self.akp = dout("akp", [self.depth, WIN, ATT_W])
        self.avp = dout("avp", [self.depth, WIN, ATT_W])
        self.cvp = dout("cvp", [self.depth, CONV_K - 1, CONV_CH])
        self.mkp = dout("mkp", [self.depth, N_MEM, X_W])
        self.mvp = dout("mvp", [self.depth, N_MEM, X_W])
        if self.with_sample:
            self.x_sample = din("x_sample", [NS, D_MODEL])
            self.cache_k = din("cache_k", [self.depth, L_CACHE, ATT_W])
            self.cache_v = din("cache_v", [self.depth, L_CACHE, ATT_W])
            self.state_conv = din("state_convT", [128, DEPTH * 4 * (CONV_K - 1)])
            self.cmk = din("cache_mem_k", [self.depth, N_MEM, X_W])
            self.cmv = din("cache_mem_v", [self.depth, N_MEM, X_W])
            self.ohs = din("ohs", [N_BUCKETS, 2064])
            self.y_sample = dout("y_sample", [NS, D_MODEL])
            self.aks = dout("aks", [self.depth, L_CACHE, ATT_W])
            self.avs = dout("avs", [self.depth, L_CACHE, ATT_W])
            self.cvs = dout("cvs", [self.depth, CONV_K - 1, CONV_CH])
        self.KTd = nc.dram_tensor("KTd", [DEPTH, N_ATT_HEADS, 128, SEQ], BF16).ap()
        self.Vd = nc.dram_tensor("Vd", [DEPTH, SEQ, ATT_W], BF16).ap()
        self.Rd = nc.dram_tensor("Rd", [3, N_ATT_HEADS, 384], F32).ap()
        self.Ed = nc.dram_tensor("Ed", [N_ATT_HEADS, 128, 6, 128], F32).ap()
        self.Fd = nc.dram_tensor("Fd", [N_ATT_HEADS, 2064], F32).ap()
        self.r_Fd = Res("Fd", dram=True)
        self.MKd = nc.dram_tensor("MKd", [DEPTH, 128, N_X_HEADS * N_MEM], BF16).ap()
        self.MVd = nc.dram_tensor("MVd", [DEPTH, 128, 2 * X_W], BF16).ap()
        self.r_KTd = [[Res("KTd%d_%d" % (l, c), dram=True) for c in range(NCH)] for l in range(DEPTH)]
        self.r_Vd = [[Res("Vd%d_%d" % (l, c), dram=True) for c in range(NCH)] for l in range(DEPTH)]
        self.r_Rd, self.r_Ed = Res("Rd", dram=True), Res("Ed", dram=True)
        self.r_MKd = [Res("MKd%d" % l, dram=True) for l in range(DEPTH)]
        self.r_MVd = [Res("MVd%d" % l, dram=True) for l in range(DEPTH)]

    def alloc(self):
        nc = self.nc
        base = (nc.sbuf_base + 63) // 64 * 64
        top = nc.sbuf_top
        self._arena = nc.alloc_sbuf_tensor("arena", [128, top - base - 64], mybir.dt.uint8)
        self._off = base
        self._top = top - 64
        self.res = {}

        def A(name, shape, dt, nres=None):
            nbytes = int(np.prod(shape[1:])) * (4 if dt == F32 else 2)
            nbytes = (nbytes + 63) // 64 * 64
            t = nc.alloc_sbuf_tensor_at(name, list(shape), dt, offset=self._off)
            if getattr(self, "_pend_lo", None) is None:
                self._pend_lo = self._off
            self._off += nbytes
            assert self._off <= self._top, "SBUF overflow at %s: %d > %d" % (name, self._off, self._top)
            return t

        self.A = A
        NT = T + NS
        self.X = A("X", [128, 4, D_MODEL], F32)
        self.rX = [Res("X%d" % b) for b in range(4)]
        self.MIX = A("MIX", [128, 16, NT], BF16)
        self.rMIX = [Res("MIX%d" % f) for f in range(16)]
        self.QT = A("QT", [128, 8, NT], BF16)
        self.rQT = [Res("QT%d" % h) for h in range(8)]
        self.QMT = A("QMT", [128, 4, NT], BF16)
        self.rQMT = [Res("QMT%d" % h) for h in range(4)]
        self.KTown = A("KTown", [128, 8, T], BF16)
        self.rKTown = [Res("KTown%d" % g) for g in range(2)]
        self.NW = 2
        self.W = [A("W%d" % i, [128, 16, 512], BF16) for i in range(self.NW)]
        self.rW = [Res("W%d" % i) for i in range(self.NW)]
        self.rMEMK, self.rMEMV = Res("MEMK"), Res("MEMV")
        self.wi = 0
        self.MEMK = A("MEMK", [128, 4, N_MEM], BF16)
        self.MEMV = A("MEMV", [128, 2, X_W], BF16)
        self.UH = A("UH", [128, DEPTH, 4, CONV_K - 1], F32)
        self.rUH = [Res("UH%d" % l) for l in range(DEPTH)]
        self.WDW = A("WDW", [128, DEPTH, 4, CONV_K], F32)
        self.CPAR = A("CPAR", [128, 3, DEPTH, 4], F32)
        self.rPAR = Res("PAR")
        self.IDB = A("IDB", [128, 128], BF16)
        self.IDF = A("IDF", [128, 128], F32)
        self.ONB = A("ONB", [128, 128], BF16)
        self.ONF = A("ONF", [128, 128], F32)
        self.rCONST = Res("CONST")
        self.rIDF, self.rIDB = Res("IDF"), Res("IDB")
        self.SS = A("SS", [128, 16], F32)
        self.rSS = [Res("SS%d" % b) for b in range(4)]
        if self.with_sample:
            self.XS = A("XS", [NS, D_MODEL], F32)
            self.KVS = A("KVS", [NS, 2 * ATT_W], F32)
            self.ES = A("ES", [128, N_ATT_HEADS, 17, NS], F32)
            self.SCS = A("SCS", [128, DEPTH, 4, CONV_K - 1], F32)
            self.rXS, self.rKVS, self.rES, self.rSCS = Res("XS"), Res("KVS"), Res("ES"), Res("SCS")
            self.rCPY = [Res("CPY%d" % i, dram=True) for i in range(2 * DEPTH)]
        self._persist_end = self._off
        self.PS = [nc.alloc_psum_tensor("ps%d" % i, [128, 512], F32) for i in range(8)]
        self.PSB = [p.bitcast(BF16) for p in self.PS]
        self.rPS = [Res("PS%d" % i, excl=True) for i in range(8)]

    def stage(self, name):
        if self.stop_after == name:
            raise _Stop()

    def overlay(self):
        old_dead = getattr(self, "dead", [])
        new_dead = []
        for r, (lo, hi) in getattr(self, "region_res", []):
            toks = {}
            for d in (r.w, r.r):
                for k, tok in d.items():
                    if k not in toks or toks[k][1] < tok[1]:
                        toks[k] = tok
            new_dead.append((lo, hi, toks))
        dead = [e for e in old_dead if not any(lo <= e[0] and e[1] <= hi for (lo, hi, _) in new_dead)] + new_dead
        self.dead = dead
        self.region_res = []
        self._off = self._persist_end
        self._pend_lo = None
        self._last_range = None

    def LR(self, name):
        r = Res(name)
        if self._pend_lo is not None:
            self._last_range = (self._pend_lo, self._off)
            self._pend_lo = None
        lo, hi = self._last_range
        for (a, b, toks) in self.dead:
            if a < hi and lo < b:
                for k, tok in toks.items():
                    if k not in r.r or r.r[k][1] < tok[1]:
                        r.r[k] = tok
        self.region_res.append((r, (lo, hi)))
        return r

    def wload(self, src_ap, kdim=16, ncols=512):
        i = self.wi % self.NW
        self.wi += 1
        self.dma(self.POOL, self.W[i][:, 0:kdim, 0:ncols], src_ap, [], [self.rW[i]])
        return self.W[i], self.rW[i]

    def setup(self):
        nc = self.nc
        P = self
        self.overlay()
        A = self.A
        P.dma(P.SP, P.IDF[:], P.ident_in, [], [P.rIDF])
        P.dma(P.POOL, P.IDB[:], P.ident_in, [], [P.rIDB])
        P.op(P.POOL, lambda: nc.gpsimd.memset(P.ONB[:], 1.0), [], [P.rCONST])
        P.op(P.POOL, lambda: nc.gpsimd.memset(P.ONF[:], 1.0), [], [P.rCONST])
        P.dma(P.SP, P.WDW[:].rearrange("p l c k -> p (l c k)"), P.wdwT, [], [P.rPAR])
        P.dma(P.SP, P.CPAR[:].rearrange("p w l c -> p (w l c)"), P.cpar, [], [P.rPAR])
        for l in range(DEPTH):
            P.op(P.POOL, lambda l=l: nc.gpsimd.memset(P.UH[:, l], 0.0), [], [P.rUH[l]])
        P.stage('s_const')
        RB = A("RB", [32, 8], F32)
        EB = A("EB", [32, 8], F32)
        OH = A("OH", [32, 3, 384], F32)
        RS = A("RS", [8, 3, 384], F32)
        HK = A("HK", [128, 8, 128], F32)
        EV = A("EV", [128, 8, 128], F32)
        rRB, rOH, rRS, rHK, rEV = P.LR("RB"), P.LR("OH"), P.LR("RS"), P.LR("HK"), P.LR("EV")
        P.dma(P.SP, RB[:], P.rel_bias, [], [rRB])
        P.dma(P.SP, OH[:], P.ohrev.rearrange("p b n -> b p n"), [], [rOH])
        P.op(P.ACT, lambda: nc.scalar.activation(out=EB[:], in_=RB[:], func=AF.Exp), [rRB], [rRB])
        for p in range(3):
            b = P.bank()
            P.mm(P.PS[b][0:8, 0:384], EB[:], OH[:, p, :], True, True, [rRB, rOH], [P.rPS[b]])
            P.op(P.DVE, lambda p=p, b=b: nc.vector.tensor_copy(out=RS[:, p, :], in_=P.PS[b][0:8, 0:384]),
                 [P.rPS[b]], [rRS])
        P.stage('s_mm')
        P.dma(P.SP, P.Rd.rearrange("p h n -> h p n"), RS[:], [rRS], [P.r_Rd])
        P.stage('s_rd')
        for p in range(3):
            for role in range(2):
                a = 128 if role == 0 else 0
                src = bass.AP(tensor=P.Rd.tensor, offset=p * 8 * 384 + a, ap=[[1, 128], [384, 8], [1, 128]])
                P.dma(P.SP, HK[:], src, [P.r_Rd], [rHK])
                P.op(P.POOL, lambda: nc.gpsimd.tensor_copy(out=EV[:], in_=HK[:, :, ::-1]), [rHK], [rEV])
                P.dma(P.SP, P.Ed[:, :, 2 * p + role, :].rearrange("h k i -> k h i"), EV[:], [rEV], [P.r_Ed])
        if self.with_sample:
            self.setup_sample(EB, rRB)
        P.stage('s_etab')
        MT = A("MT", [128, 16, N_MEM], BF16)
        rMT = P.LR("MT")
        MKs = A("MKs", [128, 4, N_MEM], BF16)
        MVs = A("MVs", [128, 2, X_W], BF16)
        MF = [A("MF%d" % i, [128, 512], F32) for i in range(2)]
        rMKs, rMVs, rMF = P.LR("MKs"), P.LR("MVs"), [P.LR("MF0"), P.LR("MF1")]
        P.dma(P.POOL, MT[:], P.memT.rearrange("(k p) m -> p k m", p=128), [], [rMT])
        mfi = 0
        for l in range(self.depth):
            wk, rwk = P.wload(P.w_mem_kv[l].rearrange("(k p) n -> p k n", p=128)[:, :, 0:512])
            for h in range(4):
                b = P.bank()
                for k in range(16):
                    P.mm(P.PS[b][:, 0:N_MEM], wk[:, k, h * 128:(h + 1) * 128], MT[:, k, :], k == 0, k == 15,
                         [rwk, rMT], [P.rPS[b]])
                P.any_copy(MKs[:, h, :], P.PS[b][:, 0:N_MEM], [P.rPS[b]], [rMKs])
            for mb in range(2):
                b = P.bank()
                for k in range(16):
                    P.mm(P.PS[b][:], MT[:, k, mb * 128:(mb + 1) * 128], wk[:, k, :], k == 0, k == 15,
                         [rwk, rMT], [P.rPS[b]])
                j = mfi % 2
                mfi += 1
                P.any_copy(MF[j][:], P.PS[b][:], [P.rPS[b]], [rMF[j]])
                P.dma(P.SP, P.mkp[l, mb * 128:(mb + 1) * 128, :], MF[j][:], [rMF[j]], [], is_output=True)
            wv, rwv = P.wload(P.w_mem_kv[l].rearrange("(k p) n -> p k n", p=128)[:, :, 512:1024])
            for mb in range(2):
                b = P.bank()
                for k in range(16):
                    P.mm(P.PS[b][:], MT[:, k, mb * 128:(mb + 1) * 128], wv[:, k, :], k == 0, k == 15,
                         [rwv, rMT], [P.rPS[b]])
                j = mfi % 2
                mfi += 1
                P.op(P.ACT, lambda b=b, j=j: nc.scalar.activation(out=MF[j][:], in_=P.PS[b][:], func=AF.Copy),
                     [P.rPS[b]], [rMF[j]])
                P.op(P.DVE, lambda b=b, mb=mb: nc.vector.tensor_copy(out=MVs[:, mb, :], in_=P.PS[b][:]),
                     [P.rPS[b]], [rMVs])
                P.dma(P.SP, P.mvp[l, mb * 128:(mb + 1) * 128, :], MF[j][:], [rMF[j]], [], is_output=True)
            P.dma(P.SP, P.MKd[l], MKs[:].rearrange("p h m -> p (h m)"), [rMKs], [P.r_MKd[l]])
            P.dma(P.SP, P.MVd[l], MVs[:].rearrange("p b n -> p (b n)"), [rMVs], [P.r_MVd[l]])


    def setup_sample(self, EB, rEB):
        nc = self.nc
        P = self
        A = self.A
        P.dma(P.SP, P.XS[:], P.x_sample, [], [P.rXS])
        P.dma(P.SP, P.SCS[:].rearrange("p l c k -> p (l c k)"), P.state_conv, [], [P.rSCS])
        for l in range(self.depth):
            P.dma(P.SP, P.aks[l, 0:L_CACHE - NS, :], P.cache_k[l, NS:L_CACHE, :], [P.rCPY[2 * l]], [], is_output=True)
            P.dma(P.SP, P.avs[l, 0:L_CACHE - NS, :], P.cache_v[l, NS:L_CACHE, :], [P.rCPY[2 * l + 1]], [], is_output=True)
        OHS = A("OHS", [32, 2064], F32)
        RSS = A("RSS", [8, 2064], F32)
        HKS = A("HKS", [128, 17, NS], F32)
        rOHS, rRSS, rHKS = P.LR("OHS"), P.LR("RSS"), P.LR("HKS")
        P.dma(P.SP, OHS[:], P.ohs, [], [rOHS])
        for j in range(5):
            n0, n1 = j * 512, min(2064, (j + 1) * 512)
            b = P.bank()
            P.mm(P.PS[b][0:8, 0:n1 - n0], EB[:], OHS[:, n0:n1], True, True, [rEB, rOHS], [P.rPS[b]])
            P.op(P.DVE, lambda b=b, n0=n0, n1=n1: nc.vector.tensor_copy(out=RSS[:, n0:n1], in_=P.PS[b][0:8, 0:n1 - n0]),
                 [P.rPS[b]], [rRSS])
        P.dma(P.SP, P.Fd, RSS[:], [rRSS], [P.r_Fd])
        for h in range(N_ATT_HEADS):
            src = bass.AP(tensor=P.Fd.tensor, offset=h * 2064, ap=[[1, 128], [128, 16], [1, NS]])
            P.dma(P.SP, HKS[:, 0:16, :], src, [P.r_Fd], [rHKS])
            src2 = bass.AP(tensor=P.Fd.tensor, offset=h * 2064 + 2048, ap=[[1, NS], [1, NS]])
            P.dma(P.SP, HKS[0:NS, 16, :], src2, [P.r_Fd], [rHKS])
            P.op(P.POOL, lambda h=h: nc.gpsimd.tensor_copy(out=P.ES[:, h, 0:16, :], in_=HKS[:, 0:16, ::-1]),
                 [rHKS], [P.rES])
            P.op(P.POOL, lambda h=h: nc.gpsimd.tensor_copy(out=P.ES[0:NS, h, 16, :], in_=HKS[0:NS, 16, ::-1]),
                 [rHKS], [P.rES])

    def chunk_layer(self, c, l):
        nc = self.nc
        P = self
        t0 = c * T
        last = (l == self.depth - 1)
        self.overlay()
        A = self.A
        HT = A("HT", [128, 16, T + NS], BF16)
        rHT = [P.LR("HT%d" % b) for b in range(5)]
        GT = A("GT", [128, D_MODEL], F32)
        rGT = P.LR("GT")
        HB = [A("HB%d" % i, [128, D_MODEL], BF16) for i in range(2)]
        rHB = [P.LR("HB0"), P.LR("HB1")]
        JUNK = A("JUNK", [128, D_MODEL], BF16)
        rJUNK = P.LR("JUNK")
        STF = [A("STF%d" % i, [128, 512], F32) for i in range(2)]
        rSTF = [P.LR("STF0"), P.LR("STF1")]
        STB = [A("STB%d" % i, [128, 512], BF16) for i in range(2)]
        rSTB = [P.LR("STB0"), P.LR("STB1")]
        UT = A("UT", [128, 4, T + CONV_K - 1], F32)
        rUT = [P.LR("UT%d" % i) for i in range(4)]
        SG = [A("SG%d" % i, [128, T], F32) for i in range(2)]
        rSG = [P.LR("SG0"), P.LR("SG1")]
        CC = A("CC", [128, 4, T], F32)
        rCC = [P.LR("CC%d" % i) for i in range(4)]
        LNM = A("LNM", [128, T], F32)
        LNR = A("LNR", [128, T], F32)
        rLNM, rLNR = P.LR("LNM"), P.LR("LNR")
        CNT = A("CNT", [128, 4, T], BF16)
        rCNT = [P.LR("CNT%d" % i) for i in range(4)]
        samp = self.with_sample and c == 0
        if samp:
            UTS = A("UTS", [128, 4, NS + CONV_K - 1], F32)
            rUTS = [P.LR("UTS%d" % i) for i in range(4)]
            HBS = A("HBS", [NS, D_MODEL], BF16)
            rHBS = P.LR("HBS")

        P.dma(P.SP, GT[:], bass.AP(tensor=P.g_pre.tensor, offset=l * D_MODEL, ap=[[0, 128], [1, D_MODEL]]),
              [], [rGT])
        for blk in range(4):
            P.op(P.ACT, lambda blk=blk: nc.scalar.activation(out=JUNK[:], in_=P.X[:, blk, :], func=AF.Square,
                                                             accum_out=P.SS[:, blk:blk + 1]),
                 [P.rX[blk]], [rJUNK, P.rSS[blk]])
            self.rstd(P.SS[:, blk:blk + 1], P.rSS[blk], 1.0 / D_MODEL)
            hb = blk % 2
            P.op(P.DVE, lambda blk=blk, hb=hb: nc.vector.scalar_tensor_tensor(
                out=HB[hb][:], in0=P.X[:, blk, :], scalar=P.SS[:, blk:blk + 1], in1=GT[:], op0=AL.mult, op1=AL.mult),
                [P.rX[blk], P.rSS[blk], rGT], [rHB[hb]])
            for dg in range(4):
                b = P.bank()
                for j in range(4):
                    dm = dg * 4 + j
                    P.op(P.PE, lambda dm=dm, j=j, b=b, hb=hb: nc.tensor.transpose(
                        P.PSB[b][:, j * 128:(j + 1) * 128], HB[hb][:, dm * 128:(dm + 1) * 128], P.IDB[:]),
                        [rHB[hb], P.rIDB], [P.rPS[b]], signal=(j == 3))
                P.any_copy(HT[:, dg * 4:dg * 4 + 4, blk * 128:(blk + 1) * 128],
                           P.PSB[b][:, 0:512].rearrange("p (j t) -> p j t", j=4), [P.rPS[b]], [rHT[blk]])

        if samp:
            P.op(P.ACT, lambda: nc.scalar.activation(out=JUNK[0:NS, :], in_=P.XS[:], func=AF.Square,
                                                     accum_out=P.SS[0:NS, 8:9]), [P.rXS], [rJUNK, P.rSS[0]])
            self.rstd(P.SS[0:NS, 8:9], P.rSS[0], 1.0 / D_MODEL)
            P.op(P.DVE, lambda: nc.vector.scalar_tensor_tensor(
                out=HBS[:], in0=P.XS[:], scalar=P.SS[0:NS, 8:9], in1=GT[0:NS, :], op0=AL.mult, op1=AL.mult),
                [P.rXS, P.rSS[0], rGT], [rHBS])
            b = P.bank()
            for dm in range(16):
                P.op(P.PE, lambda dm=dm, b=b: nc.tensor.transpose(
                    P.PSB[b][:, dm * NS:(dm + 1) * NS], HBS[:, dm * 128:(dm + 1) * 128], P.IDB[0:NS, 0:NS]),
                    [rHBS, P.rIDB], [P.rPS[b]], signal=(dm == 15))
            P.any_copy(HT[:, :, T:T + NS], P.PSB[b][:, 0:16 * NS].rearrange("p (k t) -> p k t", k=16),
                       [P.rPS[b]], [rHT[4]])
        P.stage('A')
        def wsrc(col0):
            return P.w_in[l].rearrange("(k p) n -> p k n", p=128)[:, :, col0:col0 + 512]

        stg = 0
        for gi, col0 in enumerate((C_K, C_K + 512, C_V, C_V + 512)):
            w, rw = P.wload(wsrc(col0))
            isK = gi < 2
            for blk in range(4):
                b = P.bank()
                for k in range(16):
                    P.mm(P.PS[b][:], HT[:, k, blk * 128:(blk + 1) * 128], w[:, k, :], k == 0, k == 15,
                         [rw, rHT[blk]], [P.rPS[b]])
                j = stg % 2
                stg += 1
                P.op(P.ACT, lambda b=b, j=j: nc.scalar.activation(out=STB[j][:], in_=P.PS[b][:], func=AF.Copy),
                     [P.rPS[b]], [rSTB[j]])
                seq_run = self.n_chunks * T
                lp = min(WIN, seq_run)
                if t0 >= seq_run - lp:
                    row0 = t0 - (seq_run - lp) + blk * 128
                    dst = (P.akp if isK else P.avp)[l, row0:row0 + 128, (gi % 2) * 512:(gi % 2) * 512 + 512]
                    P.op(P.DVE, lambda b=b, j=j: nc.vector.tensor_copy(out=STF[j][:], in_=P.PS[b][:]),
                         [P.rPS[b]], [rSTF[j]])
                    P.dma(P.SP, dst, STF[j][:], [rSTF[j]], [], is_output=True)
                if isK:
                    b2 = P.bank()
                    for jj in range(4):
                        P.op(P.PE, lambda jj=jj, b2=b2, j=j: nc.tensor.transpose(
                            P.PSB[b2][:, jj * 128:(jj + 1) * 128], STB[j][:, jj * 128:(jj + 1) * 128], P.IDB[:]),
                            [rSTB[j], P.rIDB], [P.rPS[b2]], signal=(jj == 3))
                    P.any_copy(P.KTown[:, gi * 4:gi * 4 + 4, blk * 128:(blk + 1) * 128],
                               P.PSB[b2][:, 0:512].rearrange("p (j t) -> p j t", j=4), [P.rPS[b2]], [P.rKTown[gi]])
                else:
                    vc = (gi - 2) * 512
                    P.dma(P.SP, P.Vd[l, t0 + blk * 128:t0 + (blk + 1) * 128, vc:vc + 512], STB[j][:],
                          [rSTB[j]], [P.r_Vd[l][c]])
            if samp:
                b = P.bank()
                for k in range(16):
                    P.mm(P.PS[b][0:NS, :], HT[:, k, T:T + NS], w[:, k, :], k == 0, k == 15, [rw, rHT[4]], [P.rPS[b]])
                P.any_copy(P.KVS[:, gi * 512:(gi + 1) * 512], P.PS[b][0:NS, :], [P.rPS[b]], [P.rKVS])
        P.dma(P.SP, P.KTd[l, :, :, t0:t0 + T].rearrange("h d t -> d h t"), P.KTown[:],
              [P.rKTown[0], P.rKTown[1]], [P.r_KTd[l][c]])

        if samp:
            P.dma(P.SP, P.aks[l, L_CACHE - NS:L_CACHE, :], P.KVS[:, 0:ATT_W], [P.rKVS], [], is_output=True)
            P.dma(P.SP, P.avs[l, L_CACHE - NS:L_CACHE, :], P.KVS[:, ATT_W:2 * ATT_W], [P.rKVS], [], is_output=True)
        P.stage('Bkv')

        def fm_group(col0, evac):
            w, rw = P.wload(wsrc(col0))
            for cc in range(4):
                b = P.bank()
                for k in range(16):
                    P.mm(P.PS[b][:], w[:, k, cc * 128:(cc + 1) * 128], HT[:, k, 0:T], k == 0, k == 15,
                         [rw] + rHT[0:4], [P.rPS[b]])
                evac(cc, P.PS[b][:], P.rPS[b], False)
                if samp:
                    b = P.bank()
                    for k in range(16):
                        P.mm(P.PS[b][:, 0:NS], w[:, k, cc * 128:(cc + 1) * 128], HT[:, k, T:T + NS], k == 0, k == 15,
                             [rw, rHT[4]], [P.rPS[b]])
                    evac(cc, P.PS[b][:, 0:NS], P.rPS[b], True)

        def ucols(sm):
            return (UTS, rUTS, NS) if sm else (UT, rUT, T)

        def mcols(sm):
            return slice(T, T + NS) if sm else slice(0, T)

        def ev_uv(cc, ps, rps, sm):
            u, ru, n = ucols(sm)
            P.op(P.DVE, lambda: nc.vector.tensor_copy(out=u[:, cc, CONV_K - 1:], in_=ps), [rps], [ru[cc]])

        def ev_ug(cc, ps, rps, sm):
            u, ru, n = ucols(sm)
            j = cc % 2
            P.op(P.ACT, lambda: nc.scalar.activation(out=SG[j][:, 0:n], in_=ps, func=AF.Sigmoid), [rps], [rSG[j]])
            P.op(P.DVE, lambda: nc.vector.tensor_tensor(out=u[:, cc, CONV_K - 1:], in0=u[:, cc, CONV_K - 1:],
                                                        in1=SG[j][:, 0:n], op=AL.mult), [rSG[j], ru[cc]], [ru[cc]])

        def ev_silu(fbase):
            def f(cc, ps, rps, sm):
                P.op(P.ACT, lambda: nc.scalar.activation(out=P.MIX[:, fbase + cc, mcols(sm)], in_=ps, func=AF.Silu),
                     [rps], [P.rMIX[fbase + cc]])
            return f

        def ev_copy(dst, rdst, hbase):
            def f(cc, ps, rps, sm):
                P.op(P.ACT, lambda: nc.scalar.activation(out=dst[:, hbase + cc, mcols(sm)], in_=ps, func=AF.Copy),
                     [rps], [rdst[hbase + cc]])
            return f

        for cc in range(4):
            P.op(P.POOL, lambda cc=cc: nc.gpsimd.tensor_copy(out=UT[:, cc, 0:CONV_K - 1], in_=P.UH[:, l, cc, :]),
                 [P.rUH[l]], [rUT[cc]])
        if samp:
            for cc in range(4):
                P.op(P.POOL, lambda cc=cc: nc.gpsimd.tensor_copy(out=UTS[:, cc, 0:CONV_K - 1], in_=P.SCS[:, l, cc, :]),
                     [P.rSCS], [rUTS[cc]])
        fm_group(C_UV, ev_uv)
        fm_group(C_UG, ev_ug)
        fm_group(C_GC, ev_silu(8))

        P.stage('Bconv')
        def conv_branch(u, ru, n, mc, hist_out, cv_out, part):
            if part == 2:
                for co in range(4):
                    b = P.bank()
                    for ci in range(4):
                        P.mm(P.PS[b][:, 0:n], wpw[:, ci, co * 128:(co + 1) * 128], CNT[:, ci, 0:n], ci == 0, ci == 3,
                             [rwpw, rCNT[ci]], [P.rPS[b]])
                    P.op(P.DVE, lambda co=co, b=b: nc.vector.tensor_tensor(out=P.MIX[:, 8 + co, mc],
                                                                           in0=P.PS[b][:, 0:n],
                                                                           in1=P.MIX[:, 8 + co, mc], op=AL.mult),
                         [P.rPS[b], P.rMIX[8 + co]], [P.rMIX[8 + co]])
                return
            for cc in range(4):
                P.op(P.DVE, lambda cc=cc: nc.vector.tensor_scalar(
                    out=CC[:, cc, 0:n], in0=u[:, cc, 0:n], scalar1=P.WDW[:, l, cc, 0:1],
                    scalar2=P.CPAR[:, 0, l, cc:cc + 1], op0=AL.mult, op1=AL.add), [ru[cc], P.rPAR], [rCC[cc]])
            for k in range(1, CONV_K):
                for cc in range(4):
                    P.op(P.DVE, lambda cc=cc, k=k: nc.vector.scalar_tensor_tensor(
                        out=CC[:, cc, 0:n], in0=u[:, cc, k:k + n], scalar=P.WDW[:, l, cc, k:k + 1], in1=CC[:, cc, 0:n],
                        op0=AL.mult, op1=AL.add), [ru[cc], P.rPAR, rCC[cc]], [rCC[cc]])
            if hist_out:
                for cc in range(4):
                    P.op(P.POOL, lambda cc=cc: nc.gpsimd.tensor_copy(out=P.UH[:, l, cc, :],
                                                                     in_=u[:, cc, n:n + CONV_K - 1]),
                         [ru[cc]], [P.rUH[l]])
            if cv_out is not None:
                b = P.bank()
                for cc in range(4):
                    P.op(P.PE, lambda cc=cc, b=b: nc.tensor.transpose(
                        P.PS[b][0:CONV_K - 1, cc * 128:(cc + 1) * 128], u[:, cc, n:n + CONV_K - 1], P.IDF[:]),
                        [ru[cc], P.rIDF], [P.rPS[b]], signal=(cc == 3))
                P.op(P.DVE, lambda b=b: nc.vector.tensor_copy(out=STF[0][0:CONV_K - 1, :],
                                                              in_=P.PS[b][0:CONV_K - 1, :]), [P.rPS[b]], [rSTF[0]])
                P.dma(P.SP, cv_out, STF[0][0:CONV_K - 1, :], [rSTF[0]], [], is_output=True)
            b1, b2 = P.bank(), P.bank()
            for cc in range(4):
                P.mm(P.PS[b1][:, 0:n], P.ONF[:], CC[:, cc, 0:n], cc == 0, cc == 3, [rCC[cc], P.rCONST], [P.rPS[b1]])
            for cc in range(4):
                j = cc % 2
                P.op(P.ACT, lambda cc=cc, j=j: nc.scalar.activation(out=SG[j][:, 0:n], in_=CC[:, cc, 0:n],
                                                                    func=AF.Square), [rCC[cc]], [rSG[j]])
                P.mm(P.PS[b2][:, 0:n], P.ONF[:], SG[j][:, 0:n], cc == 0, cc == 3, [rSG[j], P.rCONST], [P.rPS[b2]])
            P.op(P.DVE, lambda: nc.vector.tensor_scalar(out=LNM[:, 0:n], in0=P.PS[b1][:, 0:n], scalar1=1.0 / CONV_CH,
                                                        scalar2=None, op0=AL.mult), [P.rPS[b1]], [rLNM])
            P.op(P.DVE, lambda: nc.vector.tensor_tensor(out=LNR[:, 0:n], in0=LNM[:, 0:n], in1=LNM[:, 0:n], op=AL.mult),
                 [rLNM], [rLNR])
            P.op(P.DVE, lambda: nc.vector.scalar_tensor_tensor(out=LNR[:, 0:n], in0=P.PS[b2][:, 0:n],
                                                               scalar=1.0 / CONV_CH, in1=LNR[:, 0:n], op0=AL.mult,
                                                               op1=AL.subtract), [P.rPS[b2], rLNR], [rLNR])
            P.op(P.DVE, lambda: nc.vector.tensor_scalar(out=LNR[:, 0:n], in0=LNR[:, 0:n], scalar1=EPS, scalar2=None,
                                                        op0=AL.add), [rLNR], [rLNR])
            P.op(P.ACT, lambda: nc.scalar.activation(out=LNR[:, 0:n], in_=LNR[:, 0:n], func=AF.Sqrt), [rLNR], [rLNR])
            P.op(P.DVE, lambda: nc.vector.reciprocal(out=LNR[:, 0:n], in_=LNR[:, 0:n]), [rLNR], [rLNR])
            for cc in range(4):
                j = cc % 2
                P.op(P.DVE, lambda cc=cc, j=j: nc.vector.tensor_tensor(out=SG[j][:, 0:n], in0=CC[:, cc, 0:n],
                                                                       in1=LNM[:, 0:n], op=AL.subtract),
                     [rCC[cc], rLNM], [rSG[j]])
                P.op(P.DVE, lambda j=j: nc.vector.tensor_tensor(out=SG[j][:, 0:n], in0=SG[j][:, 0:n], in1=LNR[:, 0:n],
                                                                op=AL.mult), [rSG[j], rLNR], [rSG[j]])
                P.op(P.ACT, lambda cc=cc, j=j: nc.scalar.activation(
                    out=CNT[:, cc, 0:n], in_=SG[j][:, 0:n], func=AF.Silu, scale=P.CPAR[:, 1, l, cc:cc + 1],
                    bias=P.CPAR[:, 2, l, cc:cc + 1]), [rSG[j], P.rPAR], [rCNT[cc]])

        P.stage('C')
        fm_group(C_QM, ev_copy(P.QMT, P.rQMT, 0))
        fm_group(C_GM, ev_silu(12))
        fm_group(C_GA, ev_silu(0))
        fm_group(C_GA + 512, ev_silu(4))
        fm_group(C_Q, ev_copy(P.QT, P.rQT, 0))
        conv_branch(UT, rUT, T, slice(0, T), True, P.cvp[l] if c == self.n_chunks - 1 else None, 1)
        fm_group(C_Q + 512, ev_copy(P.QT, P.rQT, 4))
        wpw, rwpw = P.wload(P.w_pw2[l].rearrange("(k p) n -> p k n", p=128), kdim=4)
        conv_branch(UT, rUT, T, slice(0, T), True, None, 2)
        if samp:
            conv_branch(UTS, rUTS, NS, slice(T, T + NS), False, P.cvs[l], 1)
            conv_branch(UTS, rUTS, NS, slice(T, T + NS), False, None, 2)


        P.stage('Brest')
        self.overlay()
        EX = [A("EX%d" % i, [128, 512], F32) for i in range(2)]
        rEX = [P.LR("EX0"), P.LR("EX1")]
        NPT = 12
        PT = [A("PT%d" % i, [128, 512], BF16) for i in range(NPT)]
        rPT = [P.LR("PT%d" % i) for i in range(NPT)]
        LT = A("LT", [128, T], F32)
        RD = A("RD", [128, T], F32)
        AT = A("AT", [128, T], F32)
        rLT, rRD, rAT = P.LR("LT"), P.LR("RD"), P.LR("AT")
        ET = [A("ET%d" % i, [128, 6, 128], F32) for i in range(2)]
        rET = [P.LR("ET0"), P.LR("ET1")]
        E3S = [A("E3S%d" % i, [128, 32], F32) for i in range(2)]
        rE3S = [P.LR("E3S0"), P.LR("E3S1")]
        KH = [A("KH%d" % i, [128, WIN], BF16) for i in range(2)]
        rKH = [P.LR("KH0"), P.LR("KH1")]
        VB = [A("VB%d" % i, [128, 29, 128], BF16) for i in range(2)]
        rVB = [P.LR("VB0"), P.LR("VB1")]
        V3 = [A("V3%d" % i, [32, 16, 128], BF16) for i in range(2)]
        rV3 = [P.LR("V30"), P.LR("V31")]
        pti = [0]

        def next_pt():
            i = pti[0] % NPT
            pti[0] += 1
            return i

        def finish_head(f, bacc, bden, n=T, mc=slice(0, T)):
            P.op(P.ACT, lambda: nc.scalar.activation(out=LT[:, 0:n], in_=P.PS[bden][:, 0:n], func=AF.Ln),
                 [P.rPS[bden]], [rLT])
            P.op(P.ACT, lambda: nc.scalar.activation(out=RD[:, 0:n], in_=LT[:, 0:n], func=AF.Exp, scale=-1.0),
                 [rLT], [rRD])
            P.op(P.DVE, lambda: nc.vector.tensor_tensor(out=AT[:, 0:n], in0=P.PS[bacc][:, 0:n], in1=RD[:, 0:n],
                                                        op=AL.mult), [P.rPS[bacc], rRD], [rAT])
            P.op(P.DVE, lambda: nc.vector.tensor_tensor(out=P.MIX[:, f, mc], in0=AT[:, 0:n], in1=P.MIX[:, f, mc],
                                                        op=AL.mult), [rAT, P.rMIX[f]], [P.rMIX[f]])

        P.dma(P.SP, P.MEMK[:].rearrange("p h m -> p (h m)"), P.MKd[l], [P.r_MKd[l]], [P.rMEMK])
        P.dma(P.SP, P.MEMV[:].rearrange("p b n -> p (b n)"), P.MVd[l], [P.r_MVd[l]], [P.rMEMV])
        def mem_attn(kt, rkt, vt, rvt, n, mc):
            for h in range(4):
                pts = []
                for mb in range(2):
                    b = P.bank()
                    P.mm(P.PS[b][:, 0:n], kt[:, h, mb * 128:(mb + 1) * 128], P.QMT[:, h, mc], True, True,
                         [rkt, P.rQMT[h]], [P.rPS[b]])
                    i = next_pt()
                    P.op(P.ACT, lambda b=b, i=i: nc.scalar.activation(out=PT[i][:, 0:n], in_=P.PS[b][:, 0:n],
                                                                      func=AF.Exp, scale=SCALE), [P.rPS[b]], [rPT[i]])
                    pts.append(i)
                bacc, bden = P.bank(), P.bank()
                for mb in range(2):
                    i = pts[mb]
                    P.mm(P.PS[bacc][:, 0:n], vt[:, mb, h * 128:(h + 1) * 128], PT[i][:, 0:n], mb == 0, mb == 1,
                         [rvt, rPT[i]], [P.rPS[bacc]])
                for mb in range(2):
                    i = pts[mb]
                    P.mm(P.PS[bden][:, 0:n], P.ONB[:], PT[i][:, 0:n], mb == 0, mb == 1, [P.rCONST, rPT[i]],
                         [P.rPS[bden]])
                finish_head(12 + h, bacc, bden, n, mc)

        mem_attn(P.MEMK, P.rMEMK, P.MEMV, P.rMEMV, T, slice(0, T))
        if samp:
            CMK = A("CMK", [128, 2, X_W], BF16)
            CMV = A("CMV", [128, 2, X_W], BF16)
            MKS = A("MKS", [128, 4, N_MEM], BF16)
            rCMK, rCMV, rMKS = P.LR("CMK"), P.LR("CMV"), P.LR("MKS")
            P.dma(P.POOL, CMK[:], P.cmk[l].rearrange("(b p) n -> p b n", p=128), [], [rCMK])
            P.dma(P.POOL, CMV[:], P.cmv[l].rearrange("(b p) n -> p b n", p=128), [], [rCMV])
            for mb in range(2):
                b = P.bank()
                for h in range(4):
                    P.op(P.PE, lambda mb=mb, h=h, b=b: nc.tensor.transpose(
                        P.PSB[b][:, h * 128:(h + 1) * 128], CMK[:, mb, h * 128:(h + 1) * 128], P.IDB[:]),
                        [rCMK, P.rIDB], [P.rPS[b]], signal=(h == 3))
                P.any_copy(MKS[:, :, mb * 128:(mb + 1) * 128], P.PSB[b][:, 0:512].rearrange("p (h m) -> p h m", h=4),
                           [P.rPS[b]], [rMKS])
            mem_attn(MKS, rMKS, CMV, rCMV, NS, slice(T, T + NS))

        P.stage('D')
        nh = min(c, 4)
        M3 = 32 * nh
        jlo = 0 if nh else 1
        q, ko = P.QT, P.KTown

        def sbank():
            b = 4 + (self.bank_i % 4)
            self.bank_i += 1
            return b

        def bc(e2d, n):
            a = e2d
            return bass.AP(tensor=a.tensor, offset=a.offset, ap=[list(a.ap[0]), [0, n], list(a.ap[1])])

        def bci(e2d, n):
            a = e2d
            return bass.AP(tensor=a.tensor, offset=a.offset, ap=[list(a.ap[0]), list(a.ap[1]), [0, n]])

        def softmax_tile(s, b, np_, width, e_ap, r_extra=None, inner=False):
            j = self.ee % 2
            self.ee += 1
            P.op(P.ACT, lambda: nc.scalar.activation(out=EX[j][0:np_, 0:width], in_=P.PS[b][0:np_, 0:width],
                                                     func=AF.Exp, scale=SCALE), [P.rPS[b]], [rEX[j]])
            i = next_pt()
            if inner:
                n = e_ap.shape[2]
                pat, kw = "p (w n) -> p w n", dict(n=n)
            else:
                n = e_ap.shape[1]
                pat, kw = "p (n w) -> p n w", dict(n=n)
            P.op(P.DVE, lambda: nc.vector.tensor_tensor(
                out=PT[i][0:np_, 0:width].rearrange(pat, **kw), in0=EX[j][0:np_, 0:width].rearrange(pat, **kw),
                in1=e_ap, op=AL.mult), [rEX[j], rET[s]] + ([r_extra] if r_extra is not None else []), [rPT[i]])
            return i

        def stage1(h):
            s = h % 2
            rq, rko = P.rQT[h], P.rKTown[h // 4]
            P.dma(P.SP, ET[s][:], P.Ed[h], [P.r_Ed], [rET[s]])
            if 0 < M3 < 128:
                P.dma(P.SP, E3S[s][0:M3, :], P.Ed[h, 128 - M3:128, 5, 0:32], [P.r_Ed], [rE3S[s]])
            if nh:
                P.dma(P.SP, KH[s][:, WIN - nh * T:WIN], P.KTd[l, h, :, t0 - nh * T:t0],
                      [P.r_KTd[l][cc_] for cc_ in range(c - nh, c)], [rKH[s]])
            vt = P.Vd.tensor
            vbase = l * SEQ * ATT_W + h * 128

            def vap(tok0, pstride, n_part, nblk, bstride):
                return bass.AP(tensor=vt, offset=vbase + tok0 * ATT_W,
                               ap=[[pstride * ATT_W, n_part], [bstride * ATT_W, nblk], [1, 128]])
            rv_own = [P.r_Vd[l][c]]
            rv_h = [P.r_Vd[l][cc_] for cc_ in range(c - nh, c)]
            P.dma(P.SP, VB[s][:, 0:4, :], vap(t0, 1, 128, 4, 128), rv_own, [rVB[s]])
            if nh:
                P.dma(P.SP, VB[s][:, 4:5, :], vap(t0 - 128, 1, 128, 1, 128), rv_h, [rVB[s]])
            P.dma(P.SP, VB[s][:, 5:9, :], vap(t0, 4, 128, 4, 1), rv_own, [rVB[s]])
            if nh:
                P.dma(P.SP, VB[s][:, 9:13, :], vap(t0 - 512, 4, 128, 4, 1), rv_h, [rVB[s]])
            if nh:
                P.dma(P.SP, VB[s][0:M3, 13:29, :], vap(t0 - nh * T, 16, M3, 16, 1), rv_h, [rVB[s]])
            P.dma(P.SP, V3[s][:], vap(t0, 16, 32, 16, 1), rv_own, [rV3[s]])
            pt = {}
            b = sbank()
            for j in range(4):
                P.mm(P.PS[b][:, j * 128:(j + 1) * 128], ko[:, h, j * 128:(j + 1) * 128], q[:, h, j * 128:(j + 1) * 128],
                     True, True, [rko, rq], [P.rPS[b]], signal=(j == 3))
            pt["1c"] = softmax_tile(s, b, 128, 512, bc(ET[s][:, 0, :], 4))
            b = sbank()
            for j in range(jlo, 4):
                kap = KH[s][:, WIN - 128:WIN] if j == 0 else ko[:, h, (j - 1) * 128:j * 128]
                P.mm(P.PS[b][:, j * 128:(j + 1) * 128], kap, q[:, h, j * 128:(j + 1) * 128], True, True,
                     [rko, rq, rKH[s]], [P.rPS[b]], signal=(j == 3))
            pt["1p"] = softmax_tile(s, b, 128, 512, bc(ET[s][:, 1, :], 4))
            b = sbank()
            for r in range(4):
                P.mm(P.PS[b][:, r:512:4], ko[:, h, r:T:4], q[:, h, r:T:4], True, True,
                     [rko, rq], [P.rPS[b]], signal=(r == 3))
            pt["2c"] = softmax_tile(s, b, 128, 512, bci(ET[s][:, 2, :], 4), inner=True)
            if nh:
                b = sbank()
                for r in range(4):
                    P.mm(P.PS[b][:, r:512:4], KH[s][:, WIN - 512 + r:WIN:4], q[:, h, r:T:4], True, True,
                         [rKH[s], rq], [P.rPS[b]], signal=(r == 3))
                pt["2p"] = softmax_tile(s, b, 128, 512, bci(ET[s][:, 3, :], 4), inner=True)
            b = sbank()
            for r in range(16):
                P.mm(P.PS[b][0:32, r:512:16], ko[:, h, r:T:16], q[:, h, r:T:16], True, True,
                     [rko, rq], [P.rPS[b]], signal=(r == 15))
            pt["3b"] = softmax_tile(s, b, 32, 512, bci(ET[s][0:32, 4, 0:32], 16), inner=True)
            if nh:
                b = sbank()
                for r in range(16):
                    P.mm(P.PS[b][0:M3, r:512:16], KH[s][:, WIN - nh * T + r:WIN:16], q[:, h, r:T:16],
                         True, True, [rKH[s], rq], [P.rPS[b]], signal=(r == 15))
                e3 = ET[s][0:M3, 5, 0:32] if M3 == 128 else E3S[s][0:M3, :]
                pt["3a"] = softmax_tile(s, b, M3, 512, bci(e3, 16), rE3S[s], inner=True)
            return pt

        def stage2(h, pt):
            s = h % 2
            bacc, bden = (0, 1) if s == 0 else (2, 3)
            acc, den = P.PS[bacc], P.PS[bden]

            def pv(out_ap, v_ap, i, pt_ap, rv):
                P.mm(out_ap, v_ap, pt_ap, False, False, [rv, rPT[i]], [P.rPS[bacc]], signal=False,
                     skip_group_check=True)

            def dn(out_ap, np_, i, pt_ap):
                P.mm(out_ap, P.ONB[0:np_, :], pt_ap, False, False, [P.rCONST, rPT[i]], [P.rPS[bden]], signal=False,
                     skip_group_check=True)

            i_c, i_p = pt["1c"], pt["1p"]
            for j in range(4):
                pv(acc[:, j * 128:(j + 1) * 128], VB[s][:, j, :], i_c, PT[i_c][:, j * 128:(j + 1) * 128], rVB[s])
            for j in range(jlo, 4):
                vb = VB[s][:, 4, :] if j == 0 else VB[s][:, j - 1, :]
                pv(acc[:, j * 128:(j + 1) * 128], vb, i_p, PT[i_p][:, j * 128:(j + 1) * 128], rVB[s])
            dn(den[:, 0:512], 128, i_c, PT[i_c][:, 0:512])
            dn(den[:, jlo * 128:512], 128, i_p, PT[i_p][:, jlo * 128:512])
            i_c = pt["2c"]
            for r in range(4):
                pv(acc[:, r:512:4], VB[s][:, 5 + r, :], i_c, PT[i_c][:, r:512:4], rVB[s])
            dn(den[:, 0:512], 128, i_c, PT[i_c][:, 0:512])
            if nh:
                i_p = pt["2p"]
                for r in range(4):
                    pv(acc[:, r:512:4], VB[s][:, 9 + r, :], i_p, PT[i_p][:, r:512:4], rVB[s])
                dn(den[:, 0:512], 128, i_p, PT[i_p][:, 0:512])
            i_b = pt["3b"]
            for r in range(16):
                pv(acc[:, r:512:16], V3[s][:, r, :], i_b, PT[i_b][0:32, r:512:16], rV3[s])
            dn(den[:, 0:512], 32, i_b, PT[i_b][0:32, 0:512])
            if nh:
                i_a = pt["3a"]
                for r in range(16):
                    pv(acc[:, r:512:16], VB[s][0:M3, 13 + r, :], i_a, PT[i_a][0:M3, r:512:16], rVB[s])
                dn(den[:, 0:512], M3, i_a, PT[i_a][0:M3, 0:512])
            finish_head(h, bacc, bden)

        def zero_acc(h):
            bacc, bden = (0, 1) if h % 2 == 0 else (2, 3)
            P.op(P.DVE, lambda: nc.vector.memset(P.PS[bacc][:], 0.0), [], [P.rPS[bacc]])
            P.op(P.DVE, lambda: nc.vector.memset(P.PS[bden][:], 0.0), [], [P.rPS[bden]])

        pts = stage1(0)
        for h in range(8):
            zero_acc(h)
            nxt = stage1(h + 1) if h + 1 < 8 else None
            stage2(h, pts)
            pts = nxt

        if samp:
            self.sample_attention(l, A, EX, rEX, PT, rPT, finish_head, KH, rKH, VB, rVB)
        P.stage('E')
        self.overlay()
        Y = A("Y", [128, 4, D_MODEL], F32)
        rY = [P.LR("Y%d" % b) for b in range(4)]
        GP = A("GP", [128, D_MODEL], F32)
        rGP = P.LR("GP")
        JK = A("JK", [128, D_MODEL], BF16)
        rJK = P.LR("JK")
        TM = A("TM", [128, D_MODEL], F32)
        rTM = P.LR("TM")
        SQ = A("SQ", [128, 16], F32)
        rSQ = [P.LR("SQ%d" % b) for b in range(4)]
        if samp:
            YS = A("YS", [NS, D_MODEL], F32)
            rYS = P.LR("YS")
        P.dma(P.SP, GP[:], bass.AP(tensor=P.g_post.tensor, offset=l * D_MODEL, ap=[[0, 128], [1, D_MODEL]]),
              [], [rGP])
        for g in range(4):
            w, rw = P.wload(P.w_out[l].rearrange("(k p) n -> p k n", p=128)[:, :, g * 512:(g + 1) * 512])
            for blk in range(4):
                b = P.bank()
                for f in range(16):
                    P.mm(P.PS[b][:], P.MIX[:, f, blk * 128:(blk + 1) * 128], w[:, f, :], f == 0, f == 15,
                         [rw, P.rMIX[f]], [P.rPS[b]])
                P.op(P.ACT, lambda b=b, blk=blk, g=g: nc.scalar.activation(
                    out=JK[:, 0:512], in_=P.PS[b][:], func=AF.Square, accum_out=SQ[:, blk * 4 + g:blk * 4 + g + 1]),
                    [P.rPS[b]], [rJK, rSQ[blk]])
                P.op(P.DVE, lambda b=b, blk=blk, g=g: nc.vector.tensor_tensor(
                    out=Y[:, blk, g * 512:(g + 1) * 512], in0=P.PS[b][:], in1=GP[:, g * 512:(g + 1) * 512], op=AL.mult),
                    [P.rPS[b], rGP], [rY[blk]])
            if samp:
                b = P.bank()
                for f in range(16):
                    P.mm(P.PS[b][0:NS, :], P.MIX[:, f, T:T + NS], w[:, f, :], f == 0, f == 15, [rw, P.rMIX[f]],
                         [P.rPS[b]])
                P.any_copy(YS[:, g * 512:(g + 1) * 512], P.PS[b][0:NS, :], [P.rPS[b]], [rYS])
        P.op(P.DVE, lambda: nc.vector.tensor_reduce(
            out=P.SS[:, 4:8], in_=SQ[:, 0:16].rearrange("p (b g) -> p b g", g=4), axis=mybir.AxisListType.X,
            op=AL.add), rSQ, P.rSS)
        P.op(P.DVE, lambda: nc.vector.tensor_scalar(out=P.SS[:, 4:8], in0=P.SS[:, 4:8], scalar1=1.0 / D_MODEL,
                                                    scalar2=EPS, op0=AL.mult, op1=AL.add), P.rSS, P.rSS)
        P.op(P.ACT, lambda: nc.scalar.activation(out=P.SS[:, 4:8], in_=P.SS[:, 4:8], func=AF.Sqrt), P.rSS, P.rSS)
        P.op(P.DVE, lambda: nc.vector.reciprocal(out=P.SS[:, 4:8], in_=P.SS[:, 4:8]), P.rSS, P.rSS)
        for blk in range(4):
            P.op(P.DVE, lambda blk=blk: nc.vector.scalar_tensor_tensor(
                out=P.X[:, blk, :], in0=Y[:, blk, :], scalar=P.SS[:, 4 + blk:5 + blk], in1=P.X[:, blk, :],
                op0=AL.mult, op1=AL.add), [rY[blk], P.rSS[blk], P.rX[blk]], [P.rX[blk]])
            if last:
                P.dma(P.SP, P.y_prompt[t0 + blk * 128:t0 + (blk + 1) * 128, :], P.X[:, blk, :], [P.rX[blk]], [], is_output=True)
        if samp:
            P.op(P.ACT, lambda: nc.scalar.activation(out=JK[0:NS, :], in_=YS[:], func=AF.Square,
                                                     accum_out=P.SS[0:NS, 9:10]), [rYS], [rJK, P.rSS[0]])
            self.rstd(P.SS[0:NS, 9:10], P.rSS[0], 1.0 / D_MODEL)
            P.op(P.DVE, lambda: nc.vector.scalar_tensor_tensor(
                out=TM[0:NS, :], in0=YS[:], scalar=P.SS[0:NS, 9:10], in1=GP[0:NS, :], op0=AL.mult, op1=AL.mult),
                [rYS, P.rSS[0], rGP], [rTM])
            P.op(P.POOL, lambda: nc.gpsimd.tensor_tensor(out=P.XS[:], in0=P.XS[:], in1=TM[0:NS, :], op=AL.add),
                 [rTM, P.rXS], [P.rXS])
            if last:
                P.dma(P.SP, P.y_sample, P.XS[:], [P.rXS], [], is_output=True)

    def sample_attention(self, l, A, EX, rEX, PT, rPT, finish_head, KH, rKH, VB, rVB):
        nc = self.nc
        P = self
        KC = [VB[0][:, 0:16, :], VB[1][:, 0:16, :]]
        rKC = rVB
        KTS, rKTS = KH, rKH
        VC = [A("VC%d" % i, [128, 16, 128], BF16) for i in range(2)]
        rVC = [P.LR("VC0"), P.LR("VC1")]
        KNB = A("KNB", [NS, 128], BF16)
        VNB = A("VNB", [NS, 128], BF16)
        KTN = A("KTN", [128, NS], BF16)
        rKNB, rVNB, rKTN = P.LR("KNB"), P.LR("VNB"), P.LR("KTN")
        PN = A("PN", [NS, NS], BF16)
        rPN = P.LR("PN")
        qc = slice(T, T + NS)
        for h in range(N_ATT_HEADS):
            s = h % 2
            hc = slice(h * 128, (h + 1) * 128)
            P.dma(P.POOL, KC[s], P.cache_k[l, :, hc].rearrange("(b p) d -> p b d", p=128), [], [rKC[s]])
            P.dma(P.POOL, VC[s][:], P.cache_v[l, :, hc].rearrange("(b p) d -> p b d", p=128), [], [rVC[s]])
            for g in range(4):
                b = P.bank()
                for j in range(4):
                    blk = g * 4 + j
                    P.op(P.PE, lambda blk=blk, j=j, b=b: nc.tensor.transpose(
                        P.PSB[b][:, j * 128:(j + 1) * 128], KC[s][:, blk, :], P.IDB[:]),
                        [rKC[s], P.rIDB], [P.rPS[b]], signal=(j == 3))
                P.any_copy(KTS[s][:, g * 512:(g + 1) * 512], P.PSB[b][:, 0:512], [P.rPS[b]], [rKTS[s]])
            P.op(P.DVE, lambda: nc.vector.tensor_copy(out=KNB[:], in_=P.KVS[:, h * 128:(h + 1) * 128]), [P.rKVS], [rKNB])
            P.op(P.DVE, lambda: nc.vector.tensor_copy(out=VNB[:], in_=P.KVS[:, ATT_W + h * 128:ATT_W + (h + 1) * 128]),
                 [P.rKVS], [rVNB])
            b = P.bank()
            P.op(P.PE, lambda b=b: nc.tensor.transpose(P.PSB[b][:, 0:NS], KNB[:], P.IDB[0:NS, 0:NS]),
                 [rKNB, P.rIDB], [P.rPS[b]])
            P.any_copy(KTN[:], P.PSB[b][:, 0:NS], [P.rPS[b]], [rKTN])
            b = P.bank()
            for blk in range(16):
                P.mm(P.PS[b][:, blk * NS:(blk + 1) * NS], KTS[s][:, blk * 128:(blk + 1) * 128], P.QT[:, h, qc], True, True,
                     [rKTS[s], P.rQT[h]], [P.rPS[b]], signal=(blk == 15))
            j = self.ee % 2
            self.ee += 1
            P.op(P.ACT, lambda b=b, j=j: nc.scalar.activation(out=EX[j][:, 0:128], in_=P.PS[b][:, 0:128], func=AF.Exp,
                                                              scale=SCALE), [P.rPS[b]], [rEX[j]])
            i = self.spt % len(PT)
            self.spt += 1
            P.op(P.DVE, lambda j=j, i=i: nc.vector.tensor_tensor(
                out=PT[i][:, 0:128].rearrange("p (b q) -> p b q", b=16),
                in0=EX[j][:, 0:128].rearrange("p (b q) -> p b q", b=16), in1=P.ES[:, h, 0:16, :], op=AL.mult),
                [rEX[j], P.rES], [rPT[i]])
            b2 = P.bank()
            P.mm(P.PS[b2][0:NS, 0:NS], KTN[:], P.QT[:, h, qc], True, True, [rKTN, P.rQT[h]], [P.rPS[b2]])
            j2 = self.ee % 2
            self.ee += 1
            P.op(P.ACT, lambda b2=b2, j2=j2: nc.scalar.activation(out=EX[j2][0:NS, 0:NS], in_=P.PS[b2][0:NS, 0:NS],
                                                                  func=AF.Exp, scale=SCALE), [P.rPS[b2]], [rEX[j2]])
            P.op(P.DVE, lambda j2=j2: nc.vector.tensor_tensor(out=PN[:], in0=EX[j2][0:NS, 0:NS], in1=P.ES[0:NS, h, 16, :],
                                                              op=AL.mult), [rEX[j2], P.rES], [rPN])
            bacc, bden = P.bank(), P.bank()
            for blk in range(16):
                P.mm(P.PS[bacc][:, 0:NS], VC[s][:, blk, :], PT[i][:, blk * NS:(blk + 1) * NS], blk == 0, False,
                     [rVC[s], rPT[i]], [P.rPS[bacc]], signal=False)
            P.mm(P.PS[bacc][:, 0:NS], VNB[:], PN[:], False, True, [rVNB, rPN], [P.rPS[bacc]])
            for blk in range(16):
                P.mm(P.PS[bden][:, 0:NS], P.ONB[:], PT[i][:, blk * NS:(blk + 1) * NS], blk == 0, False,
                     [P.rCONST, rPT[i]], [P.rPS[bden]], signal=False)
            P.mm(P.PS[bden][:, 0:NS], P.ONB[0:NS, :], PN[:], False, True, [P.rCONST, rPN], [P.rPS[bden]])
            finish_head(h, bacc, bden, NS, qc)

    def rstd(self, ap, res, inv_n):
        nc = self.nc
        P = self
        P.op(P.DVE, lambda: nc.vector.tensor_scalar(out=ap, in0=ap, scalar1=inv_n, scalar2=EPS, op0=AL.mult, op1=AL.add),
             [res], [res])
        P.op(P.ACT, lambda: nc.scalar.activation(out=ap, in_=ap, func=AF.Sqrt), [res], [res])
        P.op(P.DVE, lambda: nc.vector.reciprocal(out=ap, in_=ap), [res], [res])

    def build(self):
        P = self
        try:
            P.dma(P.SP, P.X[:], P.x_prompt[0:T, :].rearrange("(b p) d -> p b d", p=128), [], P.rX)
            self.setup()
            P.stage('setup')
            for c in range(self.n_chunks):
                t0 = c * T
                if c > 0:
                    P.dma(P.SP, P.X[:], P.x_prompt[t0:t0 + T, :].rearrange("(b p) d -> p b d", p=128), [], P.rX)
                for l in range(self.depth):
                    self.chunk_layer(c, l)
        except _Stop:
            pass
        for tok in self.out_tokens.values():
            self._wait(P.SP, tok)
        return self.nc


def _static_tables():
    oh = np.zeros((3, N_BUCKETS, 384), np.float32)
    for p, d in enumerate(DILS):
        for j in range(129):
            n = 128 + j
            oh[p, int(t5_bucket(j * d)), 383 - n] = 1.0
    return oh


def _sample_table():
    oh = np.zeros((N_BUCKETS, 2064), np.float32)
    for n in range(2056):
        d = 2055 - n
        mult = int(d <= 128) + int(d % 4 == 0 and d <= 512) + int(d % 16 == 0 and d <= 2048)
        if mult:
            oh[int(t5_bucket(d)), n] = float(mult)
    return oh


def _host_inputs(inp, n_cores=N_CORES, with_sample=True, depth=DEPTH, seqr=SEQ):
    f = lambda a: np.ascontiguousarray(np.asarray(a, dtype=np.float32))
    oh = _static_tables()
    ident = np.eye(128, dtype=np.float32)
    wdwT = f(np.asarray(inp["w_dw"]).reshape(DEPTH, CONV_K, 4, 128).transpose(3, 0, 2, 1).reshape(128, -1))
    cpar = f(np.stack([np.asarray(inp[k]).reshape(DEPTH, 4, 128) for k in ("b_dw", "ln_conv_g", "ln_conv_b")], 0)
             .transpose(3, 0, 1, 2).reshape(128, -1))
    shared = {
        "rel_bias": f(inp["rel_bias"]), "ohrev": oh, "ident": ident,
        "norm_pre_g": f(inp["norm_pre_g"][:depth]), "norm_post_g": f(inp["norm_post_g"][:depth]),
        "w_in": f(inp["w_in"][:depth]), "w_out": f(inp["w_out"][:depth]), "w_pw2": f(inp["w_pw2"][:depth]),
        "w_mem_kv": f(inp["w_mem_kv"][:depth]), "wdwT": wdwT, "cpar": cpar,
    }
    maps = []
    for core in range(n_cores):
        b = core // 4
        m = dict(shared)
        if core % 4 == 0:
            m["x_prompt"] = f(inp["x_prompt"][b][:seqr])
            m["memT"] = f(np.asarray(inp["mem_prompt"][b]).T)
        else:
            m["x_prompt"] = np.zeros((seqr, D_MODEL), np.float32)
            m["memT"] = np.zeros((D_MODEL, N_MEM), np.float32)
        if with_sample:
            m["x_sample"] = f(inp["x_sample"][core])
            m["cache_k"] = f(np.asarray(inp["cache_attn_k"])[:depth, core].reshape(depth, L_CACHE, ATT_W))
            m["cache_v"] = f(np.asarray(inp["cache_attn_v"])[:depth, core].reshape(depth, L_CACHE, ATT_W))
            m["cache_mem_k"] = f(np.asarray(inp["cache_mem_k"])[:depth, core].reshape(depth, N_MEM, X_W))
            m["cache_mem_v"] = f(np.asarray(inp["cache_mem_v"])[:depth, core].reshape(depth, N_MEM, X_W))
            m["state_convT"] = f(np.asarray(inp["state_conv"])[:, core].reshape(DEPTH, CONV_K - 1, 4, 128)
                                 .transpose(3, 0, 2, 1).reshape(128, -1))
            m["ohs"] = _sample_table()
        maps.append(m)
    return maps


_CACHE = {}


def kernel(**inputs):
    with_sample = True
    key = ("full", with_sample)
    if key not in _CACHE:
        _CACHE[key] = Prog(with_sample=with_sample).build()
    nc = _CACHE[key]
    maps = _host_inputs(inputs, with_sample=with_sample)
    res = run_bass_kernel_spmd(nc, maps, core_ids=list(range(N_CORES))).results
    pc = (0, 4)
    y_prompt = np.stack([res[c]["y_prompt"] for c in pc], 0)
    akp = np.stack([res[c]["akp"] for c in pc], 1).reshape(DEPTH, BATCH, WIN, N_ATT_HEADS, HEAD_DIM)
    avp = np.stack([res[c]["avp"] for c in pc], 1).reshape(DEPTH, BATCH, WIN, N_ATT_HEADS, HEAD_DIM)
    cvp = np.stack([res[c]["cvp"] for c in pc], 1)
    mkp = np.stack([res[c]["mkp"] for c in pc], 1).reshape(DEPTH, BATCH, N_MEM, N_X_HEADS, HEAD_DIM)
    mvp = np.stack([res[c]["mvp"] for c in pc], 1).reshape(DEPTH, BATCH, N_MEM, N_X_HEADS, HEAD_DIM)
    allc = range(N_CORES)
    y_sample = np.stack([res[c]["y_sample"] for c in allc], 0)
    aks = np.stack([res[c]["aks"] for c in allc], 1).reshape(DEPTH, DEC_BATCH, L_CACHE, N_ATT_HEADS, HEAD_DIM)
    avs = np.stack([res[c]["avs"] for c in allc], 1).reshape(DEPTH, DEC_BATCH, L_CACHE, N_ATT_HEADS, HEAD_DIM)
    cvs = np.stack([res[c]["cvs"] for c in allc], 1)
    return (y_prompt, y_sample, akp, avp, cvp, mkp, mvp, aks, avs, cvs)
```

```python
import numpy as np
import concourse.bass as bass
import concourse.mybir as mybir
from concourse.bass_utils import run_bass_kernel_spmd

F32 = mybir.dt.float32
BF16 = mybir.dt.bfloat16
AF = mybir.ActivationFunctionType
AL = mybir.AluOpType

D_MODEL = 2048
BATCH = 2
SEQ = 4096
DEPTH = 4
DEC_BATCH = 8
DEC_SEQ = 8
N_MEM = 256
HEAD_DIM = 128
ATT_W = 1024
N_ATT_HEADS = 8
DILS = (1, 4, 16)
WIN = 2048
N_BUCKETS = 32
MAX_DIST = WIN
CONV_CH = 512
CONV_K = 31
X_W = 512
N_X_HEADS = 4
MIX_W = 2048
IN_W = 6656
EPS = 1e-6
SCALE = HEAD_DIM ** -0.5
N_CORES = 8
T = 512
NCH = SEQ // T
L_CACHE = 2048
NS = DEC_SEQ

C_Q, C_K, C_V, C_GA, C_UV, C_UG, C_GC, C_QM, C_GM = 0, 1024, 2048, 3072, 4096, 4608, 5120, 5632, 6144


def t5_bucket(dist):
    dist = np.asarray(dist)
    max_exact = N_BUCKETS // 2
    large = max_exact + (np.log(np.maximum(dist, 1) / max_exact)
                         / np.log(MAX_DIST / max_exact) * (N_BUCKETS - max_exact)).astype(np.int32)
    large = np.minimum(large, N_BUCKETS - 1)
    return np.where(dist < max_exact, dist, large).astype(np.int32)


class Res:
    __slots__ = ("name", "w", "r", "slot", "dram", "excl")

    def __init__(self, name, dram=False, excl=False):
        self.name = name
        self.w = {}
        self.r = {}
        self.slot = None
        self.dram = dram
        self.excl = excl


class Eng:
    def __init__(self, name, h, sem):
        self.name, self.h, self.sem, self.count = name, h, sem, 0


class Slot:
    def __init__(self, sem):
        self.sem, self.count = sem, 0


class _Stop(Exception):
    pass


class Prog:
    def __init__(self, n_chunks=NCH, depth=DEPTH, with_sample=True, stop_after=None):
        self.n_chunks, self.depth, self.with_sample = n_chunks, depth, with_sample
        self.stop_after = stop_after
        self.SEQR = n_chunks * T
        nc = self.nc = bass.Bass("TRN2", target_bir_lowering=False)
        self.semid = 0
        self.PE = Eng("pe", nc.tensor, self.new_sem())
        self.ACT = Eng("act", nc.scalar, self.new_sem())
        self.DVE = Eng("dve", nc.vector, self.new_sem())
        self.POOL = Eng("pool", nc.gpsimd, self.new_sem())
        self.SP = Eng("sp", nc.sync, self.new_sem())
        self.waited = {}
        self.slots_by_name = {}
        self.pe_last = None
        self.spt = 0
        self.out_tokens = {}
        self.sem_names = {}
        self.declare_io()
        self.alloc()
        self.bank_i = 0
        self.ee = 0

    def new_sem(self):
        self.semid += 1
        return self.nc.alloc_semaphore("s%d" % self.semid)

    def new_slot(self):
        return Slot(self.new_sem())

    def _wait(self, eng, tok):
        sem, val = tok
        if sem is self.PE.sem and val > self.PE.count:
            assert self.pe_last is not None and val == self.PE.count + 1
            self.pe_last.then_inc(self.PE.sem, 1)
            self.PE.count += 1
            self.pe_last = None
        key = (eng.name, id(sem))
        if self.waited.get(key, 0) >= val:
            return
        eng.h.wait_ge(sem, val)
        self.waited[key] = val

    def _deps(self, eng, reads, writes, skip_waw=False):
        toks = []
        for r in reads:
            toks.extend(r.w.values())
            if r.excl:
                toks.extend(r.r.values())
        for w in writes:
            if not skip_waw:
                toks.extend(w.w.values())
            toks.extend(w.r.values())
        for t in toks:
            if eng is self.PE and t[0] is self.PE.sem:
                continue
            self._wait(eng, t)

    def _update(self, tok, reads, writes, accumulate=False):
        k = id(tok[0])
        for w in writes:
            if accumulate:
                if k not in w.w or w.w[k][1] < tok[1]:
                    w.w[k] = tok
            else:
                w.w = {k: tok}
            w.r = {}
        for r in reads:
            if r in writes:
                continue
            if k not in r.r or r.r[k][1] < tok[1]:
                r.r[k] = tok

    def op(self, eng, fn, reads=(), writes=(), signal=True):
        self._deps(eng, reads, writes)
        ins = fn()
        if signal:
            eng.count += 1
            ins.then_inc(eng.sem, 1)
            tok = (eng.sem, eng.count)
            if eng is self.PE:
                self.pe_last = None
        else:
            assert eng is self.PE
            tok = (eng.sem, eng.count + 1)
            self.pe_last = ins
        self._update(tok, reads, writes)
        return tok

    def dma(self, q, out, in_, reads, writes, is_output=False):
        owner = None
        for w in writes:
            if not w.dram:
                owner = w
                break
        if owner is None:
            owner = reads[0]
        key = (owner.name, q.name)
        if key not in self.slots_by_name:
            self.slots_by_name[key] = self.new_slot()
        slot = self.slots_by_name[key]
        self._deps(q, reads, writes, skip_waw=True)
        ins = q.h.dma_start(out=out, in_=in_)
        slot.count += 16
        ins.then_inc(slot.sem, 16)
        tok = (slot.sem, slot.count)
        self._update(tok, reads, writes, accumulate=True)
        if is_output:
            self.out_tokens[id(slot.sem)] = tok
        return tok

    def any_copy(self, out, in_, reads, writes):
        self.ee += 1
        if self.ee % 2:
            return self.op(self.ACT, lambda: self.nc.scalar.activation(out=out, in_=in_, func=AF.Copy), reads, writes)
        return self.op(self.DVE, lambda: self.nc.vector.tensor_copy(out=out, in_=in_), reads, writes)

    def bank(self):
        b = self.bank_i % 8
        self.bank_i += 1
        return b

    def mm(self, out, lhsT, rhs, start, stop, reads, writes, signal=None, **kw):
        if signal is None:
            signal = stop
        return self.op(self.PE, lambda: self.nc.tensor.matmul(out, lhsT, rhs, start=start, stop=stop, **kw),
                       reads, writes, signal=signal)

    def declare_io(self):
        nc = self.nc

        def din(name, shape, dt=F32):
            return nc.dram_tensor(name, list(shape), dt, kind="ExternalInput").ap()

        def dout(name, shape, dt=F32):
            return nc.dram_tensor(name, list(shape), dt, kind="ExternalOutput").ap()

        self.x_prompt = din("x_prompt", [self.SEQR, D_MODEL])
        self.memT = din("memT", [D_MODEL, N_MEM])
        self.rel_bias = din("rel_bias", [N_BUCKETS, N_ATT_HEADS])
        self.ohrev = din("ohrev", [4, N_BUCKETS, 384])
        self.ident_in = din("ident", [128, 128])
        self.g_pre = din("norm_pre_g", [self.depth, D_MODEL])
        self.g_post = din("norm_post_g", [self.depth, D_MODEL])
        self.w_in = din("w_in", [self.depth, D_MODEL, IN_W])
        self.w_out = din("w_out", [self.depth, MIX_W, D_MODEL])
        self.w_pw2 = din("w_pw2", [self.depth, CONV_CH, CONV_CH])
        self.w_mem_kv = din("w_mem_kv", [self.depth, D_MODEL, 2 * X_W])
        self.wdwT = din("wdwT", [128, DEPTH * 4 * CONV_K])
        self.cpar = din("cpar", [128, 3 * DEPTH * 4])
        self.y_prompt = dout("y_prompt", [self.SEQR, D_MODEL])
        self.akp = dout("akp", [self.depth, WIN, ATT_W])
        self.avp = dout("avp", [self.depth, WIN, ATT_W])
        self.cvp = dout("cvp", [self.depth, CONV_K - 1, CONV_CH])
        self.mkp = dout("mkp", [self.depth, N_MEM, X_W])
        self.mvp = dout("mvp", [self.depth, N_MEM, X_W])
        if self.with_sample:
            self.x_sample = din("x_sample", [NS, D_MODEL])
            self.cache_k = din("cache_k", [self.depth, L_CACHE, ATT_W])
            self.cache_v = din("cache_v", [self.depth, L_CACHE, ATT_W])
            self.state_conv = din("state_convT", [128, DEPTH * 4 * (CONV_K - 1)])
            self.cmk = din("cache_mem_k", [self.depth, N_MEM, X_W])
            self.cmv = din("cache_mem_v", [self.depth, N_MEM, X_W])
            self.ohs = din("ohs", [N_BUCKETS, 2064])
            self.y_sample = dout("y_sample", [NS, D_MODEL])
            self.aks = dout("aks", [self.depth, L_CACHE, ATT_W])
            self.avs = dout("avs", [self.depth, L_CACHE, ATT_W])
            self.cvs = dout("cvs", [self.depth, CONV_K - 1, CONV_CH])
        self.KTd = nc.dram_tensor("KTd", [DEPTH, N_ATT_HEADS, 128, SEQ], BF16).ap()
        self.Vd = nc.dram_tensor("Vd", [DEPTH, SEQ, ATT_W], BF16).ap()
        self.Rd = nc.dram_tensor("Rd", [4, N_ATT_HEADS, 384], F32).ap()
        self.Ed = nc.dram_tensor("Ed", [N_ATT_HEADS, 128, 6, 128], F32).ap()
        self.Fd = nc.dram_tensor("Fd", [N_ATT_HEADS, 2064], F32).ap()
        self.r_Fd = Res("Fd", dram=True)
        self.MKd = nc.dram_tensor("MKd", [DEPTH, 128, N_X_HEADS * N_MEM], BF16).ap()
        self.MVd = nc.dram_tensor("MVd", [DEPTH, 128, 2 * X_W], BF16).ap()
        self.r_KTd = [[Res("KTd%d_%d" % (l, c), dram=True) for c in range(NCH)] for l in range(DEPTH)]
        self.r_Vd = [[Res("Vd%d_%d" % (l, c), dram=True) for c in range(NCH)] for l in range(DEPTH)]
        self.r_Rd, self.r_Ed = Res("Rd", dram=True), Res("Ed", dram=True)
        self.r_MKd = [Res("MKd%d" % l, dram=True) for l in range(DEPTH)]
        self.r_MVd = [Res("MVd%d" % l, dram=True) for l in range(DEPTH)]

    def alloc(self):
        nc = self.nc
        base = (nc.sbuf_base + 63) // 64 * 64
        top = nc.sbuf_top
        self._arena = nc.alloc_sbuf_tensor("arena", [128, top - base - 64], mybir.dt.uint8)
        self._off = base
        self._top = top - 64
        self.res = {}

        def A(name, shape, dt, nres=None):
            nbytes = int(np.prod(shape[1:])) * (4 if dt == F32 else 2)
            nbytes = (nbytes + 63) // 64 * 64
            t = nc.alloc_sbuf_tensor_at(name, list(shape), dt, offset=self._off)
            if getattr(self, "_pend_lo", None) is None:
                self._pend_lo = self._off
            self._off += nbytes
            assert self._off <= self._top, "SBUF overflow at %s: %d > %d" % (name, self._off, self._top)
            return t

        self.A = A
        NT = T + NS
        self.X = A("X", [128, 4, D_MODEL], F32)
        self.rX = [Res("X%d" % b) for b in range(4)]
        self.MIX = A("MIX", [128, 16, NT], BF16)
        self.rMIX = [Res("MIX%d" % f) for f in range(16)]
        self.QT = A("QT", [128, 8, NT], BF16)
        self.rQT = [Res("QT%d" % h) for h in range(8)]
        self.QMT = A("QMT", [128, 4, NT], BF16)
        self.rQMT = [Res("QMT%d" % h) for h in range(4)]
        self.KTown = A("KTown", [128, 8, T], BF16)
        self.rKTown = [Res("KTown%d" % g) for g in range(2)]
        self.NW = 2
        self.W = [A("W%d" % i, [128, 16, 512], BF16) for i in range(self.NW)]
        self.rW = [Res("W%d" % i) for i in range(self.NW)]
        self.rMEMK, self.rMEMV = Res("MEMK"), Res("MEMV")
        self.wi = 0
        self.MEMK = A("MEMK", [128, 4, N_MEM], BF16)
        self.MEMV = A("MEMV", [128, 2, X_W], BF16)
        self.UH = A("UH", [128, DEPTH, 4, CONV_K - 1], F32)
        self.rUH = [Res("UH%d" % l) for l in range(DEPTH)]
        self.WDW = A("WDW", [128, DEPTH, 4, CONV_K], F32)
        self.CPAR = A("CPAR", [128, 3, DEPTH, 4], F32)
        self.rPAR = Res("PAR")
        self.IDB = A("IDB", [128, 128], BF16)
        self.IDF = A("IDF", [128, 128], F32)
        self.ONB = A("ONB", [128, 128], BF16)
        self.ONF = A("ONF", [128, 128], F32)
        self.rCONST = Res("CONST")
        self.rIDF, self.rIDB = Res("IDF"), Res("IDB")
        self.SS = A("SS", [128, 16], F32)
        self.rSS = [Res("SS%d" % b) for b in range(4)]
        if self.with_sample:
            self.XS = A("XS", [NS, D_MODEL], F32)
            self.KVS = A("KVS", [NS, 2 * ATT_W], F32)
            self.ES = A("ES", [128, N_ATT_HEADS, 17, NS], F32)
            self.SCS = A("SCS", [128, DEPTH, 4, CONV_K - 1], F32)
            self.rXS, self.rKVS, self.rES, self.rSCS = Res("XS"), Res("KVS"), Res("ES"), Res("SCS")
            self.rCPY = [Res("CPY%d" % i, dram=True) for i in range(2 * DEPTH)]
        self._persist_end = self._off
        self.PS = [nc.alloc_psum_tensor("ps%d" % i, [128, 512], F32) for i in range(8)]
        self.PSB = [p.bitcast(BF16) for p in self.PS]
        self.rPS = [Res("PS%d" % i, excl=True) for i in range(8)]

    def stage(self, name):
        if self.stop_after == name:
            raise _Stop()

    def overlay(self):
        old_dead = getattr(self, "dead", [])
        new_dead = []
        for r, (lo, hi) in getattr(self, "region_res", []):
            toks = {}
            for d in (r.w, r.r):
                for k, tok in d.items():
                    if k not in toks or toks[k][1] < tok[1]:
                        toks[k] = tok
            new_dead.append((lo, hi, toks))
        dead = [e for e in old_dead if not any(lo <= e[0] and e[1] <= hi for (lo, hi, _) in new_dead)] + new_dead
        self.dead = dead
        self.region_res = []
        self._off = self._persist_end
        self._pend_lo = None
        self._last_range = None

    def LR(self, name):
        r = Res(name)
        if self._pend_lo is not None:
            self._last_range = (self._pend_lo, self._off)
            self._pend_lo = None
        lo, hi = self._last_range
        for (a, b, toks) in self.dead:
            if a < hi and lo < b:
                for k, tok in toks.items():
                    if k not in r.r or r.r[k][1] < tok[1]:
                        r.r[k] = tok
        self.region_res.append((r, (lo, hi)))
        return r

    def wload(self, src_ap, kdim=16, ncols=512):
        i = self.wi % self.NW
        self.wi += 1
        self.dma(self.POOL, self.W[i][:, 0:kdim, 0:ncols], src_ap, [], [self.rW[i]])
        return self.W[i], self.rW[i]

    def setup(self):
        nc = self.nc
        P = self
        self.overlay()
        A = self.A
        P.dma(P.SP, P.IDF[:], P.ident_in, [], [P.rIDF])
        P.dma(P.POOL, P.IDB[:], P.ident_in, [], [P.rIDB])
        P.op(P.POOL, lambda: nc.gpsimd.memset(P.ONB[:], 1.0), [], [P.rCONST])
        P.op(P.POOL, lambda: nc.gpsimd.memset(P.ONF[:], 1.0), [], [P.rCONST])
        P.dma(P.SP, P.WDW[:].rearrange("p l c k -> p (l c k)"), P.wdwT, [], [P.rPAR])
        P.dma(P.SP, P.CPAR[:].rearrange("p w l c -> p (w l c)"), P.cpar, [], [P.rPAR])
        for l in range(DEPTH):
            P.op(P.POOL, lambda l=l: nc.gpsimd.memset(P.UH[:, l], 0.0), [], [P.rUH[l]])
        P.stage('s_const')
        RB = A("RB", [32, 8], F32)
        EB = A("EB", [32, 8], F32)
        OH = A("OH", [32, 4, 384], F32)
        RS = A("RS", [8, 4, 384], F32)
        HK = A("HK", [128, 8, 128], F32)
        EV = A("EV", [128, 8, 128], F32)
        rRB, rOH, rRS, rHK, rEV = P.LR("RB"), P.LR("OH"), P.LR("RS"), P.LR("HK"), P.LR("EV")
        P.dma(P.SP, RB[:], P.rel_bias, [], [rRB])
        P.dma(P.SP, OH[:], P.ohrev.rearrange("p b n -> b p n"), [], [rOH])
        P.op(P.ACT, lambda: nc.scalar.activation(out=EB[:], in_=RB[:], func=AF.Exp), [rRB], [rRB])
        for p in range(4):
            b = P.bank()
            P.mm(P.PS[b][0:8, 0:384], EB[:], OH[:, p, :], True, True, [rRB, rOH], [P.rPS[b]])
            P.op(P.DVE, lambda p=p, b=b: nc.vector.tensor_copy(out=RS[:, p, :], in_=P.PS[b][0:8, 0:384]),
                 [P.rPS[b]], [rRS])
        P.stage('s_mm')
        P.dma(P.SP, P.Rd.rearrange("p h n -> h p n"), RS[:], [rRS], [P.r_Rd])
        P.stage('s_rd')
        for p in range(3):
            for role in range(2):
                a = 128 if role == 0 else 0
                row = 3 if (p == 1 and role == 0) else p
                src = bass.AP(tensor=P.Rd.tensor, offset=row * 8 * 384 + a, ap=[[1, 128], [384, 8], [1, 128]])
                P.dma(P.SP, HK[:], src, [P.r_Rd], [rHK])
                P.op(P.POOL, lambda: nc.gpsimd.tensor_copy(out=EV[:], in_=HK[:, :, ::-1]), [rHK], [rEV])
                P.dma(P.SP, P.Ed[:, :, 2 * p + role, :].rearrange("h k i -> k h i"), EV[:], [rEV], [P.r_Ed])
        if self.with_sample:
            self.setup_sample(EB, rRB)
        P.stage('s_etab')
        MT = A("MT", [128, 16, N_MEM], BF16)
        rMT = P.LR("MT")
        MKs = A("MKs", [128, 4, N_MEM], BF16)
        MVs = A("MVs", [128, 2, X_W], BF16)
        MF = [A("MF%d" % i, [128, 512], F32) for i in range(2)]
        rMKs, rMVs, rMF = P.LR("MKs"), P.LR("MVs"), [P.LR("MF0"), P.LR("MF1")]
        P.dma(P.POOL, MT[:], P.memT.rearrange("(k p) m -> p k m", p=128), [], [rMT])
        mfi = 0
        for l in range(self.depth):
            wk, rwk = P.wload(P.w_mem_kv[l].rearrange("(k p) n -> p k n", p=128)[:, :, 0:512])
            for h in range(4):
                b = P.bank()
                for k in range(16):
                    P.mm(P.PS[b][:, 0:N_MEM], wk[:, k, h * 128:(h + 1) * 128], MT[:, k, :], k == 0, k == 15,
                         [rwk, rMT], [P.rPS[b]])
                P.any_copy(MKs[:, h, :], P.PS[b][:, 0:N_MEM], [P.rPS[b]], [rMKs])
            for mb in range(2):
                b = P.bank()
                for k in range(16):
                    P.mm(P.PS[b][:], MT[:, k, mb * 128:(mb + 1) * 128], wk[:, k, :], k == 0, k == 15,
                         [rwk, rMT], [P.rPS[b]])
                j = mfi % 2
                mfi += 1
                P.any_copy(MF[j][:], P.PS[b][:], [P.rPS[b]], [rMF[j]])
                P.dma(P.SP, P.mkp[l, mb * 128:(mb + 1) * 128, :], MF[j][:], [rMF[j]], [], is_output=True)
            wv, rwv = P.wload(P.w_mem_kv[l].rearrange("(k p) n -> p k n", p=128)[:, :, 512:1024])
            for mb in range(2):
                b = P.bank()
                for k in range(16):
                    P.mm(P.PS[b][:], MT[:, k, mb * 128:(mb + 1) * 128], wv[:, k, :], k == 0, k == 15,
                         [rwv, rMT], [P.rPS[b]])
                j = mfi % 2
                mfi += 1
                P.op(P.ACT, lambda b=b, j=j: nc.scalar.activation(out=MF[j][:], in_=P.PS[b][:], func=AF.Copy),
                     [P.rPS[b]], [rMF[j]])
                P.op(P.DVE, lambda b=b, mb=mb: nc.vector.tensor_copy(out=MVs[:, mb, :], in_=P.PS[b][:]),
                     [P.rPS[b]], [rMVs])
                P.dma(P.SP, P.mvp[l, mb * 128:(mb + 1) * 128, :], MF[j][:], [rMF[j]], [], is_output=True)
            P.dma(P.SP, P.MKd[l], MKs[:].rearrange("p h m -> p (h m)"), [rMKs], [P.r_MKd[l]])
            P.dma(P.SP, P.MVd[l], MVs[:].rearrange("p b n -> p (b n)"), [rMVs], [P.r_MVd[l]])


    def setup_sample(self, EB, rEB):
        nc = self.nc
        P = self
        A = self.A
        P.dma(P.SP, P.XS[:], P.x_sample, [], [P.rXS])
        P.dma(P.SP, P.SCS[:].rearrange("p l c k -> p (l c k)"), P.state_conv, [], [P.rSCS])
        for l in range(self.depth):
            P.dma(P.SP, P.aks[l, 0:L_CACHE - NS, :], P.cache_k[l, NS:L_CACHE, :], [P.rCPY[2 * l]], [], is_output=True)
            P.dma(P.SP, P.avs[l, 0:L_CACHE - NS, :], P.cache_v[l, NS:L_CACHE, :], [P.rCPY[2 * l + 1]], [], is_output=True)
        OHS = A("OHS", [32, 2064], F32)
        RSS = A("RSS", [8, 2064], F32)
        HKS = A("HKS", [128, 17, NS], F32)
        rOHS, rRSS, rHKS = P.LR("OHS"), P.LR("RSS"), P.LR("HKS")
        P.dma(P.SP, OHS[:], P.ohs, [], [rOHS])
        for j in range(5):
            n0, n1 = j * 512, min(2064, (j + 1) * 512)
            b = P.bank()
            P.mm(P.PS[b][0:8, 0:n1 - n0], EB[:], OHS[:, n0:n1], True, True, [rEB, rOHS], [P.rPS[b]])
            P.op(P.DVE, lambda b=b, n0=n0, n1=n1: nc.vector.tensor_copy(out=RSS[:, n0:n1], in_=P.PS[b][0:8, 0:n1 - n0]),
                 [P.rPS[b]], [rRSS])
        P.dma(P.SP, P.Fd, RSS[:], [rRSS], [P.r_Fd])
        for h in range(N_ATT_HEADS):
            src = bass.AP(tensor=P.Fd.tensor, offset=h * 2064, ap=[[1, 128], [128, 16], [1, NS]])
            P.dma(P.SP, HKS[:, 0:16, :], src, [P.r_Fd], [rHKS])
            src2 = bass.AP(tensor=P.Fd.tensor, offset=h * 2064 + 2048, ap=[[1, NS], [1, NS]])
            P.dma(P.SP, HKS[0:NS, 16, :], src2, [P.r_Fd], [rHKS])
            P.op(P.POOL, lambda h=h: nc.gpsimd.tensor_copy(out=P.ES[:, h, 0:16, :], in_=HKS[:, 0:16, ::-1]),
                 [rHKS], [P.rES])
            P.op(P.POOL, lambda h=h: nc.gpsimd.tensor_copy(out=P.ES[0:NS, h, 16, :], in_=HKS[0:NS, 16, ::-1]),
                 [rHKS], [P.rES])

    def chunk_layer(self, c, l):
        nc = self.nc
        P = self
        t0 = c * T
        last = (l == self.depth - 1)
        self.overlay()
        A = self.A
        HT = A("HT", [128, 16, T + NS], BF16)
        rHT = [P.LR("HT%d" % b) for b in range(5)]
        GT = A("GT", [128, D_MODEL], F32)
        rGT = P.LR("GT")
        HB = [A("HB%d" % i, [128, D_MODEL], BF16) for i in range(2)]
        rHB = [P.LR("HB0"), P.LR("HB1")]
        JUNK = A("JUNK", [128, D_MODEL], BF16)
        rJUNK = P.LR("JUNK")
        STF = [A("STF%d" % i, [128, 512], F32) for i in range(2)]
        rSTF = [P.LR("STF0"), P.LR("STF1")]
        STB = [A("STB%d" % i, [128, 512], BF16) for i in range(2)]
        rSTB = [P.LR("STB0"), P.LR("STB1")]
        UT = A("UT", [128, 4, T + CONV_K - 1], F32)
        rUT = [P.LR("UT%d" % i) for i in range(4)]
        SG = [A("SG%d" % i, [128, T], F32) for i in range(2)]
        rSG = [P.LR("SG0"), P.LR("SG1")]
        CC = A("CC", [128, 4, T], F32)
        rCC = [P.LR("CC%d" % i) for i in range(4)]
        LNM = A("LNM", [128, T], F32)
        LNR = A("LNR", [128, T], F32)
        rLNM, rLNR = P.LR("LNM"), P.LR("LNR")
        CNT = A("CNT", [128, 4, T], BF16)
        rCNT = [P.LR("CNT%d" % i) for i in range(4)]
        samp = self.with_sample and c == 0
        if samp:
            UTS = A("UTS", [128, 4, NS + CONV_K - 1], F32)
            rUTS = [P.LR("UTS%d" % i) for i in range(4)]
            HBS = A("HBS", [NS, D_MODEL], BF16)
            rHBS = P.LR("HBS")

        P.dma(P.SP, GT[:], bass.AP(tensor=P.g_pre.tensor, offset=l * D_MODEL, ap=[[0, 128], [1, D_MODEL]]),
              [], [rGT])
        for blk in range(4):
            P.op(P.ACT, lambda blk=blk: nc.scalar.activation(out=JUNK[:], in_=P.X[:, blk, :], func=AF.Square,
                                                             accum_out=P.SS[:, blk:blk + 1]),
                 [P.rX[blk]], [rJUNK, P.rSS[blk]])
            self.rstd(P.SS[:, blk:blk + 1], P.rSS[blk], 1.0 / D_MODEL)
            hb = blk % 2
            P.op(P.DVE, lambda blk=blk, hb=hb: nc.vector.scalar_tensor_tensor(
                out=HB[hb][:], in0=P.X[:, blk, :], scalar=P.SS[:, blk:blk + 1], in1=GT[:], op0=AL.mult, op1=AL.mult),
                [P.rX[blk], P.rSS[blk], rGT], [rHB[hb]])
            for dg in range(4):
                b = P.bank()
                for j in range(4):
                    dm = dg * 4 + j
                    P.op(P.PE, lambda dm=dm, j=j, b=b, hb=hb: nc.tensor.transpose(
                        P.PSB[b][:, j * 128:(j + 1) * 128], HB[hb][:, dm * 128:(dm + 1) * 128], P.IDB[:]),
                        [rHB[hb], P.rIDB], [P.rPS[b]], signal=(j == 3))
                P.any_copy(HT[:, dg * 4:dg * 4 + 4, blk * 128:(blk + 1) * 128],
                           P.PSB[b][:, 0:512].rearrange("p (j t) -> p j t", j=4), [P.rPS[b]], [rHT[blk]])

        if samp:
            P.op(P.ACT, lambda: nc.scalar.activation(out=JUNK[0:NS, :], in_=P.XS[:], func=AF.Square,
                                                     accum_out=P.SS[0:NS, 8:9]), [P.rXS], [rJUNK, P.rSS[0]])
            self.rstd(P.SS[0:NS, 8:9], P.rSS[0], 1.0 / D_MODEL)
            P.op(P.DVE, lambda: nc.vector.scalar_tensor_tensor(
                out=HBS[:], in0=P.XS[:], scalar=P.SS[0:NS, 8:9], in1=GT[0:NS, :], op0=AL.mult, op1=AL.mult),
                [P.rXS, P.rSS[0], rGT], [rHBS])
            b = P.bank()
            for dm in range(16):
                P.op(P.PE, lambda dm=dm, b=b: nc.tensor.transpose(
                    P.PSB[b][:, dm * NS:(dm + 1) * NS], HBS[:, dm * 128:(dm + 1) * 128], P.IDB[0:NS, 0:NS]),
                    [rHBS, P.rIDB], [P.rPS[b]], signal=(dm == 15))
            P.any_copy(HT[:, :, T:T + NS], P.PSB[b][:, 0:16 * NS].rearrange("p (k t) -> p k t", k=16),
                       [P.rPS[b]], [rHT[4]])
        P.stage('A')
        def wsrc(col0):
            return P.w_in[l].rearrange("(k p) n -> p k n", p=128)[:, :, col0:col0 + 512]

        stg = 0
        for gi, col0 in enumerate((C_K, C_K + 512, C_V, C_V + 512)):
            w, rw = P.wload(wsrc(col0))
            isK = gi < 2
            for blk in range(4):
                b = P.bank()
                for k in range(16):
                    P.mm(P.PS[b][:], HT[:, k, blk * 128:(blk + 1) * 128], w[:, k, :], k == 0, k == 15,
                         [rw, rHT[blk]], [P.rPS[b]])
                j = stg % 2
                stg += 1
                P.op(P.ACT, lambda b=b, j=j: nc.scalar.activation(out=STB[j][:], in_=P.PS[b][:], func=AF.Copy),
                     [P.rPS[b]], [rSTB[j]])
                seq_run = self.n_chunks * T
                lp = min(WIN, seq_run)
                if t0 >= seq_run - lp:
                    row0 = t0 - (seq_run - lp) + blk * 128
                    dst = (P.akp if isK else P.avp)[l, row0:row0 + 128, (gi % 2) * 512:(gi % 2) * 512 + 512]
                    P.op(P.DVE, lambda b=b, j=j: nc.vector.tensor_copy(out=STF[j][:], in_=P.PS[b][:]),
                         [P.rPS[b]], [rSTF[j]])
                    P.dma(P.SP, dst, STF[j][:], [rSTF[j]], [], is_output=True)
                if isK:
                    b2 = P.bank()
                    for jj in range(4):
                        P.op(P.PE, lambda jj=jj, b2=b2, j=j: nc.tensor.transpose(
                            P.PSB[b2][:, jj * 128:(jj + 1) * 128], STB[j][:, jj * 128:(jj + 1) * 128], P.IDB[:]),
                            [rSTB[j], P.rIDB], [P.rPS[b2]], signal=(jj == 3))
                    P.any_copy(P.KTown[:, gi * 4:gi * 4 + 4, blk * 128:(blk + 1) * 128],
                               P.PSB[b2][:, 0:512].rearrange("p (j t) -> p j t", j=4), [P.rPS[b2]], [P.rKTown[gi]])
                else:
                    vc = (gi - 2) * 512
                    P.dma(P.SP, P.Vd[l, t0 + blk * 128:t0 + (blk + 1) * 128, vc:vc + 512], STB[j][:],
                          [rSTB[j]], [P.r_Vd[l][c]])
            if samp:
                b = P.bank()
                for k in range(16):
                    P.mm(P.PS[b][0:NS, :], HT[:, k, T:T + NS], w[:, k, :], k == 0, k == 15, [rw, rHT[4]], [P.rPS[b]])
                P.any_copy(P.KVS[:, gi * 512:(gi + 1) * 512], P.PS[b][0:NS, :], [P.rPS[b]], [P.rKVS])
        P.dma(P.SP, P.KTd[l, :, :, t0:t0 + T].rearrange("h d t -> d h t"), P.KTown[:],
              [P.rKTown[0], P.rKTown[1]], [P.r_KTd[l][c]])

        if samp:
            P.dma(P.SP, P.aks[l, L_CACHE - NS:L_CACHE, :], P.KVS[:, 0:ATT_W], [P.rKVS], [], is_output=True)
            P.dma(P.SP, P.avs[l, L_CACHE - NS:L_CACHE, :], P.KVS[:, ATT_W:2 * ATT_W], [P.rKVS], [], is_output=True)
        P.stage('Bkv')

        def fm_group(col0, evac):
            w, rw = P.wload(wsrc(col0))
            for cc in range(4):
                b = P.bank()
                for k in range(16):
                    P.mm(P.PS[b][:], w[:, k, cc * 128:(cc + 1) * 128], HT[:, k, 0:T], k == 0, k == 15,
                         [rw] + rHT[0:4], [P.rPS[b]])
                evac(cc, P.PS[b][:], P.rPS[b], False)
                if samp:
                    b = P.bank()
                    for k in range(16):
                        P.mm(P.PS[b][:, 0:NS], w[:, k, cc * 128:(cc + 1) * 128], HT[:, k, T:T + NS], k == 0, k == 15,
                             [rw, rHT[4]], [P.rPS[b]])
                    evac(cc, P.PS[b][:, 0:NS], P.rPS[b], True)

        def ucols(sm):
            return (UTS, rUTS, NS) if sm else (UT, rUT, T)

        def mcols(sm):
            return slice(T, T + NS) if sm else slice(0, T)

        def ev_uv(cc, ps, rps, sm):
            u, ru, n = ucols(sm)
            P.op(P.DVE, lambda: nc.vector.tensor_copy(out=u[:, cc, CONV_K - 1:], in_=ps), [rps], [ru[cc]])

        def ev_ug(cc, ps, rps, sm):
            u, ru, n = ucols(sm)
            j = cc % 2
            P.op(P.ACT, lambda: nc.scalar.activation(out=SG[j][:, 0:n], in_=ps, func=AF.Sigmoid), [rps], [rSG[j]])
            P.op(P.DVE, lambda: nc.vector.tensor_tensor(out=u[:, cc, CONV_K - 1:], in0=u[:, cc, CONV_K - 1:],
                                                        in1=SG[j][:, 0:n], op=AL.mult), [rSG[j], ru[cc]], [ru[cc]])

        def ev_silu(fbase):
            def f(cc, ps, rps, sm):
                P.op(P.ACT, lambda: nc.scalar.activation(out=P.MIX[:, fbase + cc, mcols(sm)], in_=ps, func=AF.Silu),
                     [rps], [P.rMIX[fbase + cc]])
            return f

        def ev_copy(dst, rdst, hbase):
            def f(cc, ps, rps, sm):
                P.op(P.ACT, lambda: nc.scalar.activation(out=dst[:, hbase + cc, mcols(sm)], in_=ps, func=AF.Copy),
                     [rps], [rdst[hbase + cc]])
            return f

        for cc in range(4):
            P.op(P.POOL, lambda cc=cc: nc.gpsimd.tensor_copy(out=UT[:, cc, 0:CONV_K - 1], in_=P.UH[:, l, cc, :]),
                 [P.rUH[l]], [rUT[cc]])
        if samp:
            for cc in range(4):
                P.op(P.POOL, lambda cc=cc: nc.gpsimd.tensor_copy(out=UTS[:, cc, 0:CONV_K - 1], in_=P.SCS[:, l, cc, :]),
                     [P.rSCS], [rUTS[cc]])
        fm_group(C_UV, ev_uv)
        fm_group(C_UG, ev_ug)
        fm_group(C_GC, ev_silu(8))

        P.stage('Bconv')
        def conv_branch(u, ru, n, mc, hist_out, cv_out, part):
            if part == 2:
                for co in range(4):
                    b = P.bank()
                    for ci in range(4):
                        P.mm(P.PS[b][:, 0:n], wpw[:, ci, co * 128:(co + 1) * 128], CNT[:, ci, 0:n], ci == 0, ci == 3,
                             [rwpw, rCNT[ci]], [P.rPS[b]])
                    P.op(P.DVE, lambda co=co, b=b: nc.vector.tensor_tensor(out=P.MIX[:, 8 + co, mc],
                                                                           in0=P.PS[b][:, 0:n],
                                                                           in1=P.MIX[:, 8 + co, mc], op=AL.mult),
                         [P.rPS[b], P.rMIX[8 + co]], [P.rMIX[8 + co]])
                return
            for cc in range(4):
                P.op(P.DVE, lambda cc=cc: nc.vector.tensor_scalar(
                    out=CC[:, cc, 0:n], in0=u[:, cc, 0:n], scalar1=P.WDW[:, l, cc, 0:1],
                    scalar2=P.CPAR[:, 0, l, cc:cc + 1], op0=AL.mult, op1=AL.add), [ru[cc], P.rPAR], [rCC[cc]])
            for k in range(1, CONV_K):
                for cc in range(4):
                    P.op(P.DVE, lambda cc=cc, k=k: nc.vector.scalar_tensor_tensor(
                        out=CC[:, cc, 0:n], in0=u[:, cc, k:k + n], scalar=P.WDW[:, l, cc, k:k + 1], in1=CC[:, cc, 0:n],
                        op0=AL.mult, op1=AL.add), [ru[cc], P.rPAR, rCC[cc]], [rCC[cc]])
            if hist_out:
                for cc in range(4):
                    P.op(P.POOL, lambda cc=cc: nc.gpsimd.tensor_copy(out=P.UH[:, l, cc, :],
                                                                     in_=u[:, cc, n:n + CONV_K - 1]),
                         [ru[cc]], [P.rUH[l]])
            if cv_out is not None:
                b = P.bank()
                for cc in range(4):
                    P.op(P.PE, lambda cc=cc, b=b: nc.tensor.transpose(
                        P.PS[b][0:CONV_K - 1, cc * 128:(cc + 1) * 128], u[:, cc, n:n + CONV_K - 1], P.IDF[:]),
                        [ru[cc], P.rIDF], [P.rPS[b]], signal=(cc == 3))
                P.op(P.DVE, lambda b=b: nc.vector.tensor_copy(out=STF[0][0:CONV_K - 1, :],
                                                              in_=P.PS[b][0:CONV_K - 1, :]), [P.rPS[b]], [rSTF[0]])
                P.dma(P.SP, cv_out, STF[0][0:CONV_K - 1, :], [rSTF[0]], [], is_output=True)
            b1, b2 = P.bank(), P.bank()
            for cc in range(4):
                P.mm(P.PS[b1][:, 0:n], P.ONF[:], CC[:, cc, 0:n], cc == 0, cc == 3, [rCC[cc], P.rCONST], [P.rPS[b1]])
            for cc in range(4):
                j = cc % 2
                P.op(P.ACT, lambda cc=cc, j=j: nc.scalar.activation(out=SG[j][:, 0:n], in_=CC[:, cc, 0:n],
                                                                    func=AF.Square), [rCC[cc]], [rSG[j]])
                P.mm(P.PS[b2][:, 0:n], P.ONF[:], SG[j][:, 0:n], cc == 0, cc == 3, [rSG[j], P.rCONST], [P.rPS[b2]])
            P.op(P.DVE, lambda: nc.vector.tensor_scalar(out=LNM[:, 0:n], in0=P.PS[b1][:, 0:n], scalar1=1.0 / CONV_CH,
                                                        scalar2=None, op0=AL.mult), [P.rPS[b1]], [rLNM])
            P.op(P.DVE, lambda: nc.vector.tensor_tensor(out=LNR[:, 0:n], in0=LNM[:, 0:n], in1=LNM[:, 0:n], op=AL.mult),
                 [rLNM], [rLNR])
            P.op(P.DVE, lambda: nc.vector.scalar_tensor_tensor(out=LNR[:, 0:n], in0=P.PS[b2][:, 0:n],
                                                               scalar=1.0 / CONV_CH, in1=LNR[:, 0:n], op0=AL.mult,
                                                               op1=AL.subtract), [P.rPS[b2], rLNR], [rLNR])
            P.op(P.DVE, lambda: nc.vector.tensor_scalar(out=LNR[:, 0:n], in0=LNR[:, 0:n], scalar1=EPS, scalar2=None,
                                                        op0=AL.add), [rLNR], [rLNR])
            P.op(P.ACT, lambda: nc.scalar.activation(out=LNR[:, 0:n], in_=LNR[:, 0:n], func=AF.Sqrt), [rLNR], [rLNR])
            P.op(P.DVE, lambda: nc.vector.reciprocal(out=LNR[:, 0:n], in_=LNR[:, 0:n]), [rLNR], [rLNR])
            for cc in range(4):
                j = cc % 2
                P.op(P.DVE, lambda cc=cc, j=j: nc.vector.tensor_tensor(out=SG[j][:, 0:n], in0=CC[:, cc, 0:n],
                                                                       in1=LNM[:, 0:n], op=AL.subtract),
                     [rCC[cc], rLNM], [rSG[j]])
                P.op(P.DVE, lambda j=j: nc.vector.tensor_tensor(out=SG[j][:, 0:n], in0=SG[j][:, 0:n], in1=LNR[:, 0:n],
                                                                op=AL.mult), [rSG[j], rLNR], [rSG[j]])
                P.op(P.ACT, lambda cc=cc, j=j: nc.scalar.activation(
                    out=CNT[:, cc, 0:n], in_=SG[j][:, 0:n], func=AF.Silu, scale=P.CPAR[:, 1, l, cc:cc + 1],
                    bias=P.CPAR[:, 2, l, cc:cc + 1]), [rSG[j], P.rPAR], [rCNT[cc]])

        P.stage('C')
        fm_group(C_QM, ev_copy(P.QMT, P.rQMT, 0))
        fm_group(C_GM, ev_silu(12))
        fm_group(C_GA, ev_silu(0))
        fm_group(C_GA + 512, ev_silu(4))
        fm_group(C_Q, ev_copy(P.QT, P.rQT, 0))
        conv_branch(UT, rUT, T, slice(0, T), True, P.cvp[l] if c == self.n_chunks - 1 else None, 1)
        fm_group(C_Q + 512, ev_copy(P.QT, P.rQT, 4))
        wpw, rwpw = P.wload(P.w_pw2[l].rearrange("(k p) n -> p k n", p=128), kdim=4)
        conv_branch(UT, rUT, T, slice(0, T), True, None, 2)
        if samp:
            conv_branch(UTS, rUTS, NS, slice(T, T + NS), False, P.cvs[l], 1)
            conv_branch(UTS, rUTS, NS, slice(T, T + NS), False, None, 2)


        P.stage('Brest')
        self.overlay()
        EX = [A("EX%d" % i, [128, 512], F32) for i in range(2)]
        rEX = [P.LR("EX0"), P.LR("EX1")]
        NPT = 12
        PT = [A("PT%d" % i, [128, 512], BF16) for i in range(NPT)]
        rPT = [P.LR("PT%d" % i) for i in range(NPT)]
        LT = A("LT", [128, T], F32)
        RD = A("RD", [128, T], F32)
        AT = A("AT", [128, T], F32)
        rLT, rRD, rAT = P.LR("LT"), P.LR("RD"), P.LR("AT")
        ET = [A("ET%d" % i, [128, 6, 128], F32) for i in range(2)]
        rET = [P.LR("ET0"), P.LR("ET1")]
        E3S = [A("E3S%d" % i, [128, 32], F32) for i in range(2)]
        rE3S = [P.LR("E3S0"), P.LR("E3S1")]
        KH = [A("KH%d" % i, [128, WIN], BF16) for i in range(2)]
        rKH = [P.LR("KH0"), P.LR("KH1")]
        VB = [A("VB%d" % i, [128, 29, 128], BF16) for i in range(2)]
        rVB = [P.LR("VB0"), P.LR("VB1")]
        V3 = [A("V3%d" % i, [32, 16, 128], BF16) for i in range(2)]
        rV3 = [P.LR("V30"), P.LR("V31")]
        pti = [0]

        def next_pt():
            i = pti[0] % NPT
            pti[0] += 1
            return i

        def finish_head(f, bacc, bden, n=T, mc=slice(0, T)):
            P.op(P.ACT, lambda: nc.scalar.activation(out=LT[:, 0:n], in_=P.PS[bden][:, 0:n], func=AF.Ln),
                 [P.rPS[bden]], [rLT])
            P.op(P.ACT, lambda: nc.scalar.activation(out=RD[:, 0:n], in_=LT[:, 0:n], func=AF.Exp, scale=-1.0),
                 [rLT], [rRD])
            P.op(P.DVE, lambda: nc.vector.tensor_tensor(out=AT[:, 0:n], in0=P.PS[bacc][:, 0:n], in1=RD[:, 0:n],
                                                        op=AL.mult), [P.rPS[bacc], rRD], [rAT])
            P.op(P.DVE, lambda: nc.vector.tensor_tensor(out=P.MIX[:, f, mc], in0=AT[:, 0:n], in1=P.MIX[:, f, mc],
                                                        op=AL.mult), [rAT, P.rMIX[f]], [P.rMIX[f]])

        P.dma(P.SP, P.MEMK[:].rearrange("p h m -> p (h m)"), P.MKd[l], [P.r_MKd[l]], [P.rMEMK])
        P.dma(P.SP, P.MEMV[:].rearrange("p b n -> p (b n)"), P.MVd[l], [P.r_MVd[l]], [P.rMEMV])
        def mem_attn(kt, rkt, vt, rvt, n, mc):
            for h in range(4):
                pts = []
                for mb in range(2):
                    b = P.bank()
                    P.mm(P.PS[b][:, 0:n], kt[:, h, mb * 128:(mb + 1) * 128], P.QMT[:, h, mc], True, True,
                         [rkt, P.rQMT[h]], [P.rPS[b]])
                    i = next_pt()
                    P.op(P.ACT, lambda b=b, i=i: nc.scalar.activation(out=PT[i][:, 0:n], in_=P.PS[b][:, 0:n],
                                                                      func=AF.Exp, scale=SCALE), [P.rPS[b]], [rPT[i]])
                    pts.append(i)
                bacc, bden = P.bank(), P.bank()
                for mb in range(2):
                    i = pts[mb]
                    P.mm(P.PS[bacc][:, 0:n], vt[:, mb, h * 128:(h + 1) * 128], PT[i][:, 0:n], mb == 0, mb == 1,
                         [rvt, rPT[i]], [P.rPS[bacc]])
                for mb in range(2):
                    i = pts[mb]
                    P.mm(P.PS[bden][:, 0:n], P.ONB[:], PT[i][:, 0:n], mb == 0, mb == 1, [P.rCONST, rPT[i]],
                         [P.rPS[bden]])
                finish_head(12 + h, bacc, bden, n, mc)

        mem_attn(P.MEMK, P.rMEMK, P.MEMV, P.rMEMV, T, slice(0, T))
        if samp:
            CMK = A("CMK", [128, 2, X_W], BF16)
            CMV = A("CMV", [128, 2, X_W], BF16)
            MKS = A("MKS", [128, 4, N_MEM], BF16)
            rCMK, rCMV, rMKS = P.LR("CMK"), P.LR("CMV"), P.LR("MKS")
            P.dma(P.POOL, CMK[:], P.cmk[l].rearrange("(b p) n -> p b n", p=128), [], [rCMK])
            P.dma(P.POOL, CMV[:], P.cmv[l].rearrange("(b p) n -> p b n", p=128), [], [rCMV])
            for mb in range(2):
                b = P.bank()
                for h in range(4):
                    P.op(P.PE, lambda mb=mb, h=h, b=b: nc.tensor.transpose(
                        P.PSB[b][:, h * 128:(h + 1) * 128], CMK[:, mb, h * 128:(h + 1) * 128], P.IDB[:]),
                        [rCMK, P.rIDB], [P.rPS[b]], signal=(h == 3))
                P.any_copy(MKS[:, :, mb * 128:(mb + 1) * 128], P.PSB[b][:, 0:512].rearrange("p (h m) -> p h m", h=4),
                           [P.rPS[b]], [rMKS])
            mem_attn(MKS, rMKS, CMV, rCMV, NS, slice(T, T + NS))

        P.stage('D')
        nh = min(c, 4)
        M3 = 32 * nh
        jlo = 0 if nh else 1
        q, ko = P.QT, P.KTown

        def sbank():
            b = 4 + (self.bank_i % 4)
            self.bank_i += 1
            return b

        def bc(e2d, n):
            a = e2d
            return bass.AP(tensor=a.tensor, offset=a.offset, ap=[list(a.ap[0]), [0, n], list(a.ap[1])])

        def bci(e2d, n):
            a = e2d
            return bass.AP(tensor=a.tensor, offset=a.offset, ap=[list(a.ap[0]), list(a.ap[1]), [0, n]])

        def softmax_tile(s, b, np_, width, e_ap, r_extra=None, inner=False):
            j = self.ee % 2
            self.ee += 1
            P.op(P.ACT, lambda: nc.scalar.activation(out=EX[j][0:np_, 0:width], in_=P.PS[b][0:np_, 0:width],
                                                     func=AF.Exp, scale=SCALE), [P.rPS[b]], [rEX[j]])
            i = next_pt()
            if inner:
                n = e_ap.shape[2]
                pat, kw = "p (w n) -> p w n", dict(n=n)
            else:
                n = e_ap.shape[1]
                pat, kw = "p (n w) -> p n w", dict(n=n)
            P.op(P.DVE, lambda: nc.vector.tensor_tensor(
                out=PT[i][0:np_, 0:width].rearrange(pat, **kw), in0=EX[j][0:np_, 0:width].rearrange(pat, **kw),
                in1=e_ap, op=AL.mult), [rEX[j], rET[s]] + ([r_extra] if r_extra is not None else []), [rPT[i]])
            return i

        def stage1(h):
            s = h % 2
            rq, rko = P.rQT[h], P.rKTown[h // 4]
            P.dma(P.SP, ET[s][:], P.Ed[h], [P.r_Ed], [rET[s]])
            if 0 < M3 < 128:
                P.dma(P.SP, E3S[s][0:M3, :], P.Ed[h, 128 - M3:128, 5, 0:32], [P.r_Ed], [rE3S[s]])
            if nh:
                P.dma(P.SP, KH[s][:, WIN - nh * T:WIN], P.KTd[l, h, :, t0 - nh * T:t0],
                      [P.r_KTd[l][cc_] for cc_ in range(c - nh, c)], [rKH[s]])
            vt = P.Vd.tensor
            vbase = l * SEQ * ATT_W + h * 128

            def vap(tok0, pstride, n_part, nblk, bstride):
                return bass.AP(tensor=vt, offset=vbase + tok0 * ATT_W,
                               ap=[[pstride * ATT_W, n_part], [bstride * ATT_W, nblk], [1, 128]])
            rv_own = [P.r_Vd[l][c]]
            rv_h = [P.r_Vd[l][cc_] for cc_ in range(c - nh, c)]
            P.dma(P.SP, VB[s][:, 0:4, :], vap(t0, 1, 128, 4, 128), rv_own, [rVB[s]])
            if nh:
                P.dma(P.SP, VB[s][:, 4:5, :], vap(t0 - 128, 1, 128, 1, 128), rv_h, [rVB[s]])
            P.dma(P.SP, VB[s][:, 5:9, :], vap(t0, 4, 128, 4, 1), rv_own, [rVB[s]])
            if nh:
                P.dma(P.SP, VB[s][:, 9:13, :], vap(t0 - 512, 4, 128, 4, 1), rv_h, [rVB[s]])
            if nh:
                P.dma(P.SP, VB[s][0:M3, 13:29, :], vap(t0 - nh * T, 16, M3, 16, 1), rv_h, [rVB[s]])
            pt = {}
            b = sbank()
            for j in range(4):
                P.mm(P.PS[b][:, j * 128:(j + 1) * 128], ko[:, h, j * 128:(j + 1) * 128], q[:, h, j * 128:(j + 1) * 128],
                     True, True, [rko, rq], [P.rPS[b]], signal=(j == 3))
            pt["1c"] = softmax_tile(s, b, 128, 512, bc(ET[s][:, 0, :], 4))
            b = sbank()
            for j in range(jlo, 4):
                kap = KH[s][:, WIN - 128:WIN] if j == 0 else ko[:, h, (j - 1) * 128:j * 128]
                P.mm(P.PS[b][:, j * 128:(j + 1) * 128], kap, q[:, h, j * 128:(j + 1) * 128], True, True,
                     [rko, rq, rKH[s]], [P.rPS[b]], signal=(j == 3))
            pt["1p"] = softmax_tile(s, b, 128, 512, bc(ET[s][:, 1, :], 4))
            b = sbank()
            for r in range(4):
                P.mm(P.PS[b][:, r:512:4], ko[:, h, r:T:4], q[:, h, r:T:4], True, True,
                     [rko, rq], [P.rPS[b]], signal=(r == 3))
            pt["2c"] = softmax_tile(s, b, 128, 512, bci(ET[s][:, 2, :], 4), inner=True)
            if nh:
                b = sbank()
                for r in range(4):
                    P.mm(P.PS[b][:, r:512:4], KH[s][:, WIN - 512 + r:WIN:4], q[:, h, r:T:4], True, True,
                         [rKH[s], rq], [P.rPS[b]], signal=(r == 3))
                pt["2p"] = softmax_tile(s, b, 128, 512, bci(ET[s][:, 3, :], 4), inner=True)
            if nh:
                b = sbank()
                for r in range(16):
                    P.mm(P.PS[b][0:M3, r:512:16], KH[s][:, WIN - nh * T + r:WIN:16], q[:, h, r:T:16],
                         True, True, [rKH[s], rq], [P.rPS[b]], signal=(r == 15))
                e3 = ET[s][0:M3, 5, 0:32] if M3 == 128 else E3S[s][0:M3, :]
                pt["3a"] = softmax_tile(s, b, M3, 512, bci(e3, 16), rE3S[s], inner=True)
            return pt

        def stage2(h, pt):
            s = h % 2
            bacc, bden = (0, 1) if s == 0 else (2, 3)
            acc, den = P.PS[bacc], P.PS[bden]

            def pv(out_ap, v_ap, i, pt_ap, rv):
                P.mm(out_ap, v_ap, pt_ap, False, False, [rv, rPT[i]], [P.rPS[bacc]], signal=False,
                     skip_group_check=True)

            def dn(out_ap, np_, i, pt_ap):
                P.mm(out_ap, P.ONB[0:np_, :], pt_ap, False, False, [P.rCONST, rPT[i]], [P.rPS[bden]], signal=False,
                     skip_group_check=True)

            i_c, i_p = pt["1c"], pt["1p"]
            for j in range(4):
                pv(acc[:, j * 128:(j + 1) * 128], VB[s][:, j, :], i_c, PT[i_c][:, j * 128:(j + 1) * 128], rVB[s])
            for j in range(jlo, 4):
                vb = VB[s][:, 4, :] if j == 0 else VB[s][:, j - 1, :]
                pv(acc[:, j * 128:(j + 1) * 128], vb, i_p, PT[i_p][:, j * 128:(j + 1) * 128], rVB[s])
            dn(den[:, 0:512], 128, i_c, PT[i_c][:, 0:512])
            dn(den[:, jlo * 128:512], 128, i_p, PT[i_p][:, jlo * 128:512])
            i_c = pt["2c"]
            for r in range(4):
                pv(acc[:, r:512:4], VB[s][:, 5 + r, :], i_c, PT[i_c][:, r:512:4], rVB[s])
            dn(den[:, 0:512], 128, i_c, PT[i_c][:, 0:512])
            if nh:
                i_p = pt["2p"]
                for r in range(4):
                    pv(acc[:, r:512:4], VB[s][:, 9 + r, :], i_p, PT[i_p][:, r:512:4], rVB[s])
                dn(den[:, 0:512], 128, i_p, PT[i_p][:, 0:512])
            if nh:
                i_a = pt["3a"]
                for r in range(16):
                    pv(acc[:, r:512:16], VB[s][0:M3, 13 + r, :], i_a, PT[i_a][0:M3, r:512:16], rVB[s])
                dn(den[:, 0:512], M3, i_a, PT[i_a][0:M3, 0:512])
            finish_head(h, bacc, bden)

        def zero_acc(h):
            bacc, bden = (0, 1) if h % 2 == 0 else (2, 3)
            P.op(P.DVE, lambda: nc.vector.memset(P.PS[bacc][:], 0.0), [], [P.rPS[bacc]])
            P.op(P.DVE, lambda: nc.vector.memset(P.PS[bden][:], 0.0), [], [P.rPS[bden]])

        pts = stage1(0)
        for h in range(8):
            zero_acc(h)
            nxt = stage1(h + 1) if h + 1 < 8 else None
            stage2(h, pts)
            pts = nxt

        if samp:
            self.sample_attention(l, A, EX, rEX, PT, rPT, finish_head, KH, rKH, VB, rVB)
        P.stage('E')
        self.overlay()
        Y = A("Y", [128, 4, D_MODEL], F32)
        rY = [P.LR("Y%d" % b) for b in range(4)]
        GP = A("GP", [128, D_MODEL], F32)
        rGP = P.LR("GP")
        JK = A("JK", [128, D_MODEL], BF16)
        rJK = P.LR("JK")
        TM = A("TM", [128, D_MODEL], F32)
        rTM = P.LR("TM")
        SQ = A("SQ", [128, 16], F32)
        rSQ = [P.LR("SQ%d" % b) for b in range(4)]
        if samp:
            YS = A("YS", [NS, D_MODEL], F32)
            rYS = P.LR("YS")
        P.dma(P.SP, GP[:], bass.AP(tensor=P.g_post.tensor, offset=l * D_MODEL, ap=[[0, 128], [1, D_MODEL]]),
              [], [rGP])
        for g in range(4):
            w, rw = P.wload(P.w_out[l].rearrange("(k p) n -> p k n", p=128)[:, :, g * 512:(g + 1) * 512])
            for blk in range(4):
                b = P.bank()
                for f in range(16):
                    P.mm(P.PS[b][:], P.MIX[:, f, blk * 128:(blk + 1) * 128], w[:, f, :], f == 0, f == 15,
                         [rw, P.rMIX[f]], [P.rPS[b]])
                P.op(P.ACT, lambda b=b, blk=blk, g=g: nc.scalar.activation(
                    out=JK[:, 0:512], in_=P.PS[b][:], func=AF.Square, accum_out=SQ[:, blk * 4 + g:blk * 4 + g + 1]),
                    [P.rPS[b]], [rJK, rSQ[blk]])
                P.op(P.DVE, lambda b=b, blk=blk, g=g: nc.vector.tensor_tensor(
                    out=Y[:, blk, g * 512:(g + 1) * 512], in0=P.PS[b][:], in1=GP[:, g * 512:(g + 1) * 512], op=AL.mult),
                    [P.rPS[b], rGP], [rY[blk]])
            if samp:
                b = P.bank()
                for f in range(16):
                    P.mm(P.PS[b][0:NS, :], P.MIX[:, f, T:T + NS], w[:, f, :], f == 0, f == 15, [rw, P.rMIX[f]],
                         [P.rPS[b]])
                P.any_copy(YS[:, g * 512:(g + 1) * 512], P.PS[b][0:NS, :], [P.rPS[b]], [rYS])
        P.op(P.DVE, lambda: nc.vector.tensor_reduce(
            out=P.SS[:, 4:8], in_=SQ[:, 0:16].rearrange("p (b g) -> p b g", g=4), axis=mybir.AxisListType.X,
            op=AL.add), rSQ, P.rSS)
        P.op(P.DVE, lambda: nc.vector.tensor_scalar(out=P.SS[:, 4:8], in0=P.SS[:, 4:8], scalar1=1.0 / D_MODEL,
                                                    scalar2=EPS, op0=AL.mult, op1=AL.add), P.rSS, P.rSS)
        P.op(P.ACT, lambda: nc.scalar.activation(out=P.SS[:, 4:8], in_=P.SS[:, 4:8], func=AF.Sqrt), P.rSS, P.rSS)
        P.op(P.DVE, lambda: nc.vector.reciprocal(out=P.SS[:, 4:8], in_=P.SS[:, 4:8]), P.rSS, P.rSS)
        for blk in range(4):
            P.op(P.DVE, lambda blk=blk: nc.vector.scalar_tensor_tensor(
                out=P.X[:, blk, :], in0=Y[:, blk, :], scalar=P.SS[:, 4 + blk:5 + blk], in1=P.X[:, blk, :],
                op0=AL.mult, op1=AL.add), [rY[blk], P.rSS[blk], P.rX[blk]], [P.rX[blk]])
            if last:
                P.dma(P.SP, P.y_prompt[t0 + blk * 128:t0 + (blk + 1) * 128, :], P.X[:, blk, :], [P.rX[blk]], [], is_output=True)
        if samp:
            P.op(P.ACT, lambda: nc.scalar.activation(out=JK[0:NS, :], in_=YS[:], func=AF.Square,
                                                     accum_out=P.SS[0:NS, 9:10]), [rYS], [rJK, P.rSS[0]])
            self.rstd(P.SS[0:NS, 9:10], P.rSS[0], 1.0 / D_MODEL)
            P.op(P.DVE, lambda: nc.vector.scalar_tensor_tensor(
                out=TM[0:NS, :], in0=YS[:], scalar=P.SS[0:NS, 9:10], in1=GP[0:NS, :], op0=AL.mult, op1=AL.mult),
                [rYS, P.rSS[0], rGP], [rTM])
            P.op(P.POOL, lambda: nc.gpsimd.tensor_tensor(out=P.XS[:], in0=P.XS[:], in1=TM[0:NS, :], op=AL.add),
                 [rTM, P.rXS], [P.rXS])
            if last:
                P.dma(P.SP, P.y_sample, P.XS[:], [P.rXS], [], is_output=True)

    def sample_attention(self, l, A, EX, rEX, PT, rPT, finish_head, KH, rKH, VB, rVB):
        nc = self.nc
        P = self
        KC = [VB[0][:, 0:16, :], VB[1][:, 0:16, :]]
        rKC = rVB
        KTS, rKTS = KH, rKH
        VC = [A("VC%d" % i, [128, 16, 128], BF16) for i in range(2)]
        rVC = [P.LR("VC0"), P.LR("VC1")]
        KNB = A("KNB", [NS, 128], BF16)
        VNB = A("VNB", [NS, 128], BF16)
        KTN = A("KTN", [128, NS], BF16)
        rKNB, rVNB, rKTN = P.LR("KNB"), P.LR("VNB"), P.LR("KTN")
        PN = A("PN", [NS, NS], BF16)
        rPN = P.LR("PN")
        qc = slice(T, T + NS)
        for h in range(N_ATT_HEADS):
            s = h % 2
            hc = slice(h * 128, (h + 1) * 128)
            P.dma(P.POOL, KC[s], P.cache_k[l, :, hc].rearrange("(b p) d -> p b d", p=128), [], [rKC[s]])
            P.dma(P.POOL, VC[s][:], P.cache_v[l, :, hc].rearrange("(b p) d -> p b d", p=128), [], [rVC[s]])
            for g in range(4):
                b = P.bank()
                for j in range(4):
                    blk = g * 4 + j
                    P.op(P.PE, lambda blk=blk, j=j, b=b: nc.tensor.transpose(
                        P.PSB[b][:, j * 128:(j + 1) * 128], KC[s][:, blk, :], P.IDB[:]),
                        [rKC[s], P.rIDB], [P.rPS[b]], signal=(j == 3))
                P.any_copy(KTS[s][:, g * 512:(g + 1) * 512], P.PSB[b][:, 0:512], [P.rPS[b]], [rKTS[s]])
            P.op(P.DVE, lambda: nc.vector.tensor_copy(out=KNB[:], in_=P.KVS[:, h * 128:(h + 1) * 128]), [P.rKVS], [rKNB])
            P.op(P.DVE, lambda: nc.vector.tensor_copy(out=VNB[:], in_=P.KVS[:, ATT_W + h * 128:ATT_W + (h + 1) * 128]),
                 [P.rKVS], [rVNB])
            b = P.bank()
            P.op(P.PE, lambda b=b: nc.tensor.transpose(P.PSB[b][:, 0:NS], KNB[:], P.IDB[0:NS, 0:NS]),
                 [rKNB, P.rIDB], [P.rPS[b]])
            P.any_copy(KTN[:], P.PSB[b][:, 0:NS], [P.rPS[b]], [rKTN])
            b = P.bank()
            for blk in range(16):
                P.mm(P.PS[b][:, blk * NS:(blk + 1) * NS], KTS[s][:, blk * 128:(blk + 1) * 128], P.QT[:, h, qc], True, True,
                     [rKTS[s], P.rQT[h]], [P.rPS[b]], signal=(blk == 15))
            j = self.ee % 2
            self.ee += 1
            P.op(P.ACT, lambda b=b, j=j: nc.scalar.activation(out=EX[j][:, 0:128], in_=P.PS[b][:, 0:128], func=AF.Exp,
                                                              scale=SCALE), [P.rPS[b]], [rEX[j]])
            i = self.spt % len(PT)
            self.spt += 1
            P.op(P.DVE, lambda j=j, i=i: nc.vector.tensor_tensor(
                out=PT[i][:, 0:128].rearrange("p (b q) -> p b q", b=16),
                in0=EX[j][:, 0:128].rearrange("p (b q) -> p b q", b=16), in1=P.ES[:, h, 0:16, :], op=AL.mult),
                [rEX[j], P.rES], [rPT[i]])
            b2 = P.bank()
            P.mm(P.PS[b2][0:NS, 0:NS], KTN[:], P.QT[:, h, qc], True, True, [rKTN, P.rQT[h]], [P.rPS[b2]])
            j2 = self.ee % 2
            self.ee += 1
            P.op(P.ACT, lambda b2=b2, j2=j2: nc.scalar.activation(out=EX[j2][0:NS, 0:NS], in_=P.PS[b2][0:NS, 0:NS],
                                                                  func=AF.Exp, scale=SCALE), [P.rPS[b2]], [rEX[j2]])
            P.op(P.DVE, lambda j2=j2: nc.vector.tensor_tensor(out=PN[:], in0=EX[j2][0:NS, 0:NS], in1=P.ES[0:NS, h, 16, :],
                                                              op=AL.mult), [rEX[j2], P.rES], [rPN])
            bacc, bden = P.bank(), P.bank()
            for blk in range(16):
                P.mm(P.PS[bacc][:, 0:NS], VC[s][:, blk, :], PT[i][:, blk * NS:(blk + 1) * NS], blk == 0, False,
                     [rVC[s], rPT[i]], [P.rPS[bacc]], signal=False)
            P.mm(P.PS[bacc][:, 0:NS], VNB[:], PN[:], False, True, [rVNB, rPN], [P.rPS[bacc]])
            for blk in range(16):
                P.mm(P.PS[bden][:, 0:NS], P.ONB[:], PT[i][:, blk * NS:(blk + 1) * NS], blk == 0, False,
                     [P.rCONST, rPT[i]], [P.rPS[bden]], signal=False)
            P.mm(P.PS[bden][:, 0:NS], P.ONB[0:NS, :], PN[:], False, True, [P.rCONST, rPN], [P.rPS[bden]])
            finish_head(h, bacc, bden, NS, qc)

    def rstd(self, ap, res, inv_n):
        nc = self.nc
        P = self
        P.op(P.DVE, lambda: nc.vector.tensor_scalar(out=ap, in0=ap, scalar1=inv_n, scalar2=EPS, op0=AL.mult, op1=AL.add),
             [res], [res])
        P.op(P.ACT, lambda: nc.scalar.activation(out=ap, in_=ap, func=AF.Sqrt), [res], [res])
        P.op(P.DVE, lambda: nc.vector.reciprocal(out=ap, in_=ap), [res], [res])

    def build(self):
        P = self
        try:
            P.dma(P.SP, P.X[:], P.x_prompt[0:T, :].rearrange("(b p) d -> p b d", p=128), [], P.rX)
            self.setup()
            P.stage('setup')
            for c in range(self.n_chunks):
                t0 = c * T
                if c > 0:
                    P.dma(P.SP, P.X[:], P.x_prompt[t0:t0 + T, :].rearrange("(b p) d -> p b d", p=128), [], P.rX)
                for l in range(self.depth):
                    self.chunk_layer(c, l)
        except _Stop:
            pass
        for tok in self.out_tokens.values():
            self._wait(P.SP, tok)
        return self.nc


def _static_tables():
    oh = np.zeros((4, N_BUCKETS, 384), np.float32)
    for p, d in enumerate(DILS):
        for j in range(129):
            n = 128 + j
            oh[p, int(t5_bucket(j * d)), 383 - n] = 1.0
    for j in range(129):
        oh[3, int(t5_bucket(j * 4)), 383 - (128 + j)] = 2.0 if j % 4 == 0 else 1.0
    return oh


def _sample_table():
    oh = np.zeros((N_BUCKETS, 2064), np.float32)
    for n in range(2056):
        d = 2055 - n
        mult = int(d <= 128) + int(d % 4 == 0 and d <= 512) + int(d % 16 == 0 and d <= 2048)
        if mult:
            oh[int(t5_bucket(d)), n] = float(mult)
    return oh


def _host_inputs(inp, n_cores=N_CORES, with_sample=True, depth=DEPTH, seqr=SEQ):
    f = lambda a: np.ascontiguousarray(np.asarray(a, dtype=np.float32))
    oh = _static_tables()
    ident = np.eye(128, dtype=np.float32)
    wdwT = f(np.asarray(inp["w_dw"]).reshape(DEPTH, CONV_K, 4, 128).transpose(3, 0, 2, 1).reshape(128, -1))
    cpar = f(np.stack([np.asarray(inp[k]).reshape(DEPTH, 4, 128) for k in ("b_dw", "ln_conv_g", "ln_conv_b")], 0)
             .transpose(3, 0, 1, 2).reshape(128, -1))
    shared = {
        "rel_bias": f(inp["rel_bias"]), "ohrev": oh, "ident": ident,
        "norm_pre_g": f(inp["norm_pre_g"][:depth]), "norm_post_g": f(inp["norm_post_g"][:depth]),
        "w_in": f(inp["w_in"][:depth]), "w_out": f(inp["w_out"][:depth]), "w_pw2": f(inp["w_pw2"][:depth]),
        "w_mem_kv": f(inp["w_mem_kv"][:depth]), "wdwT": wdwT, "cpar": cpar,
    }
    maps = []
    for core in range(n_cores):
        b = core // 4
        m = dict(shared)
        if core % 4 == 0:
            m["x_prompt"] = f(inp["x_prompt"][b][:seqr])
            m["memT"] = f(np.asarray(inp["mem_prompt"][b]).T)
        else:
            m["x_prompt"] = np.zeros((seqr, D_MODEL), np.float32)
            m["memT"] = np.zeros((D_MODEL, N_MEM), np.float32)
        if with_sample:
            m["x_sample"] = f(inp["x_sample"][core])
            m["cache_k"] = f(np.asarray(inp["cache_attn_k"])[:depth, core].reshape(depth, L_CACHE, ATT_W))
            m["cache_v"] = f(np.asarray(inp["cache_attn_v"])[:depth, core].reshape(depth, L_CACHE, ATT_W))
            m["cache_mem_k"] = f(np.asarray(inp["cache_mem_k"])[:depth, core].reshape(depth, N_MEM, X_W))
            m["cache_mem_v"] = f(np.asarray(inp["cache_mem_v"])[:depth, core].reshape(depth, N_MEM, X_W))
            m["state_convT"] = f(np.asarray(inp["state_conv"])[:, core].reshape(DEPTH, CONV_K - 1, 4, 128)
                                 .transpose(3, 0, 2, 1).reshape(128, -1))
            m["ohs"] = _sample_table()
        maps.append(m)
    return maps


_CACHE = {}


def kernel(**inputs):
    with_sample = True
    key = ("full", with_sample)
    if key not in _CACHE:
        _CACHE[key] = Prog(with_sample=with_sample).build()
    nc = _CACHE[key]
    maps = _host_inputs(inputs, with_sample=with_sample)
    res = run_bass_kernel_spmd(nc, maps, core_ids=list(range(N_CORES))).results
    pc = (0, 4)
    y_prompt = np.stack([res[c]["y_prompt"] for c in pc], 0)
    akp = np.stack([res[c]["akp"] for c in pc], 1).reshape(DEPTH, BATCH, WIN, N_ATT_HEADS, HEAD_DIM)
    avp = np.stack([res[c]["avp"] for c in pc], 1).reshape(DEPTH, BATCH, WIN, N_ATT_HEADS, HEAD_DIM)
    cvp = np.stack([res[c]["cvp"] for c in pc], 1)
    mkp = np.stack([res[c]["mkp"] for c in pc], 1).reshape(DEPTH, BATCH, N_MEM, N_X_HEADS, HEAD_DIM)
    mvp = np.stack([res[c]["mvp"] for c in pc], 1).reshape(DEPTH, BATCH, N_MEM, N_X_HEADS, HEAD_DIM)
    allc = range(N_CORES)
    y_sample = np.stack([res[c]["y_sample"] for c in allc], 0)
    aks = np.stack([res[c]["aks"] for c in allc], 1).reshape(DEPTH, DEC_BATCH, L_CACHE, N_ATT_HEADS, HEAD_DIM)
    avs = np.stack([res[c]["avs"] for c in allc], 1).reshape(DEPTH, DEC_BATCH, L_CACHE, N_ATT_HEADS, HEAD_DIM)
    cvs = np.stack([res[c]["cvs"] for c in allc], 1)
    return (y_prompt, y_sample, akp, avp, cvp, mkp, mvp, aks, avs, cvs)
```

```python
import numpy as np
import concourse.bass as bass
import concourse.mybir as mybir
from concourse.bass_utils import run_bass_kernel_spmd

F32 = mybir.dt.float32
BF16 = mybir.dt.bfloat16
AF = mybir.ActivationFunctionType
AL = mybir.AluOpType

D_MODEL = 2048
BATCH = 2
SEQ = 4096
DEPTH = 4
DEC_BATCH = 8
DEC_SEQ = 8
N_MEM = 256
HEAD_DIM = 128
ATT_W = 1024
N_ATT_HEADS = 8
DILS = (1, 4, 16)
WIN = 2048
N_BUCKETS = 32
MAX_DIST = WIN
CONV_CH = 512
CONV_K = 31
X_W = 512
N_X_HEADS = 4
MIX_W = 2048
IN_W = 6656
EPS = 1e-6
SCALE = HEAD_DIM ** -0.5
N_CORES = 8
T = 512
NCH = SEQ // T
L_CACHE = 2048
NS = DEC_SEQ

C_Q, C_K, C_V, C_GA, C_UV, C_UG, C_GC, C_QM, C_GM = 0, 1024, 2048, 3072, 4096, 4608, 5120, 5632, 6144


def t5_bucket(dist):
    dist = np.asarray(dist)
    max_exact = N_BUCKETS // 2
    large = max_exact + (np.log(np.maximum(dist, 1) / max_exact)
                         / np.log(MAX_DIST / max_exact) * (N_BUCKETS - max_exact)).astype(np.int32)
    large = np.minimum(large, N_BUCKETS - 1)
    return np.where(dist < max_exact, dist, large).astype(np.int32)


class Res:
    __slots__ = ("name", "w", "r", "slot", "dram", "excl")

    def __init__(self, name, dram=False, excl=False):
        self.name = name
        self.w = {}
        self.r = {}
        self.slot = None
        self.dram = dram
        self.excl = excl


class Eng:
    def __init__(self, name, h, sem):
        self.name, self.h, self.sem, self.count = name, h, sem, 0


class Slot:
    def __init__(self, sem):
        self.sem, self.count = sem, 0


class _Stop(Exception):
    pass


class Prog:
    def __init__(self, n_chunks=NCH, depth=DEPTH, with_sample=True, stop_after=None):
        self.n_chunks, self.depth, self.with_sample = n_chunks, depth, with_sample
        self.stop_after = stop_after
        self.SEQR = n_chunks * T
        nc = self.nc = bass.Bass("TRN2", target_bir_lowering=False)
        self.semid = 0
        self.PE = Eng("pe", nc.tensor, self.new_sem())
        self.ACT = Eng("act", nc.scalar, self.new_sem())
        self.DVE = Eng("dve", nc.vector, self.new_sem())
        self.POOL = Eng("pool", nc.gpsimd, self.new_sem())
        self.SP = Eng("sp", nc.sync, self.new_sem())
        self.waited = {}
        self.slots_by_name = {}
        self.pe_last = None
        self.spt = 0
        self.out_tokens = {}
        self.sem_names = {}
        self.declare_io()
        self.alloc()
        self.bank_i = 0
        self.ee = 0

    def new_sem(self):
        self.semid += 1
        return self.nc.alloc_semaphore("s%d" % self.semid)

    def new_slot(self):
        return Slot(self.new_sem())

    def _wait(self, eng, tok):
        sem, val = tok
        if sem is self.PE.sem and val > self.PE.count:
            assert self.pe_last is not None and val == self.PE.count + 1
            self.pe_last.then_inc(self.PE.sem, 1)
            self.PE.count += 1
            self.pe_last = None
        key = (eng.name, id(sem))
        if self.waited.get(key, 0) >= val:
            return
        eng.h.wait_ge(sem, val)
        self.waited[key] = val

    def _deps(self, eng, reads, writes, skip_waw=False):
        toks = []
        for r in reads:
            toks.extend(r.w.values())
            if r.excl:
                toks.extend(r.r.values())
        for w in writes:
            if not skip_waw:
                toks.extend(w.w.values())
            toks.extend(w.r.values())
        for t in toks:
            if eng is self.PE and t[0] is self.PE.sem:
                continue
            self._wait(eng, t)

    def _update(self, tok, reads, writes, accumulate=False):
        k = id(tok[0])
        for w in writes:
            if accumulate:
                if k not in w.w or w.w[k][1] < tok[1]:
                    w.w[k] = tok
            else:
                w.w = {k: tok}
            w.r = {}
        for r in reads:
            if r in writes:
                continue
            if k not in r.r or r.r[k][1] < tok[1]:
                r.r[k] = tok

    def op(self, eng, fn, reads=(), writes=(), signal=True):
        self._deps(eng, reads, writes)
        ins = fn()
        if signal:
            eng.count += 1
            ins.then_inc(eng.sem, 1)
            tok = (eng.sem, eng.count)
            if eng is self.PE:
                self.pe_last = None
        else:
            assert eng is self.PE
            tok = (eng.sem, eng.count + 1)
            self.pe_last = ins
        self._update(tok, reads, writes)
        return tok

    def dma(self, q, out, in_, reads, writes, is_output=False):
        owner = None
        for w in writes:
            if not w.dram:
                owner = w
                break
        if owner is None:
            owner = reads[0]
        key = (owner.name, q.name)
        if key not in self.slots_by_name:
            self.slots_by_name[key] = self.new_slot()
        slot = self.slots_by_name[key]
        self._deps(q, reads, writes, skip_waw=True)
        ins = q.h.dma_start(out=out, in_=in_)
        slot.count += 16
        ins.then_inc(slot.sem, 16)
        tok = (slot.sem, slot.count)
        self._update(tok, reads, writes, accumulate=True)
        if is_output:
            self.out_tokens[id(slot.sem)] = tok
        return tok

    def any_copy(self, out, in_, reads, writes):
        self.ee += 1
        if self.ee % 2:
            return self.op(self.ACT, lambda: self.nc.scalar.activation(out=out, in_=in_, func=AF.Copy), reads, writes)
        return self.op(self.DVE, lambda: self.nc.vector.tensor_copy(out=out, in_=in_), reads, writes)

    def bank(self):
        b = self.bank_i % 8
        self.bank_i += 1
        return b

    def mm(self, out, lhsT, rhs, start, stop, reads, writes, signal=None, **kw):
        if signal is None:
            signal = stop
        return self.op(self.PE, lambda: self.nc.tensor.matmul(out, lhsT, rhs, start=start, stop=stop, **kw),
                       reads, writes, signal=signal)

    def declare_io(self):
        nc = self.nc

        def din(name, shape, dt=F32):
            return nc.dram_tensor(name, list(shape), dt, kind="ExternalInput").ap()

        def dout(name, shape, dt=F32):
            return nc.dram_tensor(name, list(shape), dt, kind="ExternalOutput").ap()

        self.x_prompt = din("x_prompt", [self.SEQR, D_MODEL])
        self.memT = din("memT", [D_MODEL, N_MEM])
        self.rel_bias = din("rel_bias", [N_BUCKETS, N_ATT_HEADS])
        self.ohrev = din("ohrev", [4, N_BUCKETS, 384])
        self.ident_in = din("ident", [128, 128])
        self.g_pre = din("norm_pre_g", [self.depth, D_MODEL])
        self.g_post = din("norm_post_g", [self.depth, D_MODEL])
        self.w_in = din("w_in", [self.depth, D_MODEL, IN_W])
        self.w_out = din("w_out", [self.depth, MIX_W, D_MODEL])
        self.w_pw2 = din("w_pw2", [self.depth, CONV_CH, CONV_CH])
        self.w_mem_kv = din("w_mem_kv", [self.depth, D_MODEL, 2 * X_W])
        self.wdwT = din("wdwT", [128, DEPTH * 4 * CONV_K])
        self.cpar = din("cpar", [128, 3 * DEPTH * 4])
        self.y_prompt = dout("y_prompt", [self.SEQR, D_MODEL])
        self.akp = dout("akp", [self.depth, WIN, ATT_W])
        self.avp = dout("avp", [self.depth, WIN, ATT_W])
        self.cvp = dout("cvp", [self.depth, CONV_K - 1, CONV_CH])
        self.mkp = dout("mkp", [self.depth, N_MEM, X_W])
        self.mvp = dout("mvp", [self.depth, N_MEM, X_W])
        if self.with_sample:
            self.x_sample = din("x_sample", [NS, D_MODEL])
            self.cache_k = din("cache_k", [self.depth, L_CACHE, ATT_W])
            self.cache_v = din("cache_v", [self.depth, L_CACHE, ATT_W])
            self.state_conv = din("state_convT", [128, DEPTH * 4 * (CONV_K - 1)])
            self.cmk = din("cache_mem_k", [self.depth, N_MEM, X_W])
            self.cmv = din("cache_mem_v", [self.depth, N_MEM, X_W])
            self.ohs = din("ohs", [N_BUCKETS, 2064])
            self.y_sample = dout("y_sample", [NS, D_MODEL])
            self.aks = dout("aks", [self.depth, L_CACHE, ATT_W])
            self.avs = dout("avs", [self.depth, L_CACHE, ATT_W])
            self.cvs = dout("cvs", [self.depth, CONV_K - 1, CONV_CH])
        self.KTd = nc.dram_tensor("KTd", [DEPTH, N_ATT_HEADS, 128, SEQ], BF16).ap()
        self.Vd = nc.dram_tensor("Vd", [DEPTH, SEQ, ATT_W], BF16).ap()
        self.Rd = nc.dram_tensor("Rd", [4, N_ATT_HEADS, 384], F32).ap()
        self.Ed = nc.dram_tensor("Ed", [N_ATT_HEADS, 128, 6, 128], F32).ap()
        self.Fd = nc.dram_tensor("Fd", [N_ATT_HEADS, 2064], F32).ap()
        self.r_Fd = Res("Fd", dram=True)
        self.MKd = nc.dram_tensor("MKd", [DEPTH, 128, N_X_HEADS * N_MEM], BF16).ap()
        self.MVd = nc.dram_tensor("MVd", [DEPTH, 128, 2 * X_W], BF16).ap()
        self.r_KTd = [[Res("KTd%d_%d" % (l, c), dram=True) for c in range(NCH)] for l in range(DEPTH)]
        self.r_Vd = [[Res("Vd%d_%d" % (l, c), dram=True) for c in range(NCH)] for l in range(DEPTH)]
        self.r_Rd, self.r_Ed = Res("Rd", dram=True), Res("Ed", dram=True)
        self.r_MKd = [Res("MKd%d" % l, dram=True) for l in range(DEPTH)]
        self.r_MVd = [Res("MVd%d" % l, dram=True) for l in range(DEPTH)]

    def alloc(self):
        nc = self.nc
        base = (nc.sbuf_base + 63) // 64 * 64
        top = nc.sbuf_top
        self._arena = nc.alloc_sbuf_tensor("arena", [128, top - base - 64], mybir.dt.uint8)
        self._off = base
        self._top = top - 64
        self.res = {}

        def A(name, shape, dt, nres=None):
            nbytes = int(np.prod(shape[1:])) * (4 if dt == F32 else 2)
            nbytes = (nbytes + 63) // 64 * 64
            t = nc.alloc_sbuf_tensor_at(name, list(shape), dt, offset=self._off)
            if getattr(self, "_pend_lo", None) is None:
                self._pend_lo = self._off
            self._off += nbytes
            assert self._off <= self._top, "SBUF overflow at %s: %d > %d" % (name, self._off, self._top)
            return t

        self.A = A
        NT = T + NS
        self.X = A("X", [128, 4, D_MODEL], F32)
        self.rX = [Res("X%d" % b) for b in range(4)]
        self.MIX = A("MIX", [128, 16, NT], BF16)
        self.rMIX = [Res("MIX%d" % f) for f in range(16)]
        self.QT = A("QT", [128, 8, NT], BF16)
        self.rQT = [Res("QT%d" % h) for h in range(8)]
        self.QMT = A("QMT", [128, 4, NT], BF16)
        self.rQMT = [Res("QMT%d" % h) for h in range(4)]
        self.KTown = A("KTown", [128, 8, T], BF16)
        self.rKTown = [Res("KTown%d" % g) for g in range(2)]
        self.NW = 2
        self.W = [A("W%d" % i, [128, 16, 512], BF16) for i in range(self.NW)]
        self.rW = [Res("W%d" % i) for i in range(self.NW)]
        self.rMEMK, self.rMEMV = Res("MEMK"), Res("MEMV")
        self.wi = 0
        self.MEMK = A("MEMK", [128, 4, N_MEM], BF16)
        self.MEMV = A("MEMV", [128, 2, X_W], BF16)
        self.UH = A("UH", [128, DEPTH, 4, CONV_K - 1], F32)
        self.rUH = [Res("UH%d" % l) for l in range(DEPTH)]
        self.WDW = A("WDW", [128, DEPTH, 4, CONV_K], F32)
        self.CPAR = A("CPAR", [128, 3, DEPTH, 4], F32)
        self.rPAR = Res("PAR")
        self.IDB = A("IDB", [128, 128], BF16)
        self.IDF = A("IDF", [128, 128], F32)
        self.ONB = A("ONB", [128, 128], BF16)
        self.ONF = A("ONF", [128, 128], F32)
        self.rCONST = Res("CONST")
        self.rIDF, self.rIDB = Res("IDF"), Res("IDB")
        self.SS = A("SS", [128, 16], F32)
        self.rSS = [Res("SS%d" % b) for b in range(4)]
        if self.with_sample:
            self.XS = A("XS", [NS, D_MODEL], F32)
            self.KVS = A("KVS", [NS, 2 * ATT_W], F32)
            self.ES = A("ES", [128, N_ATT_HEADS, 17, NS], F32)
            self.SCS = A("SCS", [128, DEPTH, 4, CONV_K - 1], F32)
            self.rXS, self.rKVS, self.rES, self.rSCS = Res("XS"), Res("KVS"), Res("ES"), Res("SCS")
            self.rCPY = [Res("CPY%d" % i, dram=True) for i in range(2 * DEPTH)]
        self._persist_end = self._off
        self.PS = [nc.alloc_psum_tensor("ps%d" % i, [128, 512], F32) for i in range(8)]
        self.PSB = [p.bitcast(BF16) for p in self.PS]
        self.rPS = [Res("PS%d" % i, excl=True) for i in range(8)]

    def stage(self, name):
        if self.stop_after == name:
            raise _Stop()

    def overlay(self):
        old_dead = getattr(self, "dead", [])
        new_dead = []
        for r, (lo, hi) in getattr(self, "region_res", []):
            toks = {}
            for d in (r.w, r.r):
                for k, tok in d.items():
                    if k not in toks or toks[k][1] < tok[1]:
                        toks[k] = tok
            new_dead.append((lo, hi, toks))
        dead = [e for e in old_dead if not any(lo <= e[0] and e[1] <= hi for (lo, hi, _) in new_dead)] + new_dead
        self.dead = dead
        self.region_res = []
        self._off = self._persist_end
        self._pend_lo = None
        self._last_range = None

    def LR(self, name):
        r = Res(name)
        if self._pend_lo is not None:
            self._last_range = (self._pend_lo, self._off)
            self._pend_lo = None
        lo, hi = self._last_range
        for (a, b, toks) in self.dead:
            if a < hi and lo < b:
                for k, tok in toks.items():
                    if k not in r.r or r.r[k][1] < tok[1]:
                        r.r[k] = tok
        self.region_res.append((r, (lo, hi)))
        return r

    def wload(self, src_ap, kdim=16, ncols=512):
        i = self.wi % self.NW
        self.wi += 1
        self.dma(self.POOL, self.W[i][:, 0:kdim, 0:ncols], src_ap, [], [self.rW[i]])
        return self.W[i], self.rW[i]

    def setup(self):
        nc = self.nc
        P = self
        self.overlay()
        A = self.A
        P.dma(P.SP, P.IDF[:], P.ident_in, [], [P.rIDF])
        P.dma(P.POOL, P.IDB[:], P.ident_in, [], [P.rIDB])
        P.op(P.POOL, lambda: nc.gpsimd.memset(P.ONB[:], 1.0), [], [P.rCONST])
        P.op(P.POOL, lambda: nc.gpsimd.memset(P.ONF[:], 1.0), [], [P.rCONST])
        P.dma(P.SP, P.WDW[:].rearrange("p l c k -> p (l c k)"), P.wdwT, [], [P.rPAR])
        P.dma(P.SP, P.CPAR[:].rearrange("p w l c -> p (w l c)"), P.cpar, [], [P.rPAR])
        for l in range(DEPTH):
            P.op(P.POOL, lambda l=l: nc.gpsimd.memset(P.UH[:, l], 0.0), [], [P.rUH[l]])
        P.stage('s_const')
        RB = A("RB", [32, 8], F32)
        EB = A("EB", [32, 8], F32)
        OH = A("OH", [32, 4, 384], F32)
        RS = A("RS", [8, 4, 384], F32)
        HK = A("HK", [128, 8, 128], F32)
        EV = A("EV", [128, 8, 128], F32)
        rRB, rOH, rRS, rHK, rEV = P.LR("RB"), P.LR("OH"), P.LR("RS"), P.LR("HK"), P.LR("EV")
        P.dma(P.SP, RB[:], P.rel_bias, [], [rRB])
        P.dma(P.SP, OH[:], P.ohrev.rearrange("p b n -> b p n"), [], [rOH])
        P.op(P.ACT, lambda: nc.scalar.activation(out=EB[:], in_=RB[:], func=AF.Exp), [rRB], [rRB])
        for p in range(4):
            b = P.bank()
            P.mm(P.PS[b][0:8, 0:384], EB[:], OH[:, p, :], True, True, [rRB, rOH], [P.rPS[b]])
            P.op(P.DVE, lambda p=p, b=b: nc.vector.tensor_copy(out=RS[:, p, :], in_=P.PS[b][0:8, 0:384]),
                 [P.rPS[b]], [rRS])
        P.stage('s_mm')
        P.dma(P.SP, P.Rd.rearrange("p h n -> h p n"), RS[:], [rRS], [P.r_Rd])
        P.stage('s_rd')
        for p in range(3):
            for role in range(2):
                a = 128 if role == 0 else 0
                row = 3 if (p == 1 and role == 0) else p
                src = bass.AP(tensor=P.Rd.tensor, offset=row * 8 * 384 + a, ap=[[1, 128], [384, 8], [1, 128]])
                P.dma(P.SP, HK[:], src, [P.r_Rd], [rHK])
                P.op(P.POOL, lambda: nc.gpsimd.tensor_copy(out=EV[:], in_=HK[:, :, ::-1]), [rHK], [rEV])
                P.dma(P.SP, P.Ed[:, :, 2 * p + role, :].rearrange("h k i -> k h i"), EV[:], [rEV], [P.r_Ed])
        if self.with_sample:
            self.setup_sample(EB, rRB)
        P.stage('s_etab')
        MT = A("MT", [128, 16, N_MEM], BF16)
        rMT = P.LR("MT")
        MKs = A("MKs", [128, 4, N_MEM], BF16)
        MVs = A("MVs", [128, 2, X_W], BF16)
        MF = [A("MF%d" % i, [128, 512], F32) for i in range(2)]
        rMKs, rMVs, rMF = P.LR("MKs"), P.LR("MVs"), [P.LR("MF0"), P.LR("MF1")]
        P.dma(P.POOL, MT[:], P.memT.rearrange("(k p) m -> p k m", p=128), [], [rMT])
        mfi = 0
        for l in range(self.depth):
            wk, rwk = P.wload(P.w_mem_kv[l].rearrange("(k p) n -> p k n", p=128)[:, :, 0:512])
            for h in range(4):
                b = P.bank()
                for k in range(16):
                    P.mm(P.PS[b][:, 0:N_MEM], wk[:, k, h * 128:(h + 1) * 128], MT[:, k, :], k == 0, k == 15,
                         [rwk, rMT], [P.rPS[b]])
                P.any_copy(MKs[:, h, :], P.PS[b][:, 0:N_MEM], [P.rPS[b]], [rMKs])
            for mb in range(2):
                b = P.bank()
                for k in range(16):
                    P.mm(P.PS[b][:], MT[:, k, mb * 128:(mb + 1) * 128], wk[:, k, :], k == 0, k == 15,
                         [rwk, rMT], [P.rPS[b]])
                j = mfi % 2
                mfi += 1
                P.any_copy(MF[j][:], P.PS[b][:], [P.rPS[b]], [rMF[j]])
                P.dma(P.SP, P.mkp[l, mb * 128:(mb + 1) * 128, :], MF[j][:], [rMF[j]], [], is_output=True)
            wv, rwv = P.wload(P.w_mem_kv[l].rearrange("(k p) n -> p k n", p=128)[:, :, 512:1024])
            for mb in range(2):
                b = P.bank()
                for k in range(16):
                    P.mm(P.PS[b][:], MT[:, k, mb * 128:(mb + 1) * 128], wv[:, k, :], k == 0, k == 15,
                         [rwv, rMT], [P.rPS[b]])
                j = mfi % 2
                mfi += 1
                P.op(P.ACT, lambda b=b, j=j: nc.scalar.activation(out=MF[j][:], in_=P.PS[b][:], func=AF.Copy),
                     [P.rPS[b]], [rMF[j]])
                P.op(P.DVE, lambda b=b, mb=mb: nc.vector.tensor_copy(out=MVs[:, mb, :], in_=P.PS[b][:]),
                     [P.rPS[b]], [rMVs])
                P.dma(P.SP, P.mvp[l, mb * 128:(mb + 1) * 128, :], MF[j][:], [rMF[j]], [], is_output=True)
            P.dma(P.SP, P.MKd[l], MKs[:].rearrange("p h m -> p (h m)"), [rMKs], [P.r_MKd[l]])
            P.dma(P.SP, P.MVd[l], MVs[:].rearrange("p b n -> p (b n)"), [rMVs], [P.r_MVd[l]])


    def setup_sample(self, EB, rEB):
        nc = self.nc
        P = self
        A = self.A
        P.dma(P.SP, P.XS[:], P.x_sample, [], [P.rXS])
        P.dma(P.SP, P.SCS[:].rearrange("p l c k -> p (l c k)"), P.state_conv, [], [P.rSCS])
        for l in range(self.depth):
            P.dma(P.SP, P.aks[l, 0:L_CACHE - NS, :], P.cache_k[l, NS:L_CACHE, :], [P.rCPY[2 * l]], [], is_output=True)
            P.dma(P.SP, P.avs[l, 0:L_CACHE - NS, :], P.cache_v[l, NS:L_CACHE, :], [P.rCPY[2 * l + 1]], [], is_output=True)
        OHS = A("OHS", [32, 2064], F32)
        RSS = A("RSS", [8, 2064], F32)
        HKS = A("HKS", [128, 17, NS], F32)
        rOHS, rRSS, rHKS = P.LR("OHS"), P.LR("RSS"), P.LR("HKS")
        P.dma(P.SP, OHS[:], P.ohs, [], [rOHS])
        for j in range(5):
            n0, n1 = j * 512, min(2064, (j + 1) * 512)
            b = P.bank()
            P.mm(P.PS[b][0:8, 0:n1 - n0], EB[:], OHS[:, n0:n1], True, True, [rEB, rOHS], [P.rPS[b]])
            P.op(P.DVE, lambda b=b, n0=n0, n1=n1: nc.vector.tensor_copy(out=RSS[:, n0:n1], in_=P.PS[b][0:8, 0:n1 - n0]),
                 [P.rPS[b]], [rRSS])
        P.dma(P.SP, P.Fd, RSS[:], [rRSS], [P.r_Fd])
        for h in range(N_ATT_HEADS):
            src = bass.AP(tensor=P.Fd.tensor, offset=h * 2064, ap=[[1, 128], [128, 16], [1, NS]])
            P.dma(P.SP, HKS[:, 0:16, :], src, [P.r_Fd], [rHKS])
            src2 = bass.AP(tensor=P.Fd.tensor, offset=h * 2064 + 2048, ap=[[1, NS], [1, NS]])
            P.dma(P.SP, HKS[0:NS, 16, :], src2, [P.r_Fd], [rHKS])
            P.op(P.POOL, lambda h=h: nc.gpsimd.tensor_copy(out=P.ES[:, h, 0:16, :], in_=HKS[:, 0:16, ::-1]),
                 [rHKS], [P.rES])
            P.op(P.POOL, lambda h=h: nc.gpsimd.tensor_copy(out=P.ES[0:NS, h, 16, :], in_=HKS[0:NS, 16, ::-1]),
                 [rHKS], [P.rES])

    def chunk_layer(self, c, l):
        nc = self.nc
        P = self
        t0 = c * T
        last = (l == self.depth - 1)
        self.overlay()
        A = self.A
        HT = A("HT", [128, 16, T + NS], BF16)
        rHT = [P.LR("HT%d" % b) for b in range(5)]
        GT = A("GT", [128, D_MODEL], F32)
        rGT = P.LR("GT")
        HB = [A("HB%d" % i, [128, D_MODEL], BF16) for i in range(2)]
        rHB = [P.LR("HB0"), P.LR("HB1")]
        JUNK = A("JUNK", [128, D_MODEL], BF16)
        rJUNK = P.LR("JUNK")
        STF = [A("STF%d" % i, [128, 512], F32) for i in range(2)]
        rSTF = [P.LR("STF0"), P.LR("STF1")]
        STB = [A("STB%d" % i, [128, 512], BF16) for i in range(2)]
        rSTB = [P.LR("STB0"), P.LR("STB1")]
        UT = A("UT", [128, 4, T + CONV_K - 1], F32)
        rUT = [P.LR("UT%d" % i) for i in range(4)]
        SG = [A("SG%d" % i, [128, T], F32) for i in range(2)]
        rSG = [P.LR("SG0"), P.LR("SG1")]
        CC = A("CC", [128, 4, T], F32)
        rCC = [P.LR("CC%d" % i) for i in range(4)]
        LNM = A("LNM", [128, T], F32)
        LNR = A("LNR", [128, T], F32)
        rLNM, rLNR = P.LR("LNM"), P.LR("LNR")
        CNT = A("CNT", [128, 4, T], BF16)
        rCNT = [P.LR("CNT%d" % i) for i in range(4)]
        samp = self.with_sample and c == 0
        if samp:
            UTS = A("UTS", [128, 4, NS + CONV_K - 1], F32)
            rUTS = [P.LR("UTS%d" % i) for i in range(4)]
            HBS = A("HBS", [NS, D_MODEL], BF16)
            rHBS = P.LR("HBS")

        P.dma(P.SP, GT[:], bass.AP(tensor=P.g_pre.tensor, offset=l * D_MODEL, ap=[[0, 128], [1, D_MODEL]]),
              [], [rGT])
        for blk in range(4):
            P.op(P.ACT, lambda blk=blk: nc.scalar.activation(out=JUNK[:], in_=P.X[:, blk, :], func=AF.Square,
                                                             accum_out=P.SS[:, blk:blk + 1]),
                 [P.rX[blk]], [rJUNK, P.rSS[blk]])
            self.rstd(P.SS[:, blk:blk + 1], P.rSS[blk], 1.0 / D_MODEL)
            hb = blk % 2
            P.op(P.DVE, lambda blk=blk, hb=hb: nc.vector.scalar_tensor_tensor(
                out=HB[hb][:], in0=P.X[:, blk, :], scalar=P.SS[:, blk:blk + 1], in1=GT[:], op0=AL.mult, op1=AL.mult),
                [P.rX[blk], P.rSS[blk], rGT], [rHB[hb]])
            for dg in range(4):
                b = P.bank()
                for j in range(4):
                    dm = dg * 4 + j
                    P.op(P.PE, lambda dm=dm, j=j, b=b, hb=hb: nc.tensor.transpose(
                        P.PSB[b][:, j * 128:(j + 1) * 128], HB[hb][:, dm * 128:(dm + 1) * 128], P.IDB[:]),
                        [rHB[hb], P.rIDB], [P.rPS[b]], signal=(j == 3))
                P.any_copy(HT[:, dg * 4:dg * 4 + 4, blk * 128:(blk + 1) * 128],
                           P.PSB[b][:, 0:512].rearrange("p (j t) -> p j t", j=4), [P.rPS[b]], [rHT[blk]])

        if samp:
            P.op(P.ACT, lambda: nc.scalar.activation(out=JUNK[0:NS, :], in_=P.XS[:], func=AF.Square,
                                                     accum_out=P.SS[0:NS, 8:9]), [P.rXS], [rJUNK, P.rSS[0]])
            self.rstd(P.SS[0:NS, 8:9], P.rSS[0], 1.0 / D_MODEL)
            P.op(P.DVE, lambda: nc.vector.scalar_tensor_tensor(
                out=HBS[:], in0=P.XS[:], scalar=P.SS[0:NS, 8:9], in1=GT[0:NS, :], op0=AL.mult, op1=AL.mult),
                [P.rXS, P.rSS[0], rGT], [rHBS])
            b = P.bank()
            for dm in range(16):
                P.op(P.PE, lambda dm=dm, b=b: nc.tensor.transpose(
                    P.PSB[b][:, dm * NS:(dm + 1) * NS], HBS[:, dm * 128:(dm + 1) * 128], P.IDB[0:NS, 0:NS]),
                    [rHBS, P.rIDB], [P.rPS[b]], signal=(dm == 15))
            P.any_copy(HT[:, :, T:T + NS], P.PSB[b][:, 0:16 * NS].rearrange("p (k t) -> p k t", k=16),
                       [P.rPS[b]], [rHT[4]])
        P.stage('A')
        def wsrc(col0):
            return P.w_in[l].rearrange("(k p) n -> p k n", p=128)[:, :, col0:col0 + 512]

        stg = 0
        for gi, col0 in enumerate((C_K, C_K + 512, C_V, C_V + 512)):
            w, rw = P.wload(wsrc(col0))
            isK = gi < 2
            for blk in range(4):
                b = P.bank()
                for k in range(16):
                    P.mm(P.PS[b][:], HT[:, k, blk * 128:(blk + 1) * 128], w[:, k, :], k == 0, k == 15,
                         [rw, rHT[blk]], [P.rPS[b]])
                j = stg % 2
                stg += 1
                P.op(P.ACT, lambda b=b, j=j: nc.scalar.activation(out=STB[j][:], in_=P.PS[b][:], func=AF.Copy),
                     [P.rPS[b]], [rSTB[j]])
                seq_run = self.n_chunks * T
                lp = min(WIN, seq_run)
                if t0 >= seq_run - lp:
                    row0 = t0 - (seq_run - lp) + blk * 128
                    dst = (P.akp if isK else P.avp)[l, row0:row0 + 128, (gi % 2) * 512:(gi % 2) * 512 + 512]
                    P.op(P.DVE, lambda b=b, j=j: nc.vector.tensor_copy(out=STF[j][:], in_=P.PS[b][:]),
                         [P.rPS[b]], [rSTF[j]])
                    P.dma(P.SP, dst, STF[j][:], [rSTF[j]], [], is_output=True)
                if isK:
                    b2 = P.bank()
                    for jj in range(4):
                        P.op(P.PE, lambda jj=jj, b2=b2, j=j: nc.tensor.transpose(
                            P.PSB[b2][:, jj * 128:(jj + 1) * 128], STB[j][:, jj * 128:(jj + 1) * 128], P.IDB[:]),
                            [rSTB[j], P.rIDB], [P.rPS[b2]], signal=(jj == 3))
                    P.any_copy(P.KTown[:, gi * 4:gi * 4 + 4, blk * 128:(blk + 1) * 128],
                               P.PSB[b2][:, 0:512].rearrange("p (j t) -> p j t", j=4), [P.rPS[b2]], [P.rKTown[gi]])
                else:
                    vc = (gi - 2) * 512
                    P.dma(P.SP, P.Vd[l, t0 + blk * 128:t0 + (blk + 1) * 128, vc:vc + 512], STB[j][:],
                          [rSTB[j]], [P.r_Vd[l][c]])
            if samp:
                b = P.bank()
                for k in range(16):
                    P.mm(P.PS[b][0:NS, :], HT[:, k, T:T + NS], w[:, k, :], k == 0, k == 15, [rw, rHT[4]], [P.rPS[b]])
                P.any_copy(P.KVS[:, gi * 512:(gi + 1) * 512], P.PS[b][0:NS, :], [P.rPS[b]], [P.rKVS])
        P.dma(P.SP, P.KTd[l, :, :, t0:t0 + T].rearrange("h d t -> d h t"), P.KTown[:],
              [P.rKTown[0], P.rKTown[1]], [P.r_KTd[l][c]])

        if samp:
            P.dma(P.SP, P.aks[l, L_CACHE - NS:L_CACHE, :], P.KVS[:, 0:ATT_W], [P.rKVS], [], is_output=True)
            P.dma(P.SP, P.avs[l, L_CACHE - NS:L_CACHE, :], P.KVS[:, ATT_W:2 * ATT_W], [P.rKVS], [], is_output=True)
        P.stage('Bkv')

        def fm_group(col0, evac):
            w, rw = P.wload(wsrc(col0))
            for cc in range(4):
                b = P.bank()
                for k in range(16):
                    P.mm(P.PS[b][:], w[:, k, cc * 128:(cc + 1) * 128], HT[:, k, 0:T], k == 0, k == 15,
                         [rw] + rHT[0:4], [P.rPS[b]])
                evac(cc, P.PS[b][:], P.rPS[b], False)
                if samp:
                    b = P.bank()
                    for k in range(16):
                        P.mm(P.PS[b][:, 0:NS], w[:, k, cc * 128:(cc + 1) * 128], HT[:, k, T:T + NS], k == 0, k == 15,
                             [rw, rHT[4]], [P.rPS[b]])
                    evac(cc, P.PS[b][:, 0:NS], P.rPS[b], True)

        def ucols(sm):
            return (UTS, rUTS, NS) if sm else (UT, rUT, T)

        def mcols(sm):
            return slice(T, T + NS) if sm else slice(0, T)

        def ev_uv(cc, ps, rps, sm):
            u, ru, n = ucols(sm)
            P.op(P.DVE, lambda: nc.vector.tensor_copy(out=u[:, cc, CONV_K - 1:], in_=ps), [rps], [ru[cc]])

        def ev_ug(cc, ps, rps, sm):
            u, ru, n = ucols(sm)
            j = cc % 2
            P.op(P.ACT, lambda: nc.scalar.activation(out=SG[j][:, 0:n], in_=ps, func=AF.Sigmoid), [rps], [rSG[j]])
            P.op(P.DVE, lambda: nc.vector.tensor_tensor(out=u[:, cc, CONV_K - 1:], in0=u[:, cc, CONV_K - 1:],
                                                        in1=SG[j][:, 0:n], op=AL.mult), [rSG[j], ru[cc]], [ru[cc]])

        def ev_silu(fbase):
            def f(cc, ps, rps, sm):
                P.op(P.ACT, lambda: nc.scalar.activation(out=P.MIX[:, fbase + cc, mcols(sm)], in_=ps, func=AF.Silu),
                     [rps], [P.rMIX[fbase + cc]])
            return f

        def ev_copy(dst, rdst, hbase):
            def f(cc, ps, rps, sm):
                P.op(P.ACT, lambda: nc.scalar.activation(out=dst[:, hbase + cc, mcols(sm)], in_=ps, func=AF.Copy),
                     [rps], [rdst[hbase + cc]])
            return f

        for cc in range(4):
            P.op(P.POOL, lambda cc=cc: nc.gpsimd.tensor_copy(out=UT[:, cc, 0:CONV_K - 1], in_=P.UH[:, l, cc, :]),
                 [P.rUH[l]], [rUT[cc]])
        if samp:
            for cc in range(4):
                P.op(P.POOL, lambda cc=cc: nc.gpsimd.tensor_copy(out=UTS[:, cc, 0:CONV_K - 1], in_=P.SCS[:, l, cc, :]),
                     [P.rSCS], [rUTS[cc]])
        fm_group(C_UV, ev_uv)
        fm_group(C_UG, ev_ug)
        fm_group(C_GC, ev_silu(8))

        P.stage('Bconv')
        def conv_branch(u, ru, n, mc, hist_out, cv_out, part):
            if part == 2:
                for co in range(4):
                    b = P.bank()
                    for ci in range(4):
                        P.mm(P.PS[b][:, 0:n], wpw[:, ci, co * 128:(co + 1) * 128], CNT[:, ci, 0:n], ci == 0, ci == 3,
                             [rwpw, rCNT[ci]], [P.rPS[b]])
                    P.op(P.DVE, lambda co=co, b=b: nc.vector.tensor_tensor(out=P.MIX[:, 8 + co, mc],
                                                                           in0=P.PS[b][:, 0:n],
                                                                           in1=P.MIX[:, 8 + co, mc], op=AL.mult),
                         [P.rPS[b], P.rMIX[8 + co]], [P.rMIX[8 + co]])
                return
            for cc in range(4):
                P.op(P.DVE, lambda cc=cc: nc.vector.tensor_scalar(
                    out=CC[:, cc, 0:n], in0=u[:, cc, 0:n], scalar1=P.WDW[:, l, cc, 0:1],
                    scalar2=P.CPAR[:, 0, l, cc:cc + 1], op0=AL.mult, op1=AL.add), [ru[cc], P.rPAR], [rCC[cc]])
            for k in range(1, CONV_K):
                for cc in range(4):
                    P.op(P.DVE, lambda cc=cc, k=k: nc.vector.scalar_tensor_tensor(
                        out=CC[:, cc, 0:n], in0=u[:, cc, k:k + n], scalar=P.WDW[:, l, cc, k:k + 1], in1=CC[:, cc, 0:n],
                        op0=AL.mult, op1=AL.add), [ru[cc], P.rPAR, rCC[cc]], [rCC[cc]])
            if hist_out:
                for cc in range(4):
                    P.op(P.POOL, lambda cc=cc: nc.gpsimd.tensor_copy(out=P.UH[:, l, cc, :],
                                                                     in_=u[:, cc, n:n + CONV_K - 1]),
                         [ru[cc]], [P.rUH[l]])
            if cv_out is not None:
                b = P.bank()
                for cc in range(4):
                    P.op(P.PE, lambda cc=cc, b=b: nc.tensor.transpose(
                        P.PS[b][0:CONV_K - 1, cc * 128:(cc + 1) * 128], u[:, cc, n:n + CONV_K - 1], P.IDF[:]),
                        [ru[cc], P.rIDF], [P.rPS[b]], signal=(cc == 3))
                P.op(P.DVE, lambda b=b: nc.vector.tensor_copy(out=STF[0][0:CONV_K - 1, :],
                                                              in_=P.PS[b][0:CONV_K - 1, :]), [P.rPS[b]], [rSTF[0]])
                P.dma(P.SP, cv_out, STF[0][0:CONV_K - 1, :], [rSTF[0]], [], is_output=True)
            b1, b2 = P.bank(), P.bank()
            for cc in range(4):
                P.mm(P.PS[b1][:, 0:n], P.ONF[:], CC[:, cc, 0:n], cc == 0, cc == 3, [rCC[cc], P.rCONST], [P.rPS[b1]])
            for cc in range(4):
                j = cc % 2
                P.op(P.ACT, lambda cc=cc, j=j: nc.scalar.activation(out=SG[j][:, 0:n], in_=CC[:, cc, 0:n],
                                                                    func=AF.Square), [rCC[cc]], [rSG[j]])
                P.mm(P.PS[b2][:, 0:n], P.ONF[:], SG[j][:, 0:n], cc == 0, cc == 3, [rSG[j], P.rCONST], [P.rPS[b2]])
            P.op(P.DVE, lambda: nc.vector.tensor_scalar(out=LNM[:, 0:n], in0=P.PS[b1][:, 0:n], scalar1=1.0 / CONV_CH,
                                                        scalar2=None, op0=AL.mult), [P.rPS[b1]], [rLNM])
            P.op(P.DVE, lambda: nc.vector.tensor_tensor(out=LNR[:, 0:n], in0=LNM[:, 0:n], in1=LNM[:, 0:n], op=AL.mult),
                 [rLNM], [rLNR])
            P.op(P.DVE, lambda: nc.vector.scalar_tensor_tensor(out=LNR[:, 0:n], in0=P.PS[b2][:, 0:n],
                                                               scalar=1.0 / CONV_CH, in1=LNR[:, 0:n], op0=AL.mult,
                                                               op1=AL.subtract), [P.rPS[b2], rLNR], [rLNR])
            P.op(P.DVE, lambda: nc.vector.tensor_scalar(out=LNR[:, 0:n], in0=LNR[:, 0:n], scalar1=EPS, scalar2=None,
                                                        op0=AL.add), [rLNR], [rLNR])
            P.op(P.ACT, lambda: nc.scalar.activation(out=LNR[:, 0:n], in_=LNR[:, 0:n], func=AF.Sqrt), [rLNR], [rLNR])
            P.op(P.DVE, lambda: nc.vector.reciprocal(out=LNR[:, 0:n], in_=LNR[:, 0:n]), [rLNR], [rLNR])
            for cc in range(4):
                j = cc % 2
                P.op(P.DVE, lambda cc=cc, j=j: nc.vector.tensor_tensor(out=SG[j][:, 0:n], in0=CC[:, cc, 0:n],
                                                                       in1=LNM[:, 0:n], op=AL.subtract),
                     [rCC[cc], rLNM], [rSG[j]])
                P.op(P.DVE, lambda j=j: nc.vector.tensor_tensor(out=SG[j][:, 0:n], in0=SG[j][:, 0:n], in1=LNR[:, 0:n],
                                                                op=AL.mult), [rSG[j], rLNR], [rSG[j]])
                P.op(P.ACT, lambda cc=cc, j=j: nc.scalar.activation(
                    out=CNT[:, cc, 0:n], in_=SG[j][:, 0:n], func=AF.Silu, scale=P.CPAR[:, 1, l, cc:cc + 1],
                    bias=P.CPAR[:, 2, l, cc:cc + 1]), [rSG[j], P.rPAR], [rCNT[cc]])

        P.stage('C')
        fm_group(C_QM, ev_copy(P.QMT, P.rQMT, 0))
        fm_group(C_GM, ev_silu(12))
        fm_group(C_GA, ev_silu(0))
        fm_group(C_GA + 512, ev_silu(4))
        fm_group(C_Q, ev_copy(P.QT, P.rQT, 0))
        conv_branch(UT, rUT, T, slice(0, T), True, P.cvp[l] if c == self.n_chunks - 1 else None, 1)
        fm_group(C_Q + 512, ev_copy(P.QT, P.rQT, 4))
        wpw, rwpw = P.wload(P.w_pw2[l].rearrange("(k p) n -> p k n", p=128), kdim=4)
        conv_branch(UT, rUT, T, slice(0, T), True, None, 2)
        if samp:
            conv_branch(UTS, rUTS, NS, slice(T, T + NS), False, P.cvs[l], 1)
            conv_branch(UTS, rUTS, NS, slice(T, T + NS), False, None, 2)


        P.stage('Brest')
        self.overlay()
        EX = [A("EX%d" % i, [128, 512], F32) for i in range(2)]
        rEX = [P.LR("EX0"), P.LR("EX1")]
        NPT = 12
        PT = [A("PT%d" % i, [128, 512], BF16) for i in range(NPT)]
        rPT = [P.LR("PT%d" % i) for i in range(NPT)]
        LT = A("LT", [128, T], F32)
        RD = A("RD", [128, T], F32)
        AT = A("AT", [128, T], F32)
        rLT, rRD, rAT = P.LR("LT"), P.LR("RD"), P.LR("AT")
        ET = [A("ET%d" % i, [128, 6, 128], F32) for i in range(2)]
        rET = [P.LR("ET0"), P.LR("ET1")]
        E3S = [A("E3S%d" % i, [128, 32], F32) for i in range(2)]
        rE3S = [P.LR("E3S0"), P.LR("E3S1")]
        KH = [A("KH%d" % i, [128, WIN], BF16) for i in range(2)]
        rKH = [P.LR("KH0"), P.LR("KH1")]
        VB = [A("VB%d" % i, [128, 29, 128], BF16) for i in range(2)]
        rVB = [P.LR("VB0"), P.LR("VB1")]
        V3 = [A("V3%d" % i, [32, 16, 128], BF16) for i in range(2)]
        rV3 = [P.LR("V30"), P.LR("V31")]
        pti = [0]

        def next_pt():
            i = pti[0] % NPT
            pti[0] += 1
            return i

        def finish_head(f, bacc, bden, n=T, mc=slice(0, T)):
            P.op(P.ACT, lambda: nc.scalar.activation(out=LT[:, 0:n], in_=P.PS[bden][:, 0:n], func=AF.Ln),
                 [P.rPS[bden]], [rLT])
            P.op(P.ACT, lambda: nc.scalar.activation(out=RD[:, 0:n], in_=LT[:, 0:n], func=AF.Exp, scale=-1.0),
                 [rLT], [rRD])
            P.op(P.DVE, lambda: nc.vector.tensor_tensor(out=AT[:, 0:n], in0=P.PS[bacc][:, 0:n], in1=RD[:, 0:n],
                                                        op=AL.mult), [P.rPS[bacc], rRD], [rAT])
            P.op(P.DVE, lambda: nc.vector.tensor_tensor(out=P.MIX[:, f, mc], in0=AT[:, 0:n], in1=P.MIX[:, f, mc],
                                                        op=AL.mult), [rAT, P.rMIX[f]], [P.rMIX[f]])

        P.dma(P.SP, P.MEMK[:].rearrange("p h m -> p (h m)"), P.MKd[l], [P.r_MKd[l]], [P.rMEMK])
        P.dma(P.SP, P.MEMV[:].rearrange("p b n -> p (b n)"), P.MVd[l], [P.r_MVd[l]], [P.rMEMV])
        def mem_attn(kt, rkt, vt, rvt, n, mc):
            for h in range(4):
                pts = []
                for mb in range(2):
                    b = P.bank()
                    P.mm(P.PS[b][:, 0:n], kt[:, h, mb * 128:(mb + 1) * 128], P.QMT[:, h, mc], True, True,
                         [rkt, P.rQMT[h]], [P.rPS[b]])
                    i = next_pt()
                    P.op(P.ACT, lambda b=b, i=i: nc.scalar.activation(out=PT[i][:, 0:n], in_=P.PS[b][:, 0:n],
                                                                      func=AF.Exp, scale=SCALE), [P.rPS[b]], [rPT[i]])
                    pts.append(i)
                bacc, bden = P.bank(), P.bank()
                for mb in range(2):
                    i = pts[mb]
                    P.mm(P.PS[bacc][:, 0:n], vt[:, mb, h * 128:(h + 1) * 128], PT[i][:, 0:n], mb == 0, mb == 1,
                         [rvt, rPT[i]], [P.rPS[bacc]])
                for mb in range(2):
                    i = pts[mb]
                    P.mm(P.PS[bden][:, 0:n], P.ONB[:], PT[i][:, 0:n], mb == 0, mb == 1, [P.rCONST, rPT[i]],
                         [P.rPS[bden]])
                finish_head(12 + h, bacc, bden, n, mc)

        mem_attn(P.MEMK, P.rMEMK, P.MEMV, P.rMEMV, T, slice(0, T))
        if samp:
            CMK = A("CMK", [128, 2, X_W], BF16)
            CMV = A("CMV", [128, 2, X_W], BF16)
            MKS = A("MKS", [128, 4, N_MEM], BF16)
            rCMK, rCMV, rMKS = P.LR("CMK"), P.LR("CMV"), P.LR("MKS")
            P.dma(P.POOL, CMK[:], P.cmk[l].rearrange("(b p) n -> p b n", p=128), [], [rCMK])
            P.dma(P.POOL, CMV[:], P.cmv[l].rearrange("(b p) n -> p b n", p=128), [], [rCMV])
            for mb in range(2):
                b = P.bank()
                for h in range(4):
                    P.op(P.PE, lambda mb=mb, h=h, b=b: nc.tensor.transpose(
                        P.PSB[b][:, h * 128:(h + 1) * 128], CMK[:, mb, h * 128:(h + 1) * 128], P.IDB[:]),
                        [rCMK, P.rIDB], [P.rPS[b]], signal=(h == 3))
                P.any_copy(MKS[:, :, mb * 128:(mb + 1) * 128], P.PSB[b][:, 0:512].rearrange("p (h m) -> p h m", h=4),
                           [P.rPS[b]], [rMKS])
            mem_attn(MKS, rMKS, CMV, rCMV, NS, slice(T, T + NS))

        P.stage('D')
        nh = min(c, 4)
        M3 = 32 * nh
        jlo = 0 if nh else 1
        q, ko = P.QT, P.KTown

        def sbank():
            b = 4 + (self.bank_i % 4)
            self.bank_i += 1
            return b

        def bc(e2d, n):
            a = e2d
            return bass.AP(tensor=a.tensor, offset=a.offset, ap=[list(a.ap[0]), [0, n], list(a.ap[1])])

        def bci(e2d, n):
            a = e2d
            return bass.AP(tensor=a.tensor, offset=a.offset, ap=[list(a.ap[0]), list(a.ap[1]), [0, n]])

        def softmax_tile(s, b, np_, width, e_ap, r_extra=None, inner=False):
            j = self.ee % 2
            self.ee += 1
            P.op(P.ACT, lambda: nc.scalar.activation(out=EX[j][0:np_, 0:width], in_=P.PS[b][0:np_, 0:width],
                                                     func=AF.Exp, scale=SCALE), [P.rPS[b]], [rEX[j]])
            i = next_pt()
            if inner:
                n = e_ap.shape[2]
                pat, kw = "p (w n) -> p w n", dict(n=n)
            else:
                n = e_ap.shape[1]
                pat, kw = "p (n w) -> p n w", dict(n=n)
            P.op(P.DVE, lambda: nc.vector.tensor_tensor(
                out=PT[i][0:np_, 0:width].rearrange(pat, **kw), in0=EX[j][0:np_, 0:width].rearrange(pat, **kw),
                in1=e_ap, op=AL.mult), [rEX[j], rET[s]] + ([r_extra] if r_extra is not None else []), [rPT[i]])
            return i

        def stage1(h, part, pt=None):
            s = h % 2
            rq, rko = P.rQT[h], P.rKTown[h // 4]
            if part == "b":
                if nh:
                    b = sbank()
                    for r in range(16):
                        P.mm(P.PS[b][0:M3, r:512:16], KH[s][:, WIN - nh * T + r:WIN:16], q[:, h, r:T:16],
                             True, True, [rKH[s], rq], [P.rPS[b]], signal=(r == 15))
                    e3 = ET[s][0:M3, 5, 0:32] if M3 == 128 else E3S[s][0:M3, :]
                    pt["3a"] = softmax_tile(s, b, M3, 512, bci(e3, 16), rE3S[s], inner=True)
                return pt
            P.dma(P.SP, ET[s][:], P.Ed[h], [P.r_Ed], [rET[s]])
            if 0 < M3 < 128:
                P.dma(P.SP, E3S[s][0:M3, :], P.Ed[h, 128 - M3:128, 5, 0:32], [P.r_Ed], [rE3S[s]])
            if nh:
                P.dma(P.SP, KH[s][:, WIN - nh * T:WIN], P.KTd[l, h, :, t0 - nh * T:t0],
                      [P.r_KTd[l][cc_] for cc_ in range(c - nh, c)], [rKH[s]])
            vt = P.Vd.tensor
            vbase = l * SEQ * ATT_W + h * 128

            def vap(tok0, pstride, n_part, nblk, bstride):
                return bass.AP(tensor=vt, offset=vbase + tok0 * ATT_W,
                               ap=[[pstride * ATT_W, n_part], [bstride * ATT_W, nblk], [1, 128]])
            rv_own = [P.r_Vd[l][c]]
            rv_h = [P.r_Vd[l][cc_] for cc_ in range(c - nh, c)]
            P.dma(P.SP, VB[s][:, 0:4, :], vap(t0, 1, 128, 4, 128), rv_own, [rVB[s]])
            if nh:
                P.dma(P.SP, VB[s][:, 4:5, :], vap(t0 - 128, 1, 128, 1, 128), rv_h, [rVB[s]])
            P.dma(P.SP, VB[s][:, 5:9, :], vap(t0, 4, 128, 4, 1), rv_own, [rVB[s]])
            if nh:
                P.dma(P.SP, VB[s][:, 9:13, :], vap(t0 - 512, 4, 128, 4, 1), rv_h, [rVB[s]])
            if nh:
                P.dma(P.SP, VB[s][0:M3, 13:29, :], vap(t0 - nh * T, 16, M3, 16, 1), rv_h, [rVB[s]])
            pt = {}
            b = sbank()
            for j in range(4):
                P.mm(P.PS[b][:, j * 128:(j + 1) * 128], ko[:, h, j * 128:(j + 1) * 128], q[:, h, j * 128:(j + 1) * 128],
                     True, True, [rko, rq], [P.rPS[b]], signal=(j == 3))
            pt["1c"] = softmax_tile(s, b, 128, 512, bc(ET[s][:, 0, :], 4))
            b = sbank()
            for j in range(jlo, 4):
                kap = KH[s][:, WIN - 128:WIN] if j == 0 else ko[:, h, (j - 1) * 128:j * 128]
                P.mm(P.PS[b][:, j * 128:(j + 1) * 128], kap, q[:, h, j * 128:(j + 1) * 128], True, True,
                     [rko, rq, rKH[s]], [P.rPS[b]], signal=(j == 3))
            pt["1p"] = softmax_tile(s, b, 128, 512, bc(ET[s][:, 1, :], 4))
            b = sbank()
            for r in range(4):
                P.mm(P.PS[b][:, r:512:4], ko[:, h, r:T:4], q[:, h, r:T:4], True, True,
                     [rko, rq], [P.rPS[b]], signal=(r == 3))
            pt["2c"] = softmax_tile(s, b, 128, 512, bci(ET[s][:, 2, :], 4), inner=True)
            if nh:
                b = sbank()
                for r in range(4):
                    P.mm(P.PS[b][:, r:512:4], KH[s][:, WIN - 512 + r:WIN:4], q[:, h, r:T:4], True, True,
                         [rKH[s], rq], [P.rPS[b]], signal=(r == 3))
                pt["2p"] = softmax_tile(s, b, 128, 512, bci(ET[s][:, 3, :], 4), inner=True)
            return pt

        def stage2(h, pt):
            s = h % 2
            bacc, bden = (0, 1) if s == 0 else (2, 3)
            acc, den = P.PS[bacc], P.PS[bden]

            def pv(out_ap, v_ap, i, pt_ap, rv):
                P.mm(out_ap, v_ap, pt_ap, False, False, [rv, rPT[i]], [P.rPS[bacc]], signal=False,
                     skip_group_check=True)

            def dn(out_ap, np_, i, pt_ap):
                P.mm(out_ap, P.ONB[0:np_, :], pt_ap, False, False, [P.rCONST, rPT[i]], [P.rPS[bden]], signal=False,
                     skip_group_check=True)

            i_c, i_p = pt["1c"], pt["1p"]
            for j in range(4):
                pv(acc[:, j * 128:(j + 1) * 128], VB[s][:, j, :], i_c, PT[i_c][:, j * 128:(j + 1) * 128], rVB[s])
            for j in range(jlo, 4):
                vb = VB[s][:, 4, :] if j == 0 else VB[s][:, j - 1, :]
                pv(acc[:, j * 128:(j + 1) * 128], vb, i_p, PT[i_p][:, j * 128:(j + 1) * 128], rVB[s])
            dn(den[:, 0:512], 128, i_c, PT[i_c][:, 0:512])
            dn(den[:, jlo * 128:512], 128, i_p, PT[i_p][:, jlo * 128:512])
            i_c = pt["2c"]
            for r in range(4):
                pv(acc[:, r:512:4], VB[s][:, 5 + r, :], i_c, PT[i_c][:, r:512:4], rVB[s])
            dn(den[:, 0:512], 128, i_c, PT[i_c][:, 0:512])
            if nh:
                i_p = pt["2p"]
                for r in range(4):
                    pv(acc[:, r:512:4], VB[s][:, 9 + r, :], i_p, PT[i_p][:, r:512:4], rVB[s])
                dn(den[:, 0:512], 128, i_p, PT[i_p][:, 0:512])
            if nh:
                i_a = pt["3a"]
                for r in range(16):
                    pv(acc[:, r:512:16], VB[s][0:M3, 13 + r, :], i_a, PT[i_a][0:M3, r:512:16], rVB[s])
                dn(den[:, 0:512], M3, i_a, PT[i_a][0:M3, 0:512])
            finish_head(h, bacc, bden)

        def zero_acc(h):
            bacc, bden = (0, 1) if h % 2 == 0 else (2, 3)
            P.op(P.DVE, lambda: nc.vector.memset(P.PS[bacc][:], 0.0), [], [P.rPS[bacc]])
            P.op(P.DVE, lambda: nc.vector.memset(P.PS[bden][:], 0.0), [], [P.rPS[bden]])

        pts = stage1(0, "b", stage1(0, "a"))
        zero_acc(0)
        for h in range(8):
            nxt = stage1(h + 1, "a") if h + 1 < 8 else None
            if h + 1 < 8:
                zero_acc(h + 1)
            stage2(h, pts)
            if h + 1 < 8:
                nxt = stage1(h + 1, "b", nxt)
            pts = nxt

        if samp:
            self.sample_attention(l, A, EX, rEX, PT, rPT, finish_head, KH, rKH, VB, rVB)
        P.stage('E')
        self.overlay()
        Y = A("Y", [128, 4, D_MODEL], F32)
        rY = [P.LR("Y%d" % b) for b in range(4)]
        GP = A("GP", [128, D_MODEL], F32)
        rGP = P.LR("GP")
        JK = A("JK", [128, D_MODEL], BF16)
        rJK = P.LR("JK")
        TM = A("TM", [128, D_MODEL], F32)
        rTM = P.LR("TM")
        SQ = A("SQ", [128, 16], F32)
        rSQ = [P.LR("SQ%d" % b) for b in range(4)]
        if samp:
            YS = A("YS", [NS, D_MODEL], F32)
            rYS = P.LR("YS")
        P.dma(P.SP, GP[:], bass.AP(tensor=P.g_post.tensor, offset=l * D_MODEL, ap=[[0, 128], [1, D_MODEL]]),
              [], [rGP])
        for g in range(4):
            w, rw = P.wload(P.w_out[l].rearrange("(k p) n -> p k n", p=128)[:, :, g * 512:(g + 1) * 512])
            for blk in range(4):
                b = P.bank()
                for f in range(16):
                    P.mm(P.PS[b][:], P.MIX[:, f, blk * 128:(blk + 1) * 128], w[:, f, :], f == 0, f == 15,
                         [rw, P.rMIX[f]], [P.rPS[b]])
                P.op(P.ACT, lambda b=b, blk=blk, g=g: nc.scalar.activation(
                    out=JK[:, 0:512], in_=P.PS[b][:], func=AF.Square, accum_out=SQ[:, blk * 4 + g:blk * 4 + g + 1]),
                    [P.rPS[b]], [rJK, rSQ[blk]])
                P.op(P.DVE, lambda b=b, blk=blk, g=g: nc.vector.tensor_tensor(
                    out=Y[:, blk, g * 512:(g + 1) * 512], in0=P.PS[b][:], in1=GP[:, g * 512:(g + 1) * 512], op=AL.mult),
                    [P.rPS[b], rGP], [rY[blk]])
            if samp:
                b = P.bank()
                for f in range(16):
                    P.mm(P.PS[b][0:NS, :], P.MIX[:, f, T:T + NS], w[:, f, :], f == 0, f == 15, [rw, P.rMIX[f]],
                         [P.rPS[b]])
                P.any_copy(YS[:, g * 512:(g + 1) * 512], P.PS[b][0:NS, :], [P.rPS[b]], [rYS])
        P.op(P.DVE, lambda: nc.vector.tensor_reduce(
            out=P.SS[:, 4:8], in_=SQ[:, 0:16].rearrange("p (b g) -> p b g", g=4), axis=mybir.AxisListType.X,
            op=AL.add), rSQ, P.rSS)
        P.op(P.DVE, lambda: nc.vector.tensor_scalar(out=P.SS[:, 4:8], in0=P.SS[:, 4:8], scalar1=1.0 / D_MODEL,
                                                    scalar2=EPS, op0=AL.mult, op1=AL.add), P.rSS, P.rSS)
        P.op(P.ACT, lambda: nc.scalar.activation(out=P.SS[:, 4:8], in_=P.SS[:, 4:8], func=AF.Sqrt), P.rSS, P.rSS)
        P.op(P.DVE, lambda: nc.vector.reciprocal(out=P.SS[:, 4:8], in_=P.SS[:, 4:8]), P.rSS, P.rSS)
        for blk in range(4):
            P.op(P.DVE, lambda blk=blk: nc.vector.scalar_tensor_tensor(
                out=P.X[:, blk, :], in0=Y[:, blk, :], scalar=P.SS[:, 4 + blk:5 + blk], in1=P.X[:, blk, :],
                op0=AL.mult, op1=AL.add), [rY[blk], P.rSS[blk], P.rX[blk]], [P.rX[blk]])
            if last:
                P.dma(P.SP, P.y_prompt[t0 + blk * 128:t0 + (blk + 1) * 128, :], P.X[:, blk, :], [P.rX[blk]], [], is_output=True)
        if samp:
            P.op(P.ACT, lambda: nc.scalar.activation(out=JK[0:NS, :], in_=YS[:], func=AF.Square,
                                                     accum_out=P.SS[0:NS, 9:10]), [rYS], [rJK, P.rSS[0]])
            self.rstd(P.SS[0:NS, 9:10], P.rSS[0], 1.0 / D_MODEL)
            P.op(P.DVE, lambda: nc.vector.scalar_tensor_tensor(
                out=TM[0:NS, :], in0=YS[:], scalar=P.SS[0:NS, 9:10], in1=GP[0:NS, :], op0=AL.mult, op1=AL.mult),
                [rYS, P.rSS[0], rGP], [rTM])
            P.op(P.POOL, lambda: nc.gpsimd.tensor_tensor(out=P.XS[:], in0=P.XS[:], in1=TM[0:NS, :], op=AL.add),
                 [rTM, P.rXS], [P.rXS])
            if last:
                P.dma(P.SP, P.y_sample, P.XS[:], [P.rXS], [], is_output=True)

    def sample_attention(self, l, A, EX, rEX, PT, rPT, finish_head, KH, rKH, VB, rVB):
        nc = self.nc
        P = self
        KC = [VB[0][:, 0:16, :], VB[1][:, 0:16, :]]
        rKC = rVB
        KTS, rKTS = KH, rKH
        VC = [A("VC%d" % i, [128, 16, 128], BF16) for i in range(2)]
        rVC = [P.LR("VC0"), P.LR("VC1")]
        KNB = A("KNB", [NS, 128], BF16)
        VNB = A("VNB", [NS, 128], BF16)
        KTN = A("KTN", [128, NS], BF16)
        rKNB, rVNB, rKTN = P.LR("KNB"), P.LR("VNB"), P.LR("KTN")
        PN = A("PN", [NS, NS], BF16)
        rPN = P.LR("PN")
        qc = slice(T, T + NS)
        for h in range(N_ATT_HEADS):
            s = h % 2
            hc = slice(h * 128, (h + 1) * 128)
            P.dma(P.POOL, KC[s], P.cache_k[l, :, hc].rearrange("(b p) d -> p b d", p=128), [], [rKC[s]])
            P.dma(P.POOL, VC[s][:], P.cache_v[l, :, hc].rearrange("(b p) d -> p b d", p=128), [], [rVC[s]])
            for g in range(4):
                b = P.bank()
                for j in range(4):
                    blk = g * 4 + j
                    P.op(P.PE, lambda blk=blk, j=j, b=b: nc.tensor.transpose(
                        P.PSB[b][:, j * 128:(j + 1) * 128], KC[s][:, blk, :], P.IDB[:]),
                        [rKC[s], P.rIDB], [P.rPS[b]], signal=(j == 3))
                P.any_copy(KTS[s][:, g * 512:(g + 1) * 512], P.PSB[b][:, 0:512], [P.rPS[b]], [rKTS[s]])
            P.op(P.DVE, lambda: nc.vector.tensor_copy(out=KNB[:], in_=P.KVS[:, h * 128:(h + 1) * 128]), [P.rKVS], [rKNB])
            P.op(P.DVE, lambda: nc.vector.tensor_copy(out=VNB[:], in_=P.KVS[:, ATT_W + h * 128:ATT_W + (h + 1) * 128]),
                 [P.rKVS], [rVNB])
            b = P.bank()
            P.op(P.PE, lambda b=b: nc.tensor.transpose(P.PSB[b][:, 0:NS], KNB[:], P.IDB[0:NS, 0:NS]),
                 [rKNB, P.rIDB], [P.rPS[b]])
            P.any_copy(KTN[:], P.PSB[b][:, 0:NS], [P.rPS[b]], [rKTN])
            b = P.bank()
            for blk in range(16):
                P.mm(P.PS[b][:, blk * NS:(blk + 1) * NS], KTS[s][:, blk * 128:(blk + 1) * 128], P.QT[:, h, qc], True, True,
                     [rKTS[s], P.rQT[h]], [P.rPS[b]], signal=(blk == 15))
            j = self.ee % 2
            self.ee += 1
            P.op(P.ACT, lambda b=b, j=j: nc.scalar.activation(out=EX[j][:, 0:128], in_=P.PS[b][:, 0:128], func=AF.Exp,
                                                              scale=SCALE), [P.rPS[b]], [rEX[j]])
            i = self.spt % len(PT)
            self.spt += 1
            P.op(P.DVE, lambda j=j, i=i: nc.vector.tensor_tensor(
                out=PT[i][:, 0:128].rearrange("p (b q) -> p b q", b=16),
                in0=EX[j][:, 0:128].rearrange("p (b q) -> p b q", b=16), in1=P.ES[:, h, 0:16, :], op=AL.mult),
                [rEX[j], P.rES], [rPT[i]])
            b2 = P.bank()
            P.mm(P.PS[b2][0:NS, 0:NS], KTN[:], P.QT[:, h, qc], True, True, [rKTN, P.rQT[h]], [P.rPS[b2]])
            j2 = self.ee % 2
            self.ee += 1
            P.op(P.ACT, lambda b2=b2, j2=j2: nc.scalar.activation(out=EX[j2][0:NS, 0:NS], in_=P.PS[b2][0:NS, 0:NS],
                                                                  func=AF.Exp, scale=SCALE), [P.rPS[b2]], [rEX[j2]])
            P.op(P.DVE, lambda j2=j2: nc.vector.tensor_tensor(out=PN[:], in0=EX[j2][0:NS, 0:NS], in1=P.ES[0:NS, h, 16, :],
                                                              op=AL.mult), [rEX[j2], P.rES], [rPN])
            bacc, bden = P.bank(), P.bank()
            for blk in range(16):
                P.mm(P.PS[bacc][:, 0:NS], VC[s][:, blk, :], PT[i][:, blk * NS:(blk + 1) * NS], blk == 0, False,
                     [rVC[s], rPT[i]], [P.rPS[bacc]], signal=False)
            P.mm(P.PS[bacc][:, 0:NS], VNB[:], PN[:], False, True, [rVNB, rPN], [P.rPS[bacc]])
            for blk in range(16):
                P.mm(P.PS[bden][:, 0:NS], P.ONB[:], PT[i][:, blk * NS:(blk + 1) * NS], blk == 0, False,
                     [P.rCONST, rPT[i]], [P.rPS[bden]], signal=False)
            P.mm(P.PS[bden][:, 0:NS], P.ONB[0:NS, :], PN[:], False, True, [P.rCONST, rPN], [P.rPS[bden]])
            finish_head(h, bacc, bden, NS, qc)

    def rstd(self, ap, res, inv_n):
        nc = self.nc
        P = self
        P.op(P.DVE, lambda: nc.vector.tensor_scalar(out=ap, in0=ap, scalar1=inv_n, scalar2=EPS, op0=AL.mult, op1=AL.add),
             [res], [res])
        P.op(P.ACT, lambda: nc.scalar.activation(out=ap, in_=ap, func=AF.Sqrt), [res], [res])
        P.op(P.DVE, lambda: nc.vector.reciprocal(out=ap, in_=ap), [res], [res])

    def build(self):
        P = self
        try:
            P.dma(P.SP, P.X[:], P.x_prompt[0:T, :].rearrange("(b p) d -> p b d", p=128), [], P.rX)
            self.setup()
            P.stage('setup')
            for c in range(self.n_chunks):
                t0 = c * T
                if c > 0:
                    P.dma(P.SP, P.X[:], P.x_prompt[t0:t0 + T, :].rearrange("(b p) d -> p b d", p=128), [], P.rX)
                for l in range(self.depth):
                    self.chunk_layer(c, l)
        except _Stop:
            pass
        for tok in self.out_tokens.values():
            self._wait(P.SP, tok)
        return self.nc


def _static_tables():
    oh = np.zeros((4, N_BUCKETS, 384), np.float32)
    for p, d in enumerate(DILS):
        for j in range(129):
            n = 128 + j
            oh[p, int(t5_bucket(j * d)), 383 - n] = 1.0
    for j in range(129):
        oh[3, int(t5_bucket(j * 4)), 383 - (128 + j)] = 2.0 if j % 4 == 0 else 1.0
    return oh


def _sample_table():
    oh = np.zeros((N_BUCKETS, 2064), np.float32)
    for n in range(2056):
        d = 2055 - n
        mult = int(d <= 128) + int(d % 4 == 0 and d <= 512) + int(d % 16 == 0 and d <= 2048)
        if mult:
            oh[int(t5_bucket(d)), n] = float(mult)
    return oh


def _host_inputs(inp, n_cores=N_CORES, with_sample=True, depth=DEPTH, seqr=SEQ):
    f = lambda a: np.ascontiguousarray(np.asarray(a, dtype=np.float32))
    oh = _static_tables()
    ident = np.eye(128, dtype=np.float32)
    wdwT = f(np.asarray(inp["w_dw"]).reshape(DEPTH, CONV_K, 4, 128).transpose(3, 0, 2, 1).reshape(128, -1))
    cpar = f(np.stack([np.asarray(inp[k]).reshape(DEPTH, 4, 128) for k in ("b_dw", "ln_conv_g", "ln_conv_b")], 0)
             .transpose(3, 0, 1, 2).reshape(128, -1))
    shared = {
        "rel_bias": f(inp["rel_bias"]), "ohrev": oh, "ident": ident,
        "norm_pre_g": f(inp["norm_pre_g"][:depth]), "norm_post_g": f(inp["norm_post_g"][:depth]),
        "w_in": f(inp["w_in"][:depth]), "w_out": f(inp["w_out"][:depth]), "w_pw2": f(inp["w_pw2"][:depth]),
        "w_mem_kv": f(inp["w_mem_kv"][:depth]), "wdwT": wdwT, "cpar": cpar,
    }
    maps = []
    for core in range(n_cores):
        b = core // 4
        m = dict(shared)
        if core % 4 == 0:
            m["x_prompt"] = f(inp["x_prompt"][b][:seqr])
            m["memT"] = f(np.asarray(inp["mem_prompt"][b]).T)
        else:
            m["x_prompt"] = np.zeros((seqr, D_MODEL), np.float32)
            m["memT"] = np.zeros((D_MODEL, N_MEM), np.float32)
        if with_sample:
            m["x_sample"] = f(inp["x_sample"][core])
            m["cache_k"] = f(np.asarray(inp["cache_attn_k"])[:depth, core].reshape(depth, L_CACHE, ATT_W))
            m["cache_v"] = f(np.asarray(inp["cache_attn_v"])[:depth, core].reshape(depth, L_CACHE, ATT_W))
            m["cache_mem_k"] = f(np.asarray(inp["cache_mem_k"])[:depth, core].reshape(depth, N_MEM, X_W))
            m["cache_mem_v"] = f(np.asarray(inp["cache_mem_v"])[:depth, core].reshape(depth, N_MEM, X_W))
            m["state_convT"] = f(np.asarray(inp["state_conv"])[:, core].reshape(DEPTH, CONV_K - 1, 4, 128)
                                 .transpose(3, 0, 2, 1).reshape(128, -1))
            m["ohs"] = _sample_table()
        maps.append(m)
    return maps


_CACHE = {}


def kernel(**inputs):
    with_sample = True
    key = ("full", with_sample)
    if key not in _CACHE:
        _CACHE[key] = Prog(with_sample=with_sample).build()
    nc = _CACHE[key]
    maps = _host_inputs(inputs, with_sample=with_sample)
    res = run_bass_kernel_spmd(nc, maps, core_ids=list(range(N_CORES))).results
    pc = (0, 4)
    y_prompt = np.stack([res[c]["y_prompt"] for c in pc], 0)
    akp = np.stack([res[c]["akp"] for c in pc], 1).reshape(DEPTH, BATCH, WIN, N_ATT_HEADS, HEAD_DIM)
    avp = np.stack([res[c]["avp"] for c in pc], 1).reshape(DEPTH, BATCH, WIN, N_ATT_HEADS, HEAD_DIM)
    cvp = np.stack([res[c]["cvp"] for c in pc], 1)
    mkp = np.stack([res[c]["mkp"] for c in pc], 1).reshape(DEPTH, BATCH, N_MEM, N_X_HEADS, HEAD_DIM)
    mvp = np.stack([res[c]["mvp"] for c in pc], 1).reshape(DEPTH, BATCH, N_MEM, N_X_HEADS, HEAD_DIM)
    allc = range(N_CORES)
    y_sample = np.stack([res[c]["y_sample"] for c in allc], 0)
    aks = np.stack([res[c]["aks"] for c in allc], 1).reshape(DEPTH, DEC_BATCH, L_CACHE, N_ATT_HEADS, HEAD_DIM)
    avs = np.stack([res[c]["avs"] for c in allc], 1).reshape(DEPTH, DEC_BATCH, L_CACHE, N_ATT_HEADS, HEAD_DIM)
    cvs = np.stack([res[c]["cvs"] for c in allc], 1)
    return (y_prompt, y_sample, akp, avp, cvp, mkp, mvp, aks, avs, cvs)
```
